# Optimizing a Trainium2 kernel written in Bass

```python
import math
import jax, jax.numpy as jnp
from jax import lax
import numpy as np

D_MODEL = 1024
BATCH = 4
SEQ = 8192
DEPTH = 2
DEC_BATCH = 2
DEC_SEQ = 8192
PAST_LEN = 128

N_GROUPS = 4
GROUP_W = D_MODEL // N_GROUPS
D_MIX = N_GROUPS * GROUP_W
HEAD_DIM = 64
N_HEADS_G = GROUP_W // HEAD_DIM
FNET_BLOCKS = 4
FNET_BLOCK_W = GROUP_W // FNET_BLOCKS
GDN_CHUNK = 64
CONV_K = 5
GRID_W = 64
NA_KH_MAX = 8
NA_KW = 16
N_MEM = 256
MEM_HEADS = 4
MEM_HEAD_DIM = GROUP_W // MEM_HEADS
EPS = 1e-6

N_GATE_COLS = 4 * N_HEADS_G
IN_SPLITS = (GROUP_W, 2 * GROUP_W, 5 * GROUP_W, 6 * GROUP_W, 6 * GROUP_W + N_GATE_COLS,
             9 * GROUP_W + N_GATE_COLS, 10 * GROUP_W + N_GATE_COLS, 11 * GROUP_W + N_GATE_COLS)
D_IN = 12 * GROUP_W + N_GATE_COLS

kernel_name = 'hybrid_fnet_gdn_natten_mem_encoder'


def rms_norm(x, g):
    xf = x.astype(jnp.float32)
    y = xf * lax.rsqrt(jnp.mean(xf * xf, axis=-1, keepdims=True) + EPS)
    return (y * g.astype(jnp.float32)).astype(x.dtype)


def l2_norm(x):
    return x * lax.rsqrt(jnp.sum(x * x, axis=-1, keepdims=True) + EPS)


def fourier_mix(u, w_fnet):
    b, l, _ = u.shape
    ub = u.astype(jnp.float32).reshape(b, l, FNET_BLOCKS, FNET_BLOCK_W)
    f = jnp.fft.fft2(ub, axes=(1, 3), norm='ortho').real
    return f.reshape(b, l, GROUP_W).astype(u.dtype) @ w_fnet


def centred_dwconv(x, w):
    pad = CONV_K // 2
    return lax.conv_general_dilated(x, w[:, None, :].astype(x.dtype), window_strides=(1,),
                                    padding=[(pad, pad)], dimension_numbers=('NWC', 'WIO', 'NWC'),
                                    feature_group_count=x.shape[-1])


def gdn_chunked(q, k, v, g, beta):
    b, l, h, dk = q.shape
    dv = v.shape[-1]
    c = GDN_CHUNK
    n = l // c

    def to_chunks(t):
        return jnp.moveaxis(t.reshape((b, n, c, h) + t.shape[3:]), 3, 1)

    q, k, v, g, beta = (to_chunks(t) for t in (q, k, v, g, beta))
    g = jnp.cumsum(g, axis=-1)
    idx = jnp.arange(c)
    incl = idx[:, None] >= idx[None, :]
    strict = idx[:, None] > idx[None, :]
    decay = jnp.exp(jnp.where(incl, g[..., :, None] - g[..., None, :], -jnp.inf))
    k_beta = k * beta[..., None]
    l_mat = jnp.where(strict, jnp.einsum('bhncd,bhnsd->bhncs', k_beta, k) * decay, 0.0)
    eye = jnp.eye(c, dtype=jnp.float32)
    rhs = jnp.concatenate([v * beta[..., None], k_beta * jnp.exp(g)[..., None]], axis=-1)
    sol = lax.linalg.triangular_solve(eye + l_mat, rhs, left_side=True, lower=True, unit_diagonal=True)
    u_c, w_c = sol[..., :dv], sol[..., dv:]
    qk = jnp.einsum('bhncd,bhnsd->bhncs', q, k) * decay
    q_dec = q * jnp.exp(g)[..., None]
    k_dec = k * jnp.exp(g[..., -1:] - g)[..., None]
    g_last = jnp.exp(g[..., -1])

    def step(state, xs):
        u_i, w_i, qk_i, qd_i, kd_i, gl_i = xs
        v_new = u_i - jnp.einsum('bhcd,bhde->bhce', w_i, state)
        o = jnp.einsum('bhcd,bhde->bhce', qd_i, state) + jnp.einsum('bhcs,bhse->bhce', qk_i, v_new)
        state = state * gl_i[..., None, None] + jnp.einsum('bhcd,bhce->bhde', kd_i, v_new)
        return state, o

    xs = tuple(jnp.moveaxis(t, 2, 0) for t in (u_c, w_c, qk, q_dec, k_dec, g_last))
    s0 = jnp.zeros((b, h, dk, dv), jnp.float32)
    _, o = lax.scan(step, s0, xs)
    o = jnp.moveaxis(o, 0, 2)
    return jnp.moveaxis(o, 1, 3).reshape(b, l, h, dv)


def gdn_branch(qkv, gates, conv_w, a_log, dt_bias, norm_g):
    b, l, _ = qkv.shape
    h = N_HEADS_G
    act = jax.nn.silu(centred_dwconv(qkv, conv_w)).astype(jnp.float32)
    q, k, v = jnp.split(act, 3, axis=-1)
    q = l2_norm(q.reshape(b, l, h, HEAD_DIM)) * (HEAD_DIM ** -0.5)
    k = l2_norm(k.reshape(b, l, h, HEAD_DIM))
    v = v.reshape(b, l, h, HEAD_DIM)
    gates = gates.astype(jnp.float32)
    a = gates[..., :2 * h].reshape(b, l, 2, h)
    beta = jax.nn.sigmoid(gates[..., 2 * h:].reshape(b, l, 2, h))
    g = -jnp.exp(a_log.astype(jnp.float32)) * jax.nn.softplus(a + dt_bias.astype(jnp.float32))
    o_f = gdn_chunked(q, k, v, g[:, :, 0], beta[:, :, 0])
    rev = lambda t: jnp.flip(t, axis=1)
    o_b = rev(gdn_chunked(rev(q), rev(k), rev(v), rev(g[:, :, 1]), rev(beta[:, :, 1])))
    o = rms_norm(o_f + o_b, norm_g)
    return o.reshape(b, l, GROUP_W).astype(qkv.dtype)


def neighbourhood_attn(q, k, v, rpb):
    b, l, h, d = q.shape
    rows = l // GRID_W
    kh = min(NA_KH_MAX, rows)
    qg = (q * (d ** -0.5)).reshape(b, rows, GRID_W, h, d)
    kg = k.reshape(b, rows, GRID_W, h, d)
    vg = v.reshape(b, rows, GRID_W, h, d)
    cols = np.arange(GRID_W)
    col_start = np.clip(cols - NA_KW // 2, 0, GRID_W - NA_KW)
    col_idx = col_start[:, None] + np.arange(NA_KW)[None, :]
    col_off = col_idx - cols[:, None] + (NA_KW - 1)
    rpb_cols = rpb[:, :, col_off]

    def row_block(r):
        rs = jnp.clip(r - kh // 2, 0, rows - kh)
        k_win = lax.dynamic_slice_in_dim(kg, rs, kh, axis=1)[:, :, col_idx]
        v_win = lax.dynamic_slice_in_dim(vg, rs, kh, axis=1)[:, :, col_idx]
        q_r = lax.dynamic_index_in_dim(qg, r, axis=1, keepdims=False)
        s = jnp.einsum('bwhd,biwjhd->bhwij', q_r, k_win).astype(jnp.float32)
        row_off = rs + jnp.arange(kh) - r + (NA_KH_MAX - 1)
        bias = jnp.take(rpb_cols, row_off, axis=1)
        s = s + jnp.transpose(bias, (0, 2, 1, 3))[None].astype(jnp.float32)
        p = jax.nn.softmax(s.reshape(b, h, GRID_W, kh * NA_KW), axis=-1)
        p = p.reshape(b, h, GRID_W, kh, NA_KW).astype(v.dtype)
        return jnp.einsum('bhwij,biwjhd->bwhd', p, v_win)

    out = lax.map(row_block, jnp.arange(rows))
    return jnp.moveaxis(out, 0, 1).reshape(b, l, h * d)


def memory_attn(q, mem_n, w_mem_kv):
    b, l, _ = q.shape
    m = mem_n.shape[1]
    km, vm = jnp.split(mem_n @ w_mem_kv, 2, axis=-1)
    km = km.reshape(b, m, MEM_HEADS, MEM_HEAD_DIM)
    vm = vm.reshape(b, m, MEM_HEADS, MEM_HEAD_DIM)
    qh = q.reshape(b, l, MEM_HEADS, MEM_HEAD_DIM)
    s = jnp.einsum('blhd,bmhd->bhlm', qh, km).astype(jnp.float32) * (MEM_HEAD_DIM ** -0.5)
    p = jax.nn.softmax(s, axis=-1).astype(q.dtype)
    return jnp.einsum('bhlm,bmhd->blhd', p, vm).reshape(b, l, GROUP_W)


def encoder_layer(x, mem, pre_g, post_g, w_in, w_fnet, conv_w, a_log, dt_bias, gdn_g, rpb, mem_g, w_mem_kv, w_out):
    b, l, _ = x.shape
    hx = rms_norm(x, pre_g)
    proj = hx @ w_in
    a_u, a_z, b_qkv, b_z, b_gates, c_qkv, c_z, d_q, d_z = jnp.split(proj, IN_SPLITS, axis=-1)
    y_a = fourier_mix(a_u, w_fnet) * jax.nn.silu(a_z)
    y_b = gdn_branch(b_qkv, b_gates, conv_w, a_log, dt_bias, gdn_g) * jax.nn.silu(b_z)
    cq, ck, cv = (t.reshape(b, l, N_HEADS_G, HEAD_DIM) for t in jnp.split(c_qkv, 3, axis=-1))
    y_c = neighbourhood_attn(cq, ck, cv, rpb) * jax.nn.silu(c_z)
    y_d = memory_attn(d_q, rms_norm(mem, mem_g), w_mem_kv) * jax.nn.silu(d_z)
    out = jnp.concatenate([y_a, y_b, y_c, y_d], axis=-1) @ w_out
    return x + rms_norm(out, post_g)


def setup_inputs(seed: int = 0) -> dict:
    key = jax.random.key(seed)
    ks = jax.random.split(key, 16)
    f32 = jnp.float32
    h = N_HEADS_G

    def nrm(k, shape, scale):
        return jax.random.normal(k, shape, f32) * scale

    dt = jnp.exp(jax.random.uniform(ks[12], (DEPTH, 2, h), f32, math.log(1e-3), math.log(1e-1)))
    return {
        'x_prompt': nrm(ks[0], (BATCH, SEQ, D_MODEL), 1.0),
        'x_sample': nrm(ks[1], (DEC_BATCH, DEC_SEQ, D_MODEL), 1.0),
        'mem_prompt': nrm(ks[2], (BATCH, N_MEM, D_MODEL), 1.0),
        'mem_sample': nrm(ks[3], (DEC_BATCH, N_MEM, D_MODEL), 1.0),
        'pre_norm_g': 1.0 + nrm(ks[4], (DEPTH, D_MODEL), 0.05),
        'post_norm_g': 1.0 + nrm(ks[5], (DEPTH, D_MODEL), 0.05),
        'w_in': nrm(ks[6], (DEPTH, D_MODEL, D_IN), D_MODEL ** -0.5),
        'w_fnet': nrm(ks[7], (DEPTH, GROUP_W, GROUP_W), GROUP_W ** -0.5),
        'gdn_conv_w': nrm(ks[8], (DEPTH, CONV_K, 3 * GROUP_W), CONV_K ** -0.5),
        'gdn_a_log': jnp.log(jax.random.uniform(ks[9], (DEPTH, 2, h), f32, 1.0, 16.0)),
        'gdn_dt_bias': dt + jnp.log(-jnp.expm1(-dt)),
        'gdn_norm_g': 1.0 + nrm(ks[10], (DEPTH, HEAD_DIM), 0.05),
        'na_rpb': nrm(ks[11], (DEPTH, h, 2 * NA_KH_MAX - 1, 2 * NA_KW - 1), 0.1),
        'mem_norm_g': 1.0 + nrm(ks[13], (DEPTH, D_MODEL), 0.05),
        'w_mem_kv': nrm(ks[14], (DEPTH, D_MODEL, 2 * GROUP_W), D_MODEL ** -0.5),
        'w_out': nrm(ks[15], (DEPTH, D_MIX, D_MODEL), D_MIX ** -0.5),
    }


def reference(x_prompt, x_sample, mem_prompt, mem_sample, pre_norm_g, post_norm_g, w_in, w_fnet,
              gdn_conv_w, gdn_a_log, gdn_dt_bias, gdn_norm_g, na_rpb, mem_norm_g, w_mem_kv, w_out):
    def trunk(x, mem):
        for i in range(DEPTH):
            x = encoder_layer(x, mem, pre_norm_g[i], post_norm_g[i], w_in[i], w_fnet[i], gdn_conv_w[i],
                              gdn_a_log[i], gdn_dt_bias[i], gdn_norm_g[i], na_rpb[i], mem_norm_g[i],
                              w_mem_kv[i], w_out[i])
        return x

    y_prompt = trunk(x_prompt, mem_prompt)
    y_sample = trunk(x_sample, mem_sample)
    return (y_prompt, y_sample)
```

```python
import math
import numpy as np
import ml_dtypes
from contextlib import ExitStack
import concourse.bass as bass
import concourse.mybir as mybir
from concourse.bass_utils import run_bass_kernel_spmd

F32 = mybir.dt.float32
BF16 = mybir.dt.bfloat16
F32R = mybir.dt.float32r
AF = mybir.ActivationFunctionType
ALU = mybir.AluOpType
AX = mybir.AxisListType

D_MODEL = 1024
DIN = 3088
DEPTH = 2
SEQ = 8192
NMEM = 256
EPS = 1e-6
NEG = -30000.0

ENGS = ("tensor", "vector", "scalar", "gpsimd", "sync")


class Res:
    __slots__ = ("name", "last_w", "readers")

    def __init__(self, name=""):
        self.name = name
        self.last_w = None
        self.readers = []


class Op:
    __slots__ = ("eng", "fn", "deps", "is_dma", "sig", "dma_slot", "needs")

    def __init__(self, eng, fn, is_dma):
        self.eng = eng
        self.fn = fn
        self.deps = []
        self.is_dma = is_dma
        self.sig = None
        self.needs = False
        self.dma_slot = None


class Prog:
    NDMA = 12

    def __init__(self, nc):
        self.nc = nc
        self.ops = {e: [] for e in ENGS}
        self.pending = None
        self.pending_done = set()

    def op(self, eng, fn, reads=(), writes=(), dma=False):
        o = Op(eng, fn, dma)
        deps = []
        for r in reads:
            if r.last_w is not None:
                deps.append(r.last_w)
        for w in writes:
            if w.last_w is not None:
                deps.append(w.last_w)
            deps.extend(w.readers)
        if self.pending is not None and eng not in self.pending_done:
            deps.extend(self.pending)
            self.pending_done.add(eng)
        seen = set()
        for d in deps:
            if id(d) in seen:
                continue
            seen.add(id(d))
            if d.eng == "tensor" and eng == "tensor" and not d.is_dma and not dma:
                continue
            o.deps.append(d)
            d.needs = True
        for r in reads:
            r.readers.append(o)
        for w in writes:
            w.last_w = o
            w.readers = []
        self.ops[eng].append(o)
        return o

    def dma(self, eng, out, in_, reads=(), writes=(), **kw):
        return self.op(eng, lambda e: e.dma_start(out=out, in_=in_, **kw), reads, writes, dma=True)

    def barrier(self):
        deps = []
        for e in ENGS:
            ops = self.ops[e]
            for o in reversed(ops):
                if not o.is_dma:
                    deps.append(o)
                    o.needs = True
                    break
            deps.extend([o for o in ops if o.is_dma][-self.NDMA:])
        self.pending = deps
        self.pending_done = set()

    def emit(self):
        nc = self.nc
        with ExitStack() as st:
            sems = {e: st.enter_context(nc.semaphore("s_" + e)) for e in ENGS}
            dsems = {e: [st.enter_context(nc.semaphore("d_%s_%d" % (e, i))) for i in range(self.NDMA)]
                     for e in ("sync", "gpsimd", "scalar")}
            for e in ENGS:
                cnt = 0
                dcnt = 0
                for o in self.ops[e]:
                    if o.is_dma:
                        o.dma_slot = dcnt
                        o.sig = (dsems[e][dcnt % self.NDMA], 16 * (dcnt // self.NDMA + 1))
                        dcnt += 1
                    elif o.needs:
                        cnt += 1
                        o.sig = (sems[e], cnt)
            block = st.enter_context(nc.Block())
            prog = self

            def run_engine(e, eng):
                waited = {}
                dma_list = [o for o in prog.ops[e] if o.is_dma]

                def wait(sem, val):
                    k = id(sem)
                    if waited.get(k, 0) >= val:
                        return
                    waited[k] = val
                    eng.wait_ge(sem, val)

                for o in prog.ops[e]:
                    for d in o.deps:
                        wait(*d.sig)
                    if o.is_dma and o.dma_slot >= prog.NDMA:
                        wait(*dma_list[o.dma_slot - prog.NDMA].sig)
                    ins = o.fn(eng)
                    if o.is_dma:
                        ins.then_inc(o.sig[0], 16)
                    elif o.needs:
                        ins.then_inc(o.sig[0], 1)
                for o in dma_list[-prog.NDMA:]:
                    wait(*o.sig)

            @block.tensor
            def _(eng):
                run_engine("tensor", eng)

            @block.vector
            def _(eng):
                run_engine("vector", eng)

            @block.scalar
            def _(eng):
                run_engine("scalar", eng)

            @block.gpsimd
            def _(eng):
                run_engine("gpsimd", eng)

            @block.sync
            def _(eng):
                run_engine("sync", eng)


def MM(out, lhsT, rhs, start=True, stop=True):
    return lambda e: e.matmul(out, lhsT, rhs, start=start, stop=stop)


def TR(out, in_, ident):
    return lambda e: e.transpose(out, in_, ident)


def ACTF(out, in_, func, **kw):
    return lambda e: e.activation(out, in_, func, **kw)


def TS(out, in0, s1, s2, op0, op1=None):
    if op1 is None:
        return lambda e: e.tensor_scalar(out, in0, s1, s2, op0)
    return lambda e: e.tensor_scalar(out, in0, s1, s2, op0, op1)


def TT(out, in0, in1, op):
    return lambda e: e.tensor_tensor(out, in0, in1, op)


def STT(out, in0, scalar, in1, op0, op1):
    return lambda e: e.scalar_tensor_tensor(out, in0, scalar, in1, op0, op1)


def CP(out, in_):
    return lambda e: e.tensor_copy(out, in_)


def MSET(ap, val):
    return lambda e: e.memset(ap, val)


def RECIP(out, in_):
    return lambda e: e.reciprocal(out, in_)


_uid = [0]


def _dsize(dt):
    return 4 if dt in (F32, F32R) else 2


class Arena:
    def __init__(self, nc, base, limit):
        self.nc = nc
        self.off = base
        self.base = base
        self.limit = limit

    def alloc(self, shape, dt):
        n = _dsize(dt)
        for s in shape[1:]:
            n *= s
        n = (n + 63) // 64 * 64
        _uid[0] += 1
        h = self.nc.alloc_sbuf_tensor_at("t%d" % _uid[0], list(shape), dt, offset=self.off)
        self.off += n
        assert self.off <= self.limit, ("SBUF arena overflow", self.off, self.limit)
        return h

    def reset(self, to=None):
        self.off = self.base if to is None else to


FM_CHUNKS = ([(256 + 128 * j, True, 0 + j) for j in range(2)] + [(512 + 128 * j, False, 8 + j) for j in range(6)] +
             [(1280 + 128 * j, True, 2 + j) for j in range(2)] + [(1552 + 128 * j, False, 14 + j) for j in range(4)] +
             [(2320 + 128 * j, True, 4 + j) for j in range(2)] + [(2576 + 128 * j, False, 18 + j) for j in range(2)] +
             [(2832 + 128 * j, True, 6 + j) for j in range(2)])


def pipeline(n, stages):
    for t in range(n + len(stages) - 1):
        for si, fn in enumerate(stages):
            i = t - si
            if 0 <= i < n:
                fn(i)


class K:
    pass


def build_program(L=SEQ, depth=DEPTH, debug=False, stop_after=None):
    nc = bass.Bass("TRN2", target_bir_lowering=False)
    P = Prog(nc)
    k = K()
    k.nc, k.P, k.L, k.depth = nc, P, L, depth
    N1 = L // 64
    k.N1 = N1

    def din(name, shape, dt=F32):
        return nc.dram_tensor(name, list(shape), dt, kind="ExternalInput").ap()

    skind = "ExternalOutput" if debug else "Internal"

    def dscr(name, shape, dt):
        return nc.dram_tensor(name, list(shape), dt, kind=skind).ap()

    k.x = din("x", [L, D_MODEL])
    k.mem = din("mem", [NMEM, D_MODEL])
    k.pre_g = din("pre_norm_g", [depth, D_MODEL])
    k.post_g = din("post_norm_g", [depth, D_MODEL])
    k.w_in = din("w_in", [depth, D_MODEL, DIN])
    k.w_fnet = din("w_fnet", [depth, 256, 256])
    k.conv_w = din("gdn_conv_w", [depth, 5, 768])
    k.a_log = din("gdn_a_log", [depth, 8])
    k.dt_bias = din("gdn_dt_bias", [depth, 8])
    k.gdn_g = din("gdn_norm_g", [depth, 64])
    k.rpb = din("na_rpb", [depth, 4, 15, 31])
    k.mem_g = din("mem_norm_g", [depth, D_MODEL])
    k.w_kv = din("w_mem_kv", [depth, D_MODEL, 512])
    k.w_out = din("w_out", [depth, D_MODEL, D_MODEL])
    k.c_ident = din("c_ident", [128, 128], BF16)
    k.c_identf = din("c_identf", [128, 128], F32)
    k.c_dft1 = din("c_dft1", [N1, 2 * N1], BF16)
    k.c_tw = din("c_tw", [N1, 3, 64], F32)
    k.c_dft2 = din("c_dft2", [64, 2, 128], BF16)
    k.c_bd64 = din("c_bd64", [128, 2, 128], BF16)
    k.c_gmask = din("c_gmask", [128, 12, 128], F32)
    k.c_bones = din("c_bones", [128, 128], BF16)
    k.y = nc.dram_tensor("y", [L, D_MODEL], F32, kind="ExternalOutput").ap()
    k.X1 = dscr("s_x1", [L, D_MODEL], F32)
    k.FT = dscr("s_ft", [20, 128, L], BF16)
    k.U = dscr("s_u", [L, 256], BF16)
    k.CV = dscr("s_cv", [L, 256], BF16)
    k.G = dscr("s_g", [L, 16], F32)
    k.YT = dscr("s_yt", [8, 128, L], BF16)
    k.Bd = dscr("s_bd", [N1, 64, 2, 256], BF16)
    k.QKn = dscr("s_qkn", [4, 128, L], BF16)
    k.QKVt = dscr("s_qkvt", [L, 768], BF16)
    k.OF = dscr("s_of", [L, 256], F32)
    k.OB = dscr("s_ob", [L, 256], F32)
    k.NAB = dscr("s_nab", [4, 8, 64, 512], F32)

    k.pball = nc.alloc_psum_tensor("pball", [128, 4096], F32)
    k.pb = [k.pball[:, i * 512:(i + 1) * 512] for i in range(8)]
    k.PB = [Res("pb%d" % i) for i in range(8)]

    SB_BASE = 16640
    SB_LIMIT = SB_BASE + 196608
    pa = Arena(nc, SB_BASE, SB_LIMIT)
    k.ident = pa.alloc([128, 128], BF16)
    k.identf = pa.alloc([128, 128], F32)
    k.bones = pa.alloc([128, 128], BF16)
    k.ones_bf = pa.alloc([128, 64], BF16)
    k.idr = pa.alloc([64, 64], F32R)
    k.Rconst = Res("const")
    P.dma("sync", k.ident[:], k.c_ident, writes=[k.Rconst])
    P.dma("sync", k.identf[:], k.c_identf, writes=[k.Rconst])
    P.dma("sync", k.bones[:], k.c_bones, writes=[k.Rconst])
    P.op("vector", MSET(k.ones_bf[:], 1.0), writes=[k.Rconst])
    P.op("vector", CP(k.idr[:], k.identf[0:64, 0:64]), [k.Rconst], [k.Rconst])
    k.arena = Arena(nc, pa.off, SB_LIMIT)

    for lay in range(depth):
        xin = k.x if lay == 0 else k.X1
        yout = k.y if lay == depth - 1 else k.X1
        phase1(k, lay, xin)
        P.barrier()
        if stop_after == "p1":
            break
        phase_mem(k, lay)
        P.barrier()
        if stop_after == "mem":
            break
        phase_fnet(k, lay)
        P.barrier()
        if stop_after == "fnet":
            break
        phase_na(k, lay)
        P.barrier()
        if stop_after == "na":
            break
        phase_gdn(k, lay)
        P.barrier()
        if stop_after == "gdn":
            break
        phase_out(k, lay, xin, yout)
        P.barrier()
    P.emit()
    return nc


def rms_tile(k, xb, Rxb, junk, Rjunk, st, Rst, xs, Rxs, width=D_MODEL):
    P = k.P
    P.op("scalar", ACTF(junk[:], xb[:], AF.Square, accum_out=st[:, 0:1]), [Rxb], [Rjunk, Rst])
    P.op("scalar", ACTF(st[:, 1:2], st[:, 0:1], AF.Ln, bias=EPS, scale=1.0 / width), [Rst], [Rst])
    P.op("scalar", ACTF(st[:, 2:3], st[:, 1:2], AF.Exp, scale=-0.5), [Rst], [Rst])
    P.op("vector", TS(xs[:], xb[:], st[:, 2:3], None, ALU.mult), [Rxb, Rst], [Rxs])


def phase1(k, lay, xin):
    nc, P, L = k.nc, k.P, k.L
    A = k.arena
    A.reset()
    pb, PB = k.pb, k.PB
    wst = [A.alloc([128, DIN], F32) for _ in range(2)]
    Rwst = [Res(), Res()]
    wbf = A.alloc([128, 8, DIN], BF16)
    Rwbf = Res()
    gpre = A.alloc([128, 8], F32)
    Rg = Res()
    P.dma("sync", gpre[:], k.pre_g[lay].rearrange("(k p) -> p k", p=128), writes=[Rg], allow_slow_non_contiguous=True)
    for kk in range(8):
        P.dma("sync", wst[kk % 2][:], k.w_in[lay, kk * 128:(kk + 1) * 128, :], writes=[Rwst[kk % 2]])
        if kk % 2 == 0:
            P.op("vector", TS(wbf[:, kk, :], wst[kk % 2][:], gpre[:, kk:kk + 1], None, ALU.mult), [Rwst[kk % 2], Rg], [Rwbf])
        else:
            P.op("scalar", ACTF(wbf[:, kk, :], wst[kk % 2][:], AF.Copy, scale=gpre[:, kk:kk + 1]), [Rwst[kk % 2], Rg], [Rwbf])
    dtb = A.alloc([128, 8], F32)
    negA = A.alloc([128, 8], F32)
    Rgc = Res()
    P.dma("sync", dtb[:], k.dt_bias[lay].partition_broadcast(128), writes=[Rgc])
    P.dma("sync", negA[:], k.a_log[lay].partition_broadcast(128), writes=[Rgc])
    P.op("scalar", ACTF(negA[:], negA[:], AF.Exp), [Rgc], [Rgc])
    P.op("vector", TS(negA[:], negA[:], -1.0, None, ALU.mult), [Rgc], [Rgc])

    NXB = 8
    xt = [A.alloc([128, D_MODEL], F32) for _ in range(NXB)]
    Rxt = [Res() for _ in range(NXB)]
    junk = A.alloc([128, D_MODEL], F32)
    Rjunk = Res()
    stt = [A.alloc([128, 4], F32) for _ in range(NXB)]
    Rstt = [Res() for _ in range(NXB)]
    xs = [A.alloc([128, D_MODEL], BF16) for _ in range(NXB)]
    Rxs = [Res() for _ in range(NXB)]
    hxT = [A.alloc([128, 8, 512], BF16) for _ in range(3)]
    RhxT = [Res() for _ in range(3)]
    fo = [A.alloc([128, 512], BF16) for _ in range(4)]
    Rfo = [Res() for _ in range(4)]
    tmo = [A.alloc([128, 512], BF16) for _ in range(2)]
    Rtmo = [Res(), Res()]
    gw = A.alloc([128, 4, 16], F32)
    gout = A.alloc([128, 4, 16], F32)
    Rgw, Rgout = Res(), Res()
    pT = pb[7][:, :].bitcast(BF16)
    ngroups = L // 512

    def prepA(gi):
        for t in range(4):
            tok0 = gi * 512 + t * 128
            b = (gi * 4 + t) % NXB
            P.dma("sync", xt[b][:], xin[tok0:tok0 + 128, :], writes=[Rxt[b]])
            rms_tile(k, xt[b], Rxt[b], junk, Rjunk, stt[b], Rstt[b], xs[b], Rxs[b])

    def prepB(gi):
        hb = gi % 3
        for t in range(4):
            b = (gi * 4 + t) % NXB
            for kk in range(8):
                P.op("tensor", TR(pT[:, kk * 128:(kk + 1) * 128], xs[b][:, kk * 128:(kk + 1) * 128], k.ident[:]),
                     [Rxs[b], k.Rconst], [PB[7]])
            P.op("vector", CP(hxT[hb][:, :, t * 128:(t + 1) * 128], pT.rearrange("p (k t) -> p k t", k=8)),
                 [PB[7]], [RhxT[hb]])

    def mmG(gi):
        hb = gi % 3
        for ci, (col0, is_silu, dst) in enumerate(FM_CHUNKS):
            bk = ci % 4
            for kk in range(8):
                P.op("tensor", MM(pb[bk][:, :], wbf[:, kk, col0:col0 + 128], hxT[hb][:, kk, :], kk == 0, kk == 7),
                     [Rwbf, RhxT[hb]], [PB[bk]])
            if is_silu:
                P.op("scalar", ACTF(fo[bk][:], pb[bk][:, :], AF.Silu), [PB[bk]], [Rfo[bk]])
            else:
                P.op("vector", CP(fo[bk][:], pb[bk][:, :]), [PB[bk]], [Rfo[bk]])
            P.dma("gpsimd", k.FT[dst, :, gi * 512:(gi + 1) * 512], fo[bk][:], reads=[Rfo[bk]])
        for t in range(4):
            tok0 = gi * 512 + t * 128
            bk = 4 + t % 2
            for (oap, c0, c1, bres) in ((pb[bk][:, 0:256], 0, 256, PB[bk]), (pb[bk][:, 256:512], 2064, 2320, PB[bk]),
                                        (pb[6][:, t * 16:(t + 1) * 16], 1536, 1552, PB[6])):
                for kk in range(8):
                    lt = hxT[hb][:, kk, t * 128:(t + 1) * 128]
                    P.op("tensor", MM(oap, lt, wbf[:, kk, c0:c1], kk == 0, kk == 7), [Rwbf, RhxT[hb]], [bres])
            ob = tmo[t % 2]
            P.op("vector", CP(ob[:], pb[bk][:, :]), [PB[bk]], [Rtmo[t % 2]])
            P.dma("gpsimd", k.U[tok0:tok0 + 128, :], ob[:, 0:256], reads=[Rtmo[t % 2]])
            P.dma("gpsimd", k.CV[tok0:tok0 + 128, :], ob[:, 256:512], reads=[Rtmo[t % 2]])
        pg = pb[6][:, 0:64].rearrange("p (t c) -> p t c", t=4)
        P.op("vector", TT(gw[:, :, 0:8], pg[:, :, 0:8], dtb[:, :].unsqueeze(1).to_broadcast([128, 4, 8]), ALU.add),
             [PB[6], Rgc], [Rgw])
        P.op("scalar", ACTF(gw[:, :, 0:8], gw[:, :, 0:8], AF.Exp), [Rgw], [Rgw])
        P.op("scalar", ACTF(gw[:, :, 0:8], gw[:, :, 0:8], AF.Ln, bias=1.0), [Rgw], [Rgw])
        P.op("scalar", ACTF(gw[:, :, 8:16], pg[:, :, 8:16], AF.Exp, scale=-1.0), [PB[6], Rgw], [Rgw])
        P.op("vector", TT(gout[:, :, 0:8], gw[:, :, 0:8], negA[:, :].unsqueeze(1).to_broadcast([128, 4, 8]), ALU.mult),
             [Rgw, Rgc], [Rgout])
        P.op("vector", TS(gw[:, :, 8:16], gw[:, :, 8:16], 1.0, None, ALU.add), [Rgw], [Rgw])
        P.op("vector", RECIP(gout[:, :, 8:16], gw[:, :, 8:16]), [Rgw], [Rgout])
        P.dma("gpsimd", k.G[gi * 512:(gi + 1) * 512, :].rearrange("(t p) c -> p t c", p=128), gout[:], reads=[Rgout])

    pipeline(ngroups, [prepA, prepB, mmG])


def phase_out(k, lay, xin, yout):
    nc, P, L = k.nc, k.P, k.L
    A = k.arena
    A.reset()
    pb, PB = k.pb, k.PB
    wst = [A.alloc([128, D_MODEL], F32) for _ in range(2)]
    Rwst = [Res(), Res()]
    wob = A.alloc([128, 8, D_MODEL], BF16)
    Rwob = Res()
    for kk in range(8):
        P.dma("sync", wst[kk % 2][:], k.w_out[lay, kk * 128:(kk + 1) * 128, :], writes=[Rwst[kk % 2]])
        if kk % 2 == 0:
            P.op("vector", CP(wob[:, kk, :], wst[kk % 2][:]), [Rwst[kk % 2]], [Rwob])
        else:
            P.op("scalar", ACTF(wob[:, kk, :], wst[kk % 2][:], AF.Copy), [Rwst[kk % 2]], [Rwob])
    gpost = A.alloc([128, D_MODEL], F32)
    Rgp = Res()
    P.dma("sync", gpost[:], k.post_g[lay].partition_broadcast(128), writes=[Rgp])
    NB = 4
    ycat = [A.alloc([128, 8, 512], BF16) for _ in range(2)]
    Ryc = [Res(), Res()]
    osb = [A.alloc([128, D_MODEL], F32) for _ in range(NB)]
    Ros = [Res() for _ in range(NB)]
    xr = [A.alloc([128, D_MODEL], F32) for _ in range(NB)]
    Rxr = [Res() for _ in range(NB)]
    junk = A.alloc([128, 512], BF16)
    Rjunk = Res()
    stt = [A.alloc([128, 8], F32) for _ in range(NB)]
    Rst = [Res() for _ in range(NB)]

    def s0(i):
        gi, t = i // 4, i % 4
        yb = gi % 2
        if t == 0:
            P.dma("sync", ycat[yb][:], k.YT[:, :, gi * 512:(gi + 1) * 512].rearrange("c p l -> p c l"), writes=[Ryc[yb]])
        b = i % NB
        P.dma("sync", xr[b][:], xin[i * 128:(i + 1) * 128, :], writes=[Rxr[b]])
        for half in range(2):
            bk = half + 2 * (i % 2)
            for kk in range(8):
                P.op("tensor", MM(pb[bk][:, :], ycat[yb][:, kk, t * 128:(t + 1) * 128],
                                  wob[:, kk, half * 512:(half + 1) * 512], kk == 0, kk == 7), [Ryc[yb], Rwob], [PB[bk]])
            P.op("scalar", ACTF(junk[:], pb[bk][:, :], AF.Square, accum_out=stt[b][:, half:half + 1]), [PB[bk]], [Rjunk, Rst[b]])

    def s1(i):
        b = i % NB
        st = stt[b]
        P.op("vector", TT(st[:, 2:3], st[:, 0:1], st[:, 1:2], ALU.add), [Rst[b]], [Rst[b]])
        P.op("scalar", ACTF(st[:, 3:4], st[:, 2:3], AF.Ln, bias=EPS, scale=1.0 / D_MODEL), [Rst[b]], [Rst[b]])
        P.op("scalar", ACTF(st[:, 4:5], st[:, 3:4], AF.Exp, scale=-0.5), [Rst[b]], [Rst[b]])
        for half in range(2):
            bk = half + 2 * (i % 2)
            P.op("scalar", ACTF(osb[b][:, half * 512:(half + 1) * 512], pb[bk][:, :], AF.Copy, scale=st[:, 4:5]),
                 [PB[bk], Rst[b]], [Ros[b]])

    def s2(i):
        b = i % NB
        P.op("vector", TT(osb[b][:], osb[b][:], gpost[:], ALU.mult), [Ros[b], Rgp], [Ros[b]])
        P.op("gpsimd", TT(osb[b][:], osb[b][:], xr[b][:], ALU.add), [Ros[b], Rxr[b]], [Ros[b]])
        P.dma("gpsimd", yout[i * 128:(i + 1) * 128, :], osb[b][:], reads=[Ros[b]])

    pipeline(L // 128, [s0, s1, s2])


def phase_mem(k, lay):
    nc, P, L = k.nc, k.P, k.L
    A = k.arena
    A.reset()
    pb, PB = k.pb, k.PB
    wst = [A.alloc([128, 512], F32) for _ in range(2)]
    Rwst = [Res(), Res()]
    wkv = A.alloc([128, 8, 512], BF16)
    Rwkv = Res()
    gm = A.alloc([128, 8], F32)
    Rgm = Res()
    P.dma("sync", gm[:], k.mem_g[lay].rearrange("(k p) -> p k", p=128), writes=[Rgm], allow_slow_non_contiguous=True)
    for kk in range(8):
        P.dma("sync", wst[kk % 2][:], k.w_kv[lay, kk * 128:(kk + 1) * 128, :], writes=[Rwst[kk % 2]])
        P.op("vector", TS(wkv[:, kk, :], wst[kk % 2][:], gm[:, kk:kk + 1], None, ALU.mult), [Rwst[kk % 2], Rgm], [Rwkv])
    xt = A.alloc([128, D_MODEL], F32)
    junk = A.alloc([128, D_MODEL], F32)
    st = A.alloc([128, 4], F32)
    xs = A.alloc([128, D_MODEL], BF16)
    Rxt, Rjunk, Rst, Rxs = Res(), Res(), Res(), Res()
    memT = A.alloc([128, 8, 256], BF16)
    RmemT = Res()
    pT = pb[7][:, :].bitcast(BF16)
    for t in range(2):
        P.dma("sync", xt[:], k.mem[t * 128:(t + 1) * 128, :], writes=[Rxt])
        rms_tile(k, xt, Rxt, junk, Rjunk, st, Rst, xs, Rxs)
        for kk in range(8):
            P.op("tensor", TR(pT[:, kk * 128:(kk + 1) * 128], xs[:, kk * 128:(kk + 1) * 128], k.ident[:]), [Rxs, k.Rconst], [PB[7]])
        P.op("vector", CP(memT[:, :, t * 128:(t + 1) * 128], pT.rearrange("p (k t) -> p k t", k=8)), [PB[7]], [RmemT])
    kmT = A.alloc([64, 4, 256], BF16)
    vm = A.alloc([128, 2, 256], BF16)
    Rkm, Rvm = Res(), Res()
    for h in range(4):
        bk = h % 2
        for kk in range(8):
            P.op("tensor", MM(pb[bk][0:64, 0:256], wkv[:, kk, h * 64:(h + 1) * 64], memT[:, kk, :], kk == 0, kk == 7),
                 [Rwkv, RmemT], [PB[bk]])
        P.op("vector", CP(kmT[:, h, :], pb[bk][0:64, 0:256]), [PB[bk]], [Rkm])
    for mc in range(2):
        bk = 2 + mc
        for kk in range(8):
            P.op("tensor", MM(pb[bk][:, 0:256], memT[:, kk, mc * 128:(mc + 1) * 128], wkv[:, kk, 256:512], kk == 0, kk == 7),
                 [Rwkv, RmemT], [PB[bk]])
        P.op("vector", CP(vm[:, mc, :], pb[bk][:, 0:256]), [PB[bk]], [Rvm])
    NB = 4
    mk = lambda shape, dt: [A.alloc(shape, dt) for _ in range(NB)]
    rs = lambda: [Res() for _ in range(NB)]
    qT, gz = mk([64, 512], BF16), mk([64, 512], BF16)
    PT = mk([128, 2, 512], BF16)
    rden, y1 = mk([64, 512], F32), mk([64, 512], F32)
    yo = mk([64, 512], BF16)
    Rq, Rgz, RPT, Rrd, Ry1, Ryo = rs(), rs(), rs(), rs(), rs(), rs()

    def geo(i):
        gi, h = i // 4, i % 4
        return slice(gi * 512, (gi + 1) * 512), h, (h % 2) * 64, i % NB

    def m0(i):
        sl, h, p0, b = geo(i)
        P.dma("sync", qT[b][:], k.FT[18 + h // 2, p0:p0 + 64, sl], writes=[Rq[b]])
        P.dma("sync", gz[b][:], k.FT[6 + h // 2, p0:p0 + 64, sl], writes=[Rgz[b]])
        for mc in range(2):
            bk = 4 * (i % 2) + mc
            P.op("tensor", MM(pb[bk][:, :], kmT[:, h, mc * 128:(mc + 1) * 128], qT[b][:]), [Rkm, Rq[b]], [PB[bk]])
            P.op("scalar", ACTF(PT[b][:, mc, :], pb[bk][:, :], AF.Exp, scale=0.125), [PB[bk]], [RPT[b]])

    def m1(i):
        sl, h, p0, b = geo(i)
        bo, bd = 4 * (i % 2) + 2, 4 * (i % 2) + 3
        for mc in range(2):
            P.op("tensor", MM(pb[bo][0:64, :], vm[:, mc, h * 64:(h + 1) * 64], PT[b][:, mc, :], mc == 0, mc == 1),
                 [Rvm, RPT[b]], [PB[bo]])
        for mc in range(2):
            P.op("tensor", MM(pb[bd][0:64, :], k.ones_bf[:, :], PT[b][:, mc, :], mc == 0, mc == 1),
                 [k.Rconst, RPT[b]], [PB[bd]])
        P.op("scalar", ACTF(rden[b][:], pb[bd][0:64, :], AF.Ln), [PB[bd]], [Rrd[b]])
        P.op("scalar", ACTF(rden[b][:], rden[b][:], AF.Exp, scale=-1.0), [Rrd[b]], [Rrd[b]])
        P.op("vector", TT(y1[b][:], pb[bo][0:64, :], rden[b][:], ALU.mult), [PB[bo], Rrd[b]], [Ry1[b]])

    def m2(i):
        sl, h, p0, b = geo(i)
        P.op("gpsimd", TT(yo[b][:], y1[b][:], gz[b][:], ALU.mult), [Ry1[b], Rgz[b]], [Ryo[b]])
        P.dma("gpsimd", k.YT[6 + h // 2, p0:p0 + 64, sl], yo[b][:], reads=[Ryo[b]])

    pipeline((L // 512) * 4, [m0, m1, m2])

def make_consts(L):
    N1 = L // 64
    bf = ml_dtypes.bfloat16
    c = {}
    c["c_ident"] = np.eye(128, dtype=np.float32).astype(bf)
    c["c_identf"] = np.eye(128, dtype=np.float32)
    l1 = np.arange(N1)
    ang1 = 2 * np.pi * np.outer(l1, l1) / N1
    c["c_dft1"] = np.concatenate([np.cos(ang1), np.sin(ang1)], axis=1).astype(np.float32).astype(bf)
    angt = 2 * np.pi * np.outer(np.arange(N1), np.arange(64)) / L
    sc = 1.0 / math.sqrt(L * 64.0)
    c["c_tw"] = np.stack([np.cos(angt) * sc, -np.sin(angt) * sc, -np.cos(angt) * sc], axis=1).astype(np.float32)
    a2 = 2 * np.pi * np.outer(np.arange(64), np.arange(64)) / 64
    C2, S2 = np.cos(a2), np.sin(a2)
    c["c_dft2"] = np.stack([np.concatenate([C2, -S2], 1), np.concatenate([S2, C2], 1)], axis=1).astype(np.float32).astype(bf)
    bdc = np.zeros((128, 128)); bds = np.zeros((128, 128))
    for b in range(2):
        bdc[b * 64:(b + 1) * 64, b * 64:(b + 1) * 64] = C2
        bds[b * 64:(b + 1) * 64, b * 64:(b + 1) * 64] = S2
    c["c_bd64"] = np.stack([bdc, bds], axis=1).astype(np.float32).astype(bf)
    j = np.arange(128)[:, None]
    s = np.arange(128)[None, :]
    same = (j // 64) == (s // 64)
    gm = np.zeros((128, 12, 128), np.float32)
    gm[:, 0] = (j > s) & same
    gm[:, 1] = (j < s) & same
    gm[:, 2] = (j <= s) & same
    gm[:, 3] = (j >= s) & same
    gm[:, 4] = (j >= s) & same
    gm[:, 5] = (j <= s) & same
    gm[:, 6] = (j > s) & same
    gm[:, 7] = (j < s) & same
    gm[:, 8] = (j > s) & same
    gm[:, 9] = (j < s) & same
    gm[:, 10] = (j < 64) & (s >= 0)
    gm[:, 11] = (j >= 64) & (s >= 0)
    c["c_gmask"] = gm
    bo = np.zeros((128, 128), np.float32)
    bo[:64, :64] = 1
    bo[64:, 64:] = 1
    c["c_bones"] = bo.astype(bf)
    return c


_prog_cache = {}


def run_cores(per_core_inputs, L, depth, debug=False, stop_after=None):
    key = (L, depth, debug, stop_after)
    nc = build_program(L, depth, debug, stop_after)
    consts = make_consts(L)
    in_maps = []
    for d in per_core_inputs:
        m = dict(consts)
        m.update(d)
        in_maps.append(m)
    res = run_bass_kernel_spmd(nc, in_maps, core_ids=list(range(len(in_maps))))
    return res.results


def kernel(x_prompt, x_sample, mem_prompt, mem_sample, pre_norm_g, post_norm_g, w_in, w_fnet, gdn_conv_w,
           gdn_a_log, gdn_dt_bias, gdn_norm_g, na_rpb, mem_norm_g, w_mem_kv, w_out):
    f = lambda a: np.ascontiguousarray(np.asarray(a, dtype=np.float32))
    xs = [f(x_prompt[i]) for i in range(4)] + [f(x_sample[i]) for i in range(2)]
    ms = [f(mem_prompt[i]) for i in range(4)] + [f(mem_sample[i]) for i in range(2)]
    shared = dict(pre_norm_g=f(pre_norm_g), post_norm_g=f(post_norm_g), w_in=f(w_in), w_fnet=f(w_fnet),
                  gdn_conv_w=f(gdn_conv_w), gdn_a_log=f(gdn_a_log).reshape(DEPTH, 8),
                  gdn_dt_bias=f(gdn_dt_bias).reshape(DEPTH, 8), gdn_norm_g=f(gdn_norm_g), na_rpb=f(na_rpb),
                  mem_norm_g=f(mem_norm_g), w_mem_kv=f(w_mem_kv), w_out=f(w_out))
    per_core = []
    for c in range(8):
        s = c if c < 6 else c - 6
        d = dict(shared)
        d["x"] = xs[s]
        d["mem"] = ms[s]
        per_core.append(d)
    res = run_cores(per_core, SEQ, DEPTH)
    y_prompt = np.stack([res[i]["y"] for i in range(4)], axis=0).astype(np.float32)
    y_sample = np.stack([res[4 + i]["y"] for i in range(2)], axis=0).astype(np.float32)
    return (y_prompt, y_sample)


def phase_fnet(k, lay):
    nc, P, L, N1 = k.nc, k.P, k.L, k.N1
    A = k.arena
    A.reset()
    pb, PB = k.pb, k.PB
    wf = A.alloc([128, 2, 256], F32)
    wfb = A.alloc([128, 2, 256], BF16)
    bd = A.alloc([128, 2, 128], BF16)
    wmix = A.alloc([128, 2, 2, 256], BF16)
    dft2 = A.alloc([64, 2, 128], BF16)
    Rw, Rmix = Res(), Res()
    P.dma("sync", wf[:], k.w_fnet[lay].rearrange("(c p) o -> p c o", p=128), writes=[Rw])
    P.dma("sync", bd[:], k.c_bd64, writes=[Rw])
    P.dma("sync", dft2[:], k.c_dft2, writes=[Rw])
    P.op("vector", CP(wfb[:], wf[:]), [Rw], [Rw])
    for cc in range(2):
        for ri in range(2):
            bk = cc * 2 + ri
            P.op("tensor", MM(pb[bk][:, 0:256], bd[:, ri, :], wfb[:, cc, :]), [Rw], [PB[bk]])
            P.op("vector", CP(wmix[:, cc, ri, :], pb[bk][:, 0:256]), [PB[bk]], [Rmix])
    mark = A.off
    dft1 = A.alloc([N1, 2 * N1], BF16)
    tw = A.alloc([N1, 3, 64], F32)
    X = A.alloc([N1, 64 * 256], BF16)
    Bsb = A.alloc([N1, 64, 2, 256], BF16)
    Rc1, RX, RB = Res(), Res(), Res()
    P.dma("sync", dft1[:], k.c_dft1, writes=[Rc1])
    P.dma("sync", tw[:], k.c_tw, writes=[Rc1])
    P.dma("sync", X[:], k.U.rearrange("(a b) c -> a (b c)", b=64), writes=[RX])
    t1 = [A.alloc([N1, 256], F32) for _ in range(2)]
    t2 = [A.alloc([N1, 256], F32) for _ in range(2)]
    Rt1, Rt2 = [Res(), Res()], [Res(), Res()]
    RBd = Res()
    it = 0
    for n in range(32):
        if n > 0 and n % 8 == 0:
            q = n // 8 - 1
            P.dma("gpsimd", k.Bd[:, q * 16:(q + 1) * 16, :, :], Bsb[:, q * 16:(q + 1) * 16, :, :], reads=[RB], writes=[RBd])
        ba, bs = 2 * (n % 2), 2 * (n % 2) + 1
        P.op("tensor", MM(pb[ba][:N1, :], dft1[:, 0:N1], X[:, n * 512:(n + 1) * 512]), [Rc1, RX], [PB[ba]])
        P.op("tensor", MM(pb[bs][:N1, :], dft1[:, N1:2 * N1], X[:, n * 512:(n + 1) * 512]), [Rc1, RX], [PB[bs]])
        for hh in range(2):
            l2 = 2 * n + hh
            cs = slice(hh * 256, (hh + 1) * 256)
            b = it % 2
            it += 1
            P.op("scalar", ACTF(t1[b][:], pb[ba][:N1, cs], AF.Copy, scale=tw[:, 0, l2:l2 + 1]), [PB[ba], Rc1], [Rt1[b]])
            P.op("scalar", ACTF(t2[b][:], pb[ba][:N1, cs], AF.Copy, scale=tw[:, 1, l2:l2 + 1]), [PB[ba], Rc1], [Rt2[b]])
            P.op("vector", STT(Bsb[:, l2, 0, :], pb[bs][:N1, cs], tw[:, 1, l2:l2 + 1], t1[b][:], ALU.mult, ALU.add),
                 [PB[bs], Rc1, Rt1[b]], [RB])
            P.op("vector", STT(Bsb[:, l2, 1, :], pb[bs][:N1, cs], tw[:, 2, l2:l2 + 1], t2[b][:], ALU.mult, ALU.add),
                 [PB[bs], Rc1, Rt2[b]], [RB])
    P.dma("gpsimd", k.Bd[:, 48:64, :, :], Bsb[:, 48:64, :, :], reads=[RB], writes=[RBd])
    P.barrier()
    A.reset(mark)
    B2 = A.alloc([64, N1, 2, 128], BF16)
    YTs = A.alloc([128, 2, 2, L], BF16)
    RB2, RYT = Res(), Res()
    ev = 0
    for cc in range(2):
        for ri in range(2):
            P.dma("sync", B2[:, :, ri, :], k.Bd[:, :, ri, cc * 128:(cc + 1) * 128].rearrange("k l c -> l k c"),
                  reads=[RBd], writes=[RB2])
        for k1 in range(N1):
            bk = (k1 // 4) % 4
            slot = k1 % 4
            oap = pb[bk][:, slot * 128:(slot + 1) * 128]
            P.op("tensor", MM(oap, B2[:, k1, 0, :], dft2[:, 0, :], True, False), [RB2, Rw], [PB[bk]])
            P.op("tensor", MM(oap, B2[:, k1, 1, :], dft2[:, 1, :], False, True), [RB2, Rw], [PB[bk]])
            if slot == 3:
                for ri in range(2):
                    src = pb[bk][:, :].rearrange("p (s r q) -> p s r q", s=4, r=2)[:, :, ri, :]
                    dst = YTs[:, cc, ri, :].rearrange("p (q a) -> p a q", a=N1)[:, k1 - 3:k1 + 1, :]
                    if ev % 2 == 0:
                        P.op("vector", CP(dst, src), [PB[bk]], [RYT])
                    else:
                        P.op("scalar", ACTF(dst, src, AF.Copy), [PB[bk]], [RYT])
                    ev += 1
    gz = [A.alloc([128, 512], BF16) for _ in range(2)]
    yo = [A.alloc([128, 512], BF16) for _ in range(2)]
    Rgz, Ryo = [Res(), Res()], [Res(), Res()]
    it = 0
    for gi in range(L // 512):
        sl = slice(gi * 512, (gi + 1) * 512)
        for oc in range(2):
            b = it % 2
            bk = 4 + it % 4
            it += 1
            P.dma("sync", gz[b][:], k.FT[oc, :, sl], writes=[Rgz[b]])
            n = 0
            for cc in range(2):
                for ri in range(2):
                    P.op("tensor", MM(pb[bk][:, :], wmix[:, cc, ri, oc * 128:(oc + 1) * 128], YTs[:, cc, ri, sl], n == 0, n == 3),
                         [Rmix, RYT], [PB[bk]])
                    n += 1
            P.op("vector", TT(yo[b][:], pb[bk][:, :], gz[b][:], ALU.mult), [PB[bk], Rgz[b]], [Ryo[b]])
            P.dma("gpsimd", k.YT[oc, :, sl], yo[b][:], reads=[Ryo[b]])


def phase_na(k, lay):
    nc, P, L = k.nc, k.P, k.L
    rows = L // 64
    A = k.arena
    A.reset()
    pb, PB = k.pb, k.PB
    negt = A.alloc([64, 512], F32)
    Rneg = Res()
    P.op("vector", MSET(negt[:], NEG), writes=[Rneg])
    Rfill = {}
    Rdiag = []
    for h in range(4):
        for dl in range(8):
            Rfill[(h, dl)] = Res()
            P.dma("sync" if (h * 8 + dl) % 2 == 0 else "scalar", k.NAB[h, dl, :, :], negt[:], reads=[Rneg], writes=[Rfill[(h, dl)]])
    nabt = k.NAB.tensor
    rpbt = k.rpb.tensor
    n = 0
    for h in range(4):
        for dl in range(8):
            dbase = (h * 8 + dl) * 64 * 512
            sbase = ((lay * 4 + h) * 15 + (7 - dl)) * 31
            q = "sync" if n % 2 == 0 else "scalar"
            n += 1
            rf = [Rfill[(h, dl)]]
            r_ = Res()
            Rdiag.append(r_)
            P.dma(q, bass.AP(nabt, dbase + 8 * 512, [[513, 49], [64, 8], [1, 16]]),
                  bass.AP(rpbt, sbase + 7, [[0, 49], [31, 8], [1, 16]]), reads=rf, writes=[r_])
            r_ = Res()
            Rdiag.append(r_)
            P.dma(q, bass.AP(nabt, dbase, [[64, 8], [512, 8], [1, 16]]),
                  bass.AP(rpbt, sbase + 15, [[31, 8], [-1, 8], [1, 16]]), reads=rf, writes=[r_])
            r_ = Res()
            Rdiag.append(r_)
            P.dma(q, bass.AP(nabt, dbase + 57 * 512 + 48, [[64, 8], [512, 7], [1, 16]]),
                  bass.AP(rpbt, sbase + 6, [[31, 8], [-1, 7], [1, 16]]), reads=rf, writes=[r_])
    P.barrier()
    A.reset()
    tb = A.alloc([64, 8, 512], F32)
    tbb = A.alloc([64, 8, 512], BF16)
    Rtbb = Res()
    qT = A.alloc([64, L], BF16)
    kT = A.alloc([64, L], BF16)
    gzh = A.alloc([64, L], BF16)
    vA = A.alloc([128, rows // 2, 64], BF16)
    vB = A.alloc([128, rows // 2 - 1, 64], BF16)
    yrow = A.alloc([64, L], BF16)
    Rtb, Rq, Rk, Rgz, Rv, Ry = Res(), Res(), Res(), Res(), Res(), Res()
    NB = 8
    s1 = [A.alloc([64, 512], F32) for _ in range(NB)]
    pp = [A.alloc([64, 512], F32) for _ in range(NB)]
    pn = [A.alloc([64, 512], BF16) for _ in range(NB)]
    den = [A.alloc([64, 2], F32) for _ in range(NB)]
    pTs = [A.alloc([128, 4, 64], BF16) for _ in range(NB)]
    Rs1, Rpp, Rpn, Rden, RpT = ([Res() for _ in range(NB)] for _ in range(5))
    for h in range(4):
        p0 = (h % 2) * 64
        P.dma("sync", tb[:], k.NAB[h].rearrange("d w x -> w d x"), writes=[Rtb])
        P.dma("sync", qT[:], k.FT[14 + h // 2, p0:p0 + 64, :], writes=[Rq])
        P.dma("sync", kT[:], k.FT[16 + h // 2, p0:p0 + 64, :], writes=[Rk])
        P.dma("sync", gzh[:], k.FT[4 + h // 2, p0:p0 + 64, :], writes=[Rgz])
        P.dma("sync", vA[:], k.CV[:, h * 64:(h + 1) * 64].rearrange("(m p) d -> p m d", p=128), writes=[Rv])
        P.dma("sync", vB[:], k.CV[64:L - 64, h * 64:(h + 1) * 64].rearrange("(m p) d -> p m d", p=128), writes=[Rv])
        P.op("vector", TS(qT[:], qT[:], 0.125, None, ALU.mult), [Rq], [Rq])
        P.op("gpsimd", CP(tbb[:], tb[:]), [Rtb], [Rtbb])
        rsof = lambda r: min(max(r - 4, 0), rows - 8)

        def stA1(r):
            rs = rsof(r)
            bS = r % 2
            P.op("tensor", MM(pb[bS][0:64, :], qT[:, r * 64:(r + 1) * 64], kT[:, rs * 64:rs * 64 + 512], True, False), [Rq, Rk], [PB[bS]])
            P.op("tensor", MM(pb[bS][0:64, :], k.ident[0:64, 0:64], tbb[:, r - rs, :], False, True), [k.Rconst, Rtbb], [PB[bS]])

        def stA2(r):
            b, bS = r % NB, r % 2
            P.op("scalar", ACTF(pp[b][:], pb[bS][0:64, :], AF.Exp, accum_out=den[b][:, 0:1]), [PB[bS]], [Rpp[b], Rden[b]])

        def stA3(r):
            b = r % NB
            P.op("vector", RECIP(den[b][:, 1:2], den[b][:, 0:1]), [Rden[b]], [Rden[b]])
            P.op("vector", TS(pn[b][:], pp[b][:], den[b][:, 1:2], None, ALU.mult), [Rpp[b], Rden[b]], [Rpn[b]])

        def stB(r):
            b = r % NB
            bT = 2 + r % 2
            pTp = pb[bT][:, 0:256].bitcast(BF16)
            for i in range(4):
                P.op("tensor", TR(pTp[:, i * 64:(i + 1) * 64], pn[b][:, i * 128:(i + 1) * 128], k.ident[0:64, 0:64]),
                     [Rpn[b], k.Rconst], [PB[bT]])

        def stB2(r):
            b = r % NB
            bT = 2 + r % 2
            pTp = pb[bT][:, 0:256].bitcast(BF16)
            P.op("vector", CP(pTs[b][:].rearrange("p i q -> p (i q)"), pTp[:, 0:256]), [PB[bT]], [RpT[b]])

        def stC(r):
            rs = rsof(r)
            b = r % NB
            bo = 4 + (r // 8) % 2
            slot = r % 8
            V, m0 = (vA, rs // 2) if rs % 2 == 0 else (vB, (rs - 1) // 2)
            for i in range(4):
                P.op("tensor", MM(pb[bo][0:64, slot * 64:(slot + 1) * 64], V[:, m0 + i, :], pTs[b][:, i, :], i == 0, i == 3),
                     [Rv, RpT[b]], [PB[bo]])
            if slot == 7:
                sl = slice((r - 7) * 64, (r + 1) * 64)
                P.op("vector", TT(yrow[:, sl], pb[bo][0:64, :], gzh[:, sl], ALU.mult), [PB[bo], Rgz], [Ry])

        pipeline(rows, [stA1, stA2, stA3, stB, stB2, stC])
        P.dma("gpsimd", k.YT[4 + h // 2, p0:p0 + 64, :], yrow[:], reads=[Ry])

def phase_gdn(k, lay):
    nc, P, L = k.nc, k.P, k.L
    A = k.arena
    A.reset()
    pb, PB = k.pb, k.PB
    id64 = k.ident[0:64, 0:64]
    cw = A.alloc([128, 6, 5], F32)
    Dg = A.alloc([128, 6, 5, 128], BF16)
    Rcw, RDg = Res(), Res()
    for c in range(6):
        P.dma("sync", cw[:, c, :], k.conv_w[lay, :, c * 128:(c + 1) * 128].rearrange("j p -> p j"), writes=[Rcw],
              allow_slow_non_contiguous=True)
    for c in range(6):
        for j in range(5):
            P.op("vector", TS(Dg[:, c, j, :], k.identf[:], cw[:, c, j:j + 1], None, ALU.mult), [Rcw, k.Rconst], [RDg])
    xc = [A.alloc([128, 6, 516], BF16) for _ in range(2)]
    actf = [A.alloc([128, 4, 512], F32) for _ in range(3)]
    sq = [A.alloc([128, 4, 512], BF16) for _ in range(2)]
    lnt = [A.alloc([128, 4, 512], F32) for _ in range(2)]
    qn = [A.alloc([128, 6, 512], BF16) for _ in range(3)]
    tm = [A.alloc([128, 6, 4, 128], BF16) for _ in range(2)]
    Rxc = [[Res() for _ in range(6)] for _ in range(2)]
    Ract = [[Res() for _ in range(4)] for _ in range(3)]
    Rsq = [[Res() for _ in range(4)] for _ in range(2)]
    Rln = [[Res() for _ in range(4)] for _ in range(2)]
    Rqn = [[Res() for _ in range(6)] for _ in range(3)]
    Rtm = [[Res() for _ in range(6)] for _ in range(2)]
    ng = L // 512

    def gA(gi):
        b = gi % 2
        tok0 = gi * 512
        for c in range(6):
            lo = tok0 - 2 if gi > 0 else tok0
            hi = tok0 + 514 if gi < ng - 1 else tok0 + 512
            if gi == 0:
                P.op("gpsimd", MSET(xc[b][:, c, 0:2], 0.0), writes=[Rxc[b][c]])
            if gi == ng - 1:
                P.op("gpsimd", MSET(xc[b][:, c, 514:516], 0.0), writes=[Rxc[b][c]])
            P.dma("sync", xc[b][:, c, (lo - (tok0 - 2)):(hi - (tok0 - 2))], k.FT[8 + c, :, lo:hi], writes=[Rxc[b][c]])
        for c in range(6):
            bk = c % 4
            for j in range(5):
                P.op("tensor", MM(pb[bk][:, :], Dg[:, c, j, :], xc[b][:, c, j:j + 512], j == 0, j == 4), [RDg, Rxc[b][c]], [PB[bk]])
            if c < 4:
                P.op("scalar", ACTF(actf[gi % 3][:, c, :], pb[bk][:, :], AF.Silu), [PB[bk]], [Ract[gi % 3][c]])
            else:
                P.op("scalar", ACTF(qn[gi % 3][:, c, :], pb[bk][:, :], AF.Silu), [PB[bk]], [Rqn[gi % 3][c]])

    def gB(gi):
        b = gi % 2
        for c in range(4):
            P.op("gpsimd", TT(sq[b][:, c, :], actf[gi % 3][:, c, :], actf[gi % 3][:, c, :], ALU.mult), [Ract[gi % 3][c]], [Rsq[b][c]])
            bk2 = 4 + c % 2
            P.op("tensor", MM(pb[bk2][:, :], k.bones[:], sq[b][:, c, :]), [k.Rconst, Rsq[b][c]], [PB[bk2]])
            P.op("scalar", ACTF(lnt[b][:, c, :], pb[bk2][:, :], AF.Ln, bias=EPS), [PB[bk2]], [Rln[b][c]])
        for c in range(4):
            P.op("scalar", ACTF(lnt[b][:, c, :], lnt[b][:, c, :], AF.Exp, scale=-0.5), [Rln[b][c]], [Rln[b][c]])

    def gC(gi):
        b = gi % 2
        q3 = gi % 3
        tok0 = gi * 512
        for c in range(4):
            P.op("vector", STT(qn[q3][:, c, :], actf[q3][:, c, :], 0.125 if c < 2 else 1.0, lnt[b][:, c, :], ALU.mult, ALU.mult),
                 [Ract[q3][c], Rln[b][c]], [Rqn[q3][c]])
            P.dma("gpsimd", k.QKn[c, :, tok0:tok0 + 512], qn[q3][:, c, :], reads=[Rqn[q3][c]])
        for c in range(6):
            bk = 6 + c % 2
            pT = pb[bk][:, 0:256].bitcast(BF16)
            for t in range(4):
                P.op("tensor", TR(pT[:, t * 128:(t + 1) * 128], qn[q3][:, c, t * 128:(t + 1) * 128], k.ident[:]),
                     [Rqn[q3][c], k.Rconst], [PB[bk]])
            P.op("vector", CP(tm[b][:, c, :, :], pT.rearrange("p (t c) -> p t c", t=4)), [PB[bk]], [Rtm[b][c]])
            P.dma("gpsimd", k.QKVt[tok0:tok0 + 512, c * 128:(c + 1) * 128].rearrange("(t p) c -> p t c", p=128), tm[b][:, c, :, :],
                  reads=[Rtm[b][c]])

    pipeline(ng, [gA, gB, gC])
    P.barrier()
    A.reset()
    ntile = L // 128
    gm = A.alloc([128, 12, 128], F32)
    rmask = A.alloc([128, 2], F32)
    idr = A.alloc([128, 128], F32R)
    Rgm = Res()
    P.dma("sync", gm[:], k.c_gmask, writes=[Rgm])
    P.op("vector", CP(idr[:], k.identf[:]), [k.Rconst], [Rgm])
    P.op("vector", CP(rmask[:, :], gm[:, 10:12, 0]), [Rgm], [Rgm])
    MRk = [gm[:, 0, :], gm[:, 1, :]]
    Tm = [gm[:, 2, :], gm[:, 3, :]]
    INCLk = [gm[:, 4, :], gm[:, 5, :]]
    STRk = [gm[:, 6, :], gm[:, 7, :]]
    M2 = [gm[:, 8, :], gm[:, 9, :]]
    ONEC = [gm[:, 10, :], gm[:, 11, :]]
    QKg = [[A.alloc([64, 8, 512], BF16) for _ in range(2)] for _ in range(2)]
    TMg = [[A.alloc([128, 4, 768], BF16) for _ in range(2)] for _ in range(2)]
    Gg = [[A.alloc([128, 4, 16], F32) for _ in range(2)] for _ in range(2)]
    Rgrp = [[Res(), Res()], [Res(), Res()]]
    S32 = [A.alloc([64, 4, 64], F32) for _ in range(2)]
    Sbf = [A.alloc([64, 4, 64], BF16) for _ in range(2)]
    RS32, RSbf = [Res(), Res()], [Res(), Res()]
    for d in range(2):
        P.op("vector", MSET(S32[d][:], 0.0), writes=[RS32[d]])
        P.op("vector", MSET(Sbf[d][:], 0.0), writes=[RSbf[d]])

    def al2(shape, dt):
        return [A.alloc(shape, dt) for _ in range(2)]

    Grhs, E, EMi, EMs, t1 = (al2([128, 4, 128], F32) for _ in range(5))
    EG = al2([128, 16], F32)
    ekm = al2([128, 2, 4], F32)
    nb = al2([128, 4], F32)
    be = al2([128, 4], F32)
    qkb, qkT = (al2([128, 4, 128], BF16) for _ in range(2))
    wT, qdT = (al2([64, 4, 128], BF16) for _ in range(2))
    qd, vn = (al2([128, 4, 64], BF16) for _ in range(2))
    kdm = [al2([128, 4, 64], BF16) for _ in range(2)]
    Rkdm = [[Res(), Res()], [Res(), Res()]]
    Rekm = [Res(), Res()]
    for b_ in range(2):
        P.op("vector", MSET(vn[b_][:], 0.0), writes=[Rgm])
    Rm = [al2([128, 4, 128], F32R) for _ in range(2)]
    XPt = [al2([128, 4, 2, 128], F32R) for _ in range(2)]
    Xm = [[XPt[s_][b_][:, :, 0, :] for b_ in range(2)] for s_ in range(2)]
    Pm = [[XPt[s_][b_][:, :, 1, :] for b_ in range(2)] for s_ in range(2)]
    osb = al2([128, 4, 64], F32)
    R_ = lambda: [Res(), Res()]
    RGrhs, RE, REMi, REMs, Rt1, REG, Rnb, Rbe, Rqkb, RqkT, RwT, Rqd, RqdT, Rkd, Rvn, Rosb = (R_() for _ in range(16))
    RPm = [R_() for _ in range(2)]
    RRm = [R_() for _ in range(2)]
    RXm = [R_() for _ in range(2)]
    NLV = 6
    all_steps = []
    for hs in range(2 * ntile):
        d = hs % 2
        ti = hs // 2
        tl = ti if d == 0 else ntile - 1 - ti
        n = tl % 4
        gb = (ti // 4) % 2
        b = d
        q = [0, 1, 2, 3] if d == 0 else [4, 5, 6, 7]
        stg = []
        cur = []

        def add(eng, fn, reads=(), writes=()):
            cur.append((eng, fn, tuple(reads), tuple(writes), False, None))

        def adddma(eng, out, in_, reads=(), writes=()):
            cur.append((eng, (out, in_), tuple(reads), tuple(writes), True, None))

        def stage():
            if cur:
                stg.append(list(cur))
                del cur[:]

        if ti % 4 == 0:
            g0 = (tl // 4) * 512
            sl = slice(g0, g0 + 512)
            for h in range(4):
                p0 = (h % 2) * 64
                adddma("sync", QKg[d][gb][:, h, :], k.QKn[h // 2, p0:p0 + 64, sl], writes=[Rgrp[d][gb]])
                adddma("sync", QKg[d][gb][:, 4 + h, :], k.QKn[2 + h // 2, p0:p0 + 64, sl], writes=[Rgrp[d][gb]])
            adddma("sync", TMg[d][gb][:], k.QKVt[sl, :].rearrange("(n p) c -> p n c", p=128), writes=[Rgrp[d][gb]])
            adddma("sync", Gg[d][gb][:], k.G[sl, :].rearrange("(n p) c -> p n c", p=128), writes=[Rgrp[d][gb]])
        QK, TM, GG, RG = QKg[d][gb], TMg[d][gb], Gg[d][gb], Rgrp[d][gb]
        cs = slice(n * 128, n * 128 + 128)
        gcol = GG[:, n, 4 * d:4 * d + 4]
        bcol = GG[:, n, 8 + 4 * d:12 + 4 * d]
        bcw = lambda ap: ap.unsqueeze(2).to_broadcast([128, 4, 128])
        bc64 = lambda ap: ap.unsqueeze(2).to_broadcast([128, 4, 64])
        mk = lambda m: m.unsqueeze(1).to_broadcast([128, 4, 128])
        v4 = lambda ap: ap.rearrange("p (a b) -> p a b", a=4)
        fl = lambda ap: ap.rearrange("p a b -> p (a b)")
        add("gpsimd", TT(Grhs[b][:], mk(MRk[d]), bcw(gcol), ALU.mult), [Rgm, RG], [RGrhs[b]])
        for (qq, lt) in enumerate((Tm[d], M2[d], ONEC[0], ONEC[1])):
            add("tensor", MM(pb[q[1]][:, 4 * qq:4 * qq + 4], lt, gcol), [Rgm, RG], [PB[q[1]]])
        add("scalar", ACTF(EG[b][:], pb[q[1]][:, 0:16], AF.Exp), [PB[q[1]]], [REG[b]])
        for f in range(2):
            add("vector", TS(ekm[b][:, f, :], EG[b][:, 4:8], rmask[:, f:f + 1], None, ALU.mult), [REG[b], Rgm], [Rekm[b]])
        stage()
        add("tensor", MM(pb[q[0]][:, :], Tm[d], fl(Grhs[b][:])), [Rgm, RGrhs[b]], [PB[q[0]]])
        add("scalar", ACTF(fl(E[b][:]), pb[q[0]][:, :], AF.Exp), [PB[q[0]]], [RE[b]])
        for h in range(4):
            add("tensor", MM(pb[q[2]][:, h * 128:(h + 1) * 128], QK[:, 4 + h, cs], QK[:, 4 + h, cs]), [RG], [PB[q[2]]])
        for h in range(4):
            add("tensor", MM(pb[q[3]][:, h * 128:(h + 1) * 128], QK[:, h, cs], QK[:, 4 + h, cs]), [RG], [PB[q[3]]])
        add("vector", TS(nb[b][:], bcol, -1.0, None, ALU.mult), [RG], [Rnb[b]])
        add("vector", TT(be[b][:], bcol, EG[b][:, 0:4], ALU.mult), [RG, REG[b]], [Rbe[b]])
        stage()
        add("vector", TT(EMs[b][:], E[b][:], mk(STRk[d]), ALU.mult), [RE[b], Rgm], [REMs[b]])
        add("gpsimd", TT(EMi[b][:], E[b][:], mk(INCLk[d]), ALU.mult), [RE[b], Rgm], [REMi[b]])
        add("vector", TT(Xm[0][b][:, :, 0:64], v4(TM[:, n, 512:768]), bc64(bcol), ALU.mult), [RG], [RXm[0][b]])
        add("vector", TT(Xm[0][b][:, :, 64:128], v4(TM[:, n, 256:512]), bc64(be[b][:, :]), ALU.mult), [RG, Rbe[b]], [RXm[0][b]])
        stage()
        add("vector", TT(t1[b][:], v4(pb[q[2]][:, :]), EMs[b][:], ALU.mult), [PB[q[2]], REMs[b]], [Rt1[b]])
        add("vector", TT(Pm[0][b], t1[b][:], bcw(nb[b][:, :]), ALU.mult), [Rt1[b], Rnb[b]], [RPm[0][b]])
        add("vector", TT(qkb[b][:], v4(pb[q[3]][:, :]), EMi[b][:], ALU.mult), [PB[q[3]], REMi[b]], [Rqkb[b]])
        add("gpsimd", TT(qd[b][:], v4(TM[:, n, 0:256]), bc64(EG[b][:, 0:4]), ALU.mult), [RG, REG[b]], [Rqd[b]])
        for f in range(2):
            add("gpsimd", TT(kdm[f][b][:], v4(TM[:, n, 256:512]), bc64(ekm[b][:, f, :]), ALU.mult), [RG, Rekm[b]], [Rkdm[f][b]])
        stage()
        for h in range(4):
            add("tensor", MM(pb[q[0]][:, h * 128:(h + 1) * 128], Pm[0][b][:, h, :], idr[:, :]), [RPm[0][b], Rgm], [PB[q[0]]])
        add("scalar", ACTF(fl(Rm[0][b][:]), pb[q[0]][:, :], AF.Copy), [PB[q[0]]], [RRm[0][b]])
        pTb = pb[q[1]][:, 0:256].bitcast(BF16)
        for h in range(4):
            add("tensor", TR(pTb[:, h * 128:(h + 1) * 128], qkb[b][:, h, :], k.ident[:]), [Rqkb[b], k.Rconst], [PB[q[1]]])
        add("vector", CP(fl(qkT[b][:]), pTb[:, :]), [PB[q[1]]], [RqkT[b]])
        stage()
        for h in range(4):
            add("tensor", TR(pTb[0:64, h * 128:(h + 1) * 128], qd[b][:, h, :], k.ident[:]), [Rqd[b], k.Rconst], [PB[q[1]]])
        add("vector", CP(fl(qdT[b][:]), pTb[0:64, :]), [PB[q[1]]], [RqdT[b]])
        stage()
        for j in range(NLV):
            sj, sn = j % 2, (j + 1) % 2
            wide = j < NLV - 2
            if j < NLV - 1:
                for h in range(4):
                    add("tensor", MM(pb[q[1]][:, h * 128:(h + 1) * 128], Pm[sj][b][:, h, :], Rm[sj][b][:, h, :]),
                        [RRm[sj][b], RPm[sj][b]], [PB[q[1]]])
                add("scalar", ACTF(fl(Rm[sn][b][:]), pb[q[1]][:, :], AF.Copy), [PB[q[1]]], [RRm[sn][b]])
                stage()
            if wide:
                for h in range(4):
                    bk = q[2 + h // 2]
                    add("tensor", MM(pb[bk][:, (h % 2) * 256:(h % 2) * 256 + 256], Rm[sj][b][:, h, :],
                                     XPt[sj][b][:, h, :, :].rearrange("p a c -> p (a c)")),
                        [RRm[sj][b], RXm[sj][b], RPm[sj][b]], [PB[bk]])
                pv = k.pball[:, q[2] * 512:(q[2] + 2) * 512].rearrange("p (h a c) -> p h a c", h=4, a=2)
                add("vector", CP(Pm[sn][b], pv[:, :, 1, :]), [PB[q[2]], PB[q[3]]], [RPm[sn][b]])
                add("vector", TT(Xm[sn][b], pv[:, :, 0, :], Xm[sj][b], ALU.add), [PB[q[2]], PB[q[3]], RXm[sj][b]], [RXm[sn][b]])
            else:
                for h in range(4):
                    add("tensor", MM(pb[q[2]][:, h * 128:(h + 1) * 128], Rm[sj][b][:, h, :], Xm[sj][b][:, h, :]),
                        [RRm[sj][b], RXm[sj][b]], [PB[q[2]]])
                add("vector", TT(Xm[sn][b], v4(pb[q[2]][:, :]), Xm[sj][b], ALU.add), [PB[q[2]], RXm[sj][b]], [RXm[sn][b]])
            stage()
        XF = Xm[NLV % 2][b]
        RXF = RXm[NLV % 2][b]
        for h in range(4):
            add("tensor", MM(pb[q[0]][0:64, h * 128:(h + 1) * 128], XF[:, h, 64:128], idr[:, :]), [RXF, Rgm], [PB[q[0]]])
        add("scalar", ACTF(fl(wT[b][:]), pb[q[0]][0:64, :], AF.Copy), [PB[q[0]]], [RwT[b]])
        stage()
        for f in ((0, 1) if d == 0 else (1, 0)):
            rows = slice(64 * f, 64 * f + 64)
            for h in range(4):
                add("tensor", MM(pb[q[0]][:, h * 64:(h + 1) * 64], wT[b][:, h, :], Sbf[d][:, h, :]), [RwT[b], RSbf[d]], [PB[q[0]]])
            add("vector", TT(vn[b][rows, :, :], XF[rows, :, 0:64], v4(pb[q[0]][rows, 0:256]), ALU.subtract), [RXF, PB[q[0]]], [Rvn[b]])
            add("gpsimd", TT(S32[d][:], S32[d][:], EG[b][0:64, 8 + 4 * f:12 + 4 * f].unsqueeze(2).to_broadcast([64, 4, 64]), ALU.mult),
                [RS32[d], REG[b]], [RS32[d]])
            stage()
            for h in range(4):
                add("tensor", MM(pb[q[0]][0:64, h * 64:(h + 1) * 64], kdm[f][b][:, h, :], vn[b][:, h, :]), [Rkdm[f][b], Rvn[b]], [PB[q[0]]])
            for h in range(4):
                o = pb[q[1]][:, h * 64:(h + 1) * 64]
                add("tensor", MM(o, qdT[b][:, h, :], Sbf[d][:, h, :], True, False), [RqdT[b], RSbf[d]], [PB[q[1]]])
                add("tensor", MM(o, qkT[b][:, h, :], vn[b][:, h, :], False, True), [RqkT[b], Rvn[b]], [PB[q[1]]])
            add("vector", TT(S32[d][:], S32[d][:], v4(pb[q[0]][0:64, 0:256]), ALU.add), [RS32[d], PB[q[0]]], [RS32[d]])
            add("scalar", ACTF(Sbf[d][:], S32[d][:], AF.Copy), [RS32[d]], [RSbf[d]])
            add("scalar", ACTF(fl(osb[b][rows, :, :]), pb[q[1]][rows, 0:256], AF.Copy), [PB[q[1]]], [Rosb[b]])
            stage()
        adddma("gpsimd", (k.OF if d == 0 else k.OB)[tl * 128:tl * 128 + 128, :], fl(osb[b][:]), reads=[Rosb[b]])
        stage()
        all_steps.append(stg)
    nst = max(len(sg) for sg in all_steps)
    KS = nst // 2 + 1
    nhs = len(all_steps)
    for t in range((nhs - 1) * KS + nst):
        for i in range(max(0, (t - nst) // KS), min(nhs - 1, t // KS) + 1):
            si = t - i * KS
            if 0 <= si < len(all_steps[i]):
                for (eng, fn, reads, writes, isdma, _) in all_steps[i][si]:
                    if isdma:
                        P.dma(eng, fn[0], fn[1], reads=reads, writes=writes)
                    else:
                        P.op(eng, fn, reads, writes)
    P.barrier()
    A.reset()
    gng = A.alloc([128, 64], F32)
    Rgng = Res()
    P.dma("sync", gng[:], k.gdn_g[lay].partition_broadcast(128), writes=[Rgng])
    NB = 4
    aln = lambda shape, dt: [A.alloc(shape, dt) for _ in range(NB)]
    Rn = lambda: [Res() for _ in range(NB)]
    of, ob, osum, sqq, y1 = (aln([128, 256], F32) for _ in range(5))
    ss = aln([128, 8], F32)
    ytm = aln([128, 256], BF16)
    gz = al2([128, 2, 512], BF16)
    yo = al2([128, 2, 512], BF16)
    Rof, Rob, Ros, Rsqq, Ry1, Rss, Rytm = (Rn() for _ in range(7))
    Rgz, Ryo = R_(), R_()
    v4 = lambda ap: ap.rearrange("p (a b) -> p a b", a=4)

    def c0(i):
        gi, t = i // 4, i % 4
        g2 = gi % 2
        b = i % NB
        if t == 0:
            P.dma("sync", gz[g2][:], k.FT[2:4, :, gi * 512:(gi + 1) * 512].rearrange("c p l -> p c l"), writes=[Rgz[g2]])
        P.dma("sync", of[b][:], k.OF[i * 128:(i + 1) * 128, :], writes=[Rof[b]])
        P.dma("sync", ob[b][:], k.OB[i * 128:(i + 1) * 128, :], writes=[Rob[b]])
        P.op("vector", TT(osum[b][:], of[b][:], ob[b][:], ALU.add), [Rof[b], Rob[b]], [Ros[b]])
        P.op("gpsimd", TT(sqq[b][:], osum[b][:], osum[b][:], ALU.mult), [Ros[b]], [Rsqq[b]])

    def c1(i):
        b = i % NB
        P.op("vector", lambda e, o_=ss[b][:, 0:4], i_=v4(sqq[b][:]): e.reduce_sum(o_, i_, AX.X), [Rsqq[b]], [Rss[b]])
        P.op("scalar", ACTF(ss[b][:, 4:8], ss[b][:, 0:4], AF.Ln, bias=EPS, scale=1.0 / 64), [Rss[b]], [Rss[b]])
        P.op("scalar", ACTF(ss[b][:, 4:8], ss[b][:, 4:8], AF.Exp, scale=-0.5), [Rss[b]], [Rss[b]])

    def c2(i):
        b = i % NB
        P.op("vector", TT(v4(y1[b][:]), v4(osum[b][:]), ss[b][:, 4:8].unsqueeze(2).to_broadcast([128, 4, 64]), ALU.mult),
             [Ros[b], Rss[b]], [Ry1[b]])
        P.op("gpsimd", TT(v4(ytm[b][:]), v4(y1[b][:]), gng[:, :].unsqueeze(1).to_broadcast([128, 4, 64]), ALU.mult),
             [Ry1[b], Rgng], [Rytm[b]])

    def c3(i):
        gi, t = i // 4, i % 4
        g2 = gi % 2
        b = i % NB
        bk = 4 + i % 2
        pT = pb[bk][:, 0:128].bitcast(BF16)
        for j in range(2):
            P.op("tensor", TR(pT[:, j * 128:(j + 1) * 128], ytm[b][:, j * 128:(j + 1) * 128], k.ident[:]), [Rytm[b], k.Rconst], [PB[bk]])
        P.op("vector", TT(yo[g2][:, :, t * 128:(t + 1) * 128], pT.rearrange("p (c t) -> p c t", c=2),
                          gz[g2][:, :, t * 128:(t + 1) * 128], ALU.mult), [PB[bk], Rgz[g2]], [Ryo[g2]])
        if t == 3:
            for j in range(2):
                P.dma("gpsimd", k.YT[2 + j, :, gi * 512:(gi + 1) * 512], yo[g2][:, j, :], reads=[Ryo[g2]])

    pipeline(L // 128, [c0, c1, c2, c3])
```

```python
import math
import numpy as np
import ml_dtypes
from contextlib import ExitStack
import concourse.bass as bass
import concourse.mybir as mybir
from concourse.bass_utils import run_bass_kernel_spmd

F32 = mybir.dt.float32
BF16 = mybir.dt.bfloat16
F32R = mybir.dt.float32r
AF = mybir.ActivationFunctionType
ALU = mybir.AluOpType
AX = mybir.AxisListType

D_MODEL = 1024
DIN = 3088
DEPTH = 2
SEQ = 8192
NMEM = 256
EPS = 1e-6
NEG = -30000.0

ENGS = ("tensor", "vector", "scalar", "gpsimd", "sync")


class Res:
    __slots__ = ("name", "last_w", "readers")

    def __init__(self, name=""):
        self.name = name
        self.last_w = None
        self.readers = []


class Op:
    __slots__ = ("eng", "fn", "deps", "is_dma", "sig", "dma_slot", "needs")

    def __init__(self, eng, fn, is_dma):
        self.eng = eng
        self.fn = fn
        self.deps = []
        self.is_dma = is_dma
        self.sig = None
        self.needs = False
        self.dma_slot = None


class Prog:
    NDMA = 12

    def __init__(self, nc):
        self.nc = nc
        self.ops = {e: [] for e in ENGS}
        self.pending = None
        self.pending_done = set()

    def op(self, eng, fn, reads=(), writes=(), dma=False):
        o = Op(eng, fn, dma)
        deps = []
        for r in reads:
            if r.last_w is not None:
                deps.append(r.last_w)
        for w in writes:
            if w.last_w is not None:
                deps.append(w.last_w)
            deps.extend(w.readers)
        if self.pending is not None and eng not in self.pending_done:
            deps.extend(self.pending)
            self.pending_done.add(eng)
        seen = set()
        for d in deps:
            if id(d) in seen:
                continue
            seen.add(id(d))
            if d.eng == "tensor" and eng == "tensor" and not d.is_dma and not dma:
                continue
            o.deps.append(d)
            d.needs = True
        for r in reads:
            r.readers.append(o)
        for w in writes:
            w.last_w = o
            w.readers = []
        self.ops[eng].append(o)
        return o

    def dma(self, eng, out, in_, reads=(), writes=(), **kw):
        return self.op(eng, lambda e: e.dma_start(out=out, in_=in_, **kw), reads, writes, dma=True)

    def barrier(self):
        deps = []
        for e in ENGS:
            ops = self.ops[e]
            for o in reversed(ops):
                if not o.is_dma:
                    deps.append(o)
                    o.needs = True
                    break
            deps.extend([o for o in ops if o.is_dma][-self.NDMA:])
        self.pending = deps
        self.pending_done = set()

    def emit(self):
        nc = self.nc
        with ExitStack() as st:
            sems = {e: st.enter_context(nc.semaphore("s_" + e)) for e in ENGS}
            dsems = {e: [st.enter_context(nc.semaphore("d_%s_%d" % (e, i))) for i in range(self.NDMA)]
                     for e in ("sync", "gpsimd", "scalar")}
            for e in ENGS:
                cnt = 0
                dcnt = 0
                for o in self.ops[e]:
                    if o.is_dma:
                        o.dma_slot = dcnt
                        o.sig = (dsems[e][dcnt % self.NDMA], 16 * (dcnt // self.NDMA + 1))
                        dcnt += 1
                    elif o.needs:
                        cnt += 1
                        o.sig = (sems[e], cnt)
            block = st.enter_context(nc.Block())
            prog = self

            def run_engine(e, eng):
                waited = {}
                dma_list = [o for o in prog.ops[e] if o.is_dma]

                def wait(sem, val):
                    k = id(sem)
                    if waited.get(k, 0) >= val:
                        return
                    waited[k] = val
                    eng.wait_ge(sem, val)

                for o in prog.ops[e]:
                    for d in o.deps:
                        wait(*d.sig)
                    if o.is_dma and o.dma_slot >= prog.NDMA:
                        wait(*dma_list[o.dma_slot - prog.NDMA].sig)
                    ins = o.fn(eng)
                    if o.is_dma:
                        ins.then_inc(o.sig[0], 16)
                    elif o.needs:
                        ins.then_inc(o.sig[0], 1)
                for o in dma_list[-prog.NDMA:]:
                    wait(*o.sig)

            @block.tensor
            def _(eng):
                run_engine("tensor", eng)

            @block.vector
            def _(eng):
                run_engine("vector", eng)

            @block.scalar
            def _(eng):
                run_engine("scalar", eng)

            @block.gpsimd
            def _(eng):
                run_engine("gpsimd", eng)

            @block.sync
            def _(eng):
                run_engine("sync", eng)


def MM(out, lhsT, rhs, start=True, stop=True):
    return lambda e: e.matmul(out, lhsT, rhs, start=start, stop=stop)


def TR(out, in_, ident):
    return lambda e: e.transpose(out, in_, ident)


def ACTF(out, in_, func, **kw):
    return lambda e: e.activation(out, in_, func, **kw)


def TS(out, in0, s1, s2, op0, op1=None):
    if op1 is None:
        return lambda e: e.tensor_scalar(out, in0, s1, s2, op0)
    return lambda e: e.tensor_scalar(out, in0, s1, s2, op0, op1)


def TT(out, in0, in1, op):
    return lambda e: e.tensor_tensor(out, in0, in1, op)


def STT(out, in0, scalar, in1, op0, op1):
    return lambda e: e.scalar_tensor_tensor(out, in0, scalar, in1, op0, op1)


def CP(out, in_):
    return lambda e: e.tensor_copy(out, in_)


def MSET(ap, val):
    return lambda e: e.memset(ap, val)


def RECIP(out, in_):
    return lambda e: e.reciprocal(out, in_)


_uid = [0]


def _dsize(dt):
    return 4 if dt in (F32, F32R) else 2


class Arena:
    def __init__(self, nc, base, limit):
        self.nc = nc
        self.off = base
        self.base = base
        self.limit = limit

    def alloc(self, shape, dt):
        n = _dsize(dt)
        for s in shape[1:]:
            n *= s
        n = (n + 63) // 64 * 64
        _uid[0] += 1
        h = self.nc.alloc_sbuf_tensor_at("t%d" % _uid[0], list(shape), dt, offset=self.off)
        self.off += n
        assert self.off <= self.limit, ("SBUF arena overflow", self.off, self.limit)
        return h

    def reset(self, to=None):
        self.off = self.base if to is None else to


FM_CHUNKS = ([(256 + 128 * j, True, 0 + j) for j in range(2)] + [(512 + 128 * j, False, 8 + j) for j in range(6)] +
             [(1280 + 128 * j, True, 2 + j) for j in range(2)] + [(1552 + 128 * j, False, 14 + j) for j in range(4)] +
             [(2320 + 128 * j, True, 4 + j) for j in range(2)] + [(2576 + 128 * j, False, 18 + j) for j in range(2)] +
             [(2832 + 128 * j, True, 6 + j) for j in range(2)])


def pipeline(n, stages):
    for t in range(n + len(stages) - 1):
        for si, fn in enumerate(stages):
            i = t - si
            if 0 <= i < n:
                fn(i)


class K:
    pass


def build_program(L=SEQ, depth=DEPTH, debug=False, stop_after=None):
    nc = bass.Bass("TRN2", target_bir_lowering=False)
    P = Prog(nc)
    k = K()
    k.nc, k.P, k.L, k.depth = nc, P, L, depth
    N1 = L // 64
    k.N1 = N1

    def din(name, shape, dt=F32):
        return nc.dram_tensor(name, list(shape), dt, kind="ExternalInput").ap()

    skind = "ExternalOutput" if debug else "Internal"

    def dscr(name, shape, dt):
        return nc.dram_tensor(name, list(shape), dt, kind=skind).ap()

    k.x = din("x", [L, D_MODEL])
    k.mem = din("mem", [NMEM, D_MODEL])
    k.pre_g = din("pre_norm_g", [depth, D_MODEL])
    k.post_g = din("post_norm_g", [depth, D_MODEL])
    k.w_in = din("w_in", [depth, D_MODEL, DIN])
    k.w_fnet = din("w_fnet", [depth, 256, 256])
    k.conv_w = din("gdn_conv_w", [depth, 5, 768])
    k.a_log = din("gdn_a_log", [depth, 8])
    k.dt_bias = din("gdn_dt_bias", [depth, 8])
    k.gdn_g = din("gdn_norm_g", [depth, 64])
    k.rpb = din("na_rpb", [depth, 4, 15, 31])
    k.mem_g = din("mem_norm_g", [depth, D_MODEL])
    k.w_kv = din("w_mem_kv", [depth, D_MODEL, 512])
    k.w_out = din("w_out", [depth, D_MODEL, D_MODEL])
    k.c_ident = din("c_ident", [128, 128], BF16)
    k.c_identf = din("c_identf", [128, 128], F32)
    k.c_dft1 = din("c_dft1", [N1, 2 * N1], BF16)
    k.c_tw = din("c_tw", [N1, 3, 64], F32)
    k.c_dft2 = din("c_dft2", [64, 2, 128], BF16)
    k.c_bd64 = din("c_bd64", [128, 2, 128], BF16)
    k.c_gmask = din("c_gmask", [128, 12, 128], F32)
    k.c_bones = din("c_bones", [128, 128], BF16)
    k.y = nc.dram_tensor("y", [L, D_MODEL], F32, kind="ExternalOutput").ap()
    k.X1 = dscr("s_x1", [L, D_MODEL], F32)
    k.FT = dscr("s_ft", [20, 128, L], BF16)
    k.U = dscr("s_u", [L, 256], BF16)
    k.CV = dscr("s_cv", [L, 256], BF16)
    k.G = dscr("s_g", [L, 16], F32)
    k.YT = dscr("s_yt", [8, 128, L], BF16)
    k.Bd = dscr("s_bd", [N1, 64, 2, 256], BF16)
    k.QKn = dscr("s_qkn", [4, 128, L], BF16)
    k.QKVt = dscr("s_qkvt", [L, 768], BF16)
    k.OF = dscr("s_of", [L, 256], F32)
    k.OB = dscr("s_ob", [L, 256], F32)
    k.NAB = dscr("s_nab", [4, 8, 64, 512], F32)

    k.pball = nc.alloc_psum_tensor("pball", [128, 4096], F32)
    k.pb = [k.pball[:, i * 512:(i + 1) * 512] for i in range(8)]
    k.PB = [Res("pb%d" % i) for i in range(8)]

    SB_BASE = 16640
    SB_LIMIT = SB_BASE + 196608
    pa = Arena(nc, SB_BASE, SB_LIMIT)
    k.ident = pa.alloc([128, 128], BF16)
    k.identf = pa.alloc([128, 128], F32)
    k.bones = pa.alloc([128, 128], BF16)
    k.ones_bf = pa.alloc([128, 64], BF16)
    k.idr = pa.alloc([64, 64], F32R)
    k.Rconst = Res("const")
    P.dma("sync", k.ident[:], k.c_ident, writes=[k.Rconst])
    P.dma("sync", k.identf[:], k.c_identf, writes=[k.Rconst])
    P.dma("sync", k.bones[:], k.c_bones, writes=[k.Rconst])
    P.op("vector", MSET(k.ones_bf[:], 1.0), writes=[k.Rconst])
    P.op("vector", CP(k.idr[:], k.identf[0:64, 0:64]), [k.Rconst], [k.Rconst])
    k.arena = Arena(nc, pa.off, SB_LIMIT)

    for lay in range(depth):
        xin = k.x if lay == 0 else k.X1
        yout = k.y if lay == depth - 1 else k.X1
        phase1(k, lay, xin)
        P.barrier()
        if stop_after == "p1":
            break
        phase_mem(k, lay)
        P.barrier()
        if stop_after == "mem":
            break
        phase_fnet(k, lay)
        P.barrier()
        if stop_after == "fnet":
            break
        phase_na(k, lay)
        P.barrier()
        if stop_after == "na":
            break
        phase_gdn(k, lay)
        P.barrier()
        if stop_after == "gdn":
            break
        phase_out(k, lay, xin, yout)
        P.barrier()
    P.emit()
    return nc


def rms_tile(k, xb, Rxb, junk, Rjunk, st, Rst, xs, Rxs, width=D_MODEL):
    P = k.P
    P.op("scalar", ACTF(junk[:], xb[:], AF.Square, accum_out=st[:, 0:1]), [Rxb], [Rjunk, Rst])
    P.op("scalar", ACTF(st[:, 1:2], st[:, 0:1], AF.Ln, bias=EPS, scale=1.0 / width), [Rst], [Rst])
    P.op("scalar", ACTF(st[:, 2:3], st[:, 1:2], AF.Exp, scale=-0.5), [Rst], [Rst])
    P.op("vector", TS(xs[:], xb[:], st[:, 2:3], None, ALU.mult), [Rxb, Rst], [Rxs])


def phase1(k, lay, xin):
    nc, P, L = k.nc, k.P, k.L
    A = k.arena
    A.reset()
    pb, PB = k.pb, k.PB
    wst = [A.alloc([128, DIN], F32) for _ in range(4)]
    Rwst = [Res() for _ in range(4)]
    wbf = A.alloc([128, 8, DIN], BF16)
    Rwbf = Res()
    gpre = A.alloc([128, 8], F32)
    Rg = Res()
    P.dma("sync", gpre[:], k.pre_g[lay].rearrange("(k p) -> p k", p=128), writes=[Rg], allow_slow_non_contiguous=True)
    def wload(kk):
        P.dma("sync" if kk % 2 == 0 else "scalar", wst[kk % 4][:], k.w_in[lay, kk * 128:(kk + 1) * 128, :], writes=[Rwst[kk % 4]])

    for kk in range(4):
        wload(kk)
    for kk in range(8):
        if kk % 2 == 0:
            P.op("vector", TS(wbf[:, kk, :], wst[kk % 4][:], gpre[:, kk:kk + 1], None, ALU.mult), [Rwst[kk % 4], Rg], [Rwbf])
        else:
            P.op("scalar", ACTF(wbf[:, kk, :], wst[kk % 4][:], AF.Copy, scale=gpre[:, kk:kk + 1]), [Rwst[kk % 4], Rg], [Rwbf])
        if kk + 4 < 8:
            wload(kk + 4)
    dtb = A.alloc([128, 8], F32)
    negA = A.alloc([128, 8], F32)
    Rgc = Res()
    P.dma("sync", dtb[:], k.dt_bias[lay].partition_broadcast(128), writes=[Rgc])
    P.dma("sync", negA[:], k.a_log[lay].partition_broadcast(128), writes=[Rgc])
    P.op("scalar", ACTF(negA[:], negA[:], AF.Exp), [Rgc], [Rgc])
    P.op("vector", TS(negA[:], negA[:], -1.0, None, ALU.mult), [Rgc], [Rgc])

    NXB = 8
    xt = [A.alloc([128, D_MODEL], F32) for _ in range(NXB)]
    Rxt = [Res() for _ in range(NXB)]
    junk = A.alloc([128, D_MODEL], F32)
    Rjunk = Res()
    stt = [A.alloc([128, 4], F32) for _ in range(NXB)]
    Rstt = [Res() for _ in range(NXB)]
    xs = [A.alloc([128, D_MODEL], BF16) for _ in range(NXB)]
    Rxs = [Res() for _ in range(NXB)]
    hxT = [A.alloc([128, 8, 512], BF16) for _ in range(3)]
    RhxT = [Res() for _ in range(3)]
    fo = [A.alloc([128, 512], BF16) for _ in range(4)]
    Rfo = [Res() for _ in range(4)]
    tmo = [A.alloc([128, 512], BF16) for _ in range(2)]
    Rtmo = [Res(), Res()]
    gw = A.alloc([128, 4, 16], F32)
    gout = A.alloc([128, 4, 16], F32)
    Rgw, Rgout = Res(), Res()
    pT = pb[7][:, :].bitcast(BF16)
    ngroups = L // 512

    def prepA(gi):
        for t in range(4):
            tok0 = gi * 512 + t * 128
            b = (gi * 4 + t) % NXB
            P.dma("sync", xt[b][:], xin[tok0:tok0 + 128, :], writes=[Rxt[b]])
            rms_tile(k, xt[b], Rxt[b], junk, Rjunk, stt[b], Rstt[b], xs[b], Rxs[b])

    def prepB(gi):
        hb = gi % 3
        for t in range(4):
            b = (gi * 4 + t) % NXB
            for kk in range(8):
                P.op("tensor", TR(pT[:, kk * 128:(kk + 1) * 128], xs[b][:, kk * 128:(kk + 1) * 128], k.ident[:]),
                     [Rxs[b], k.Rconst], [PB[7]])
            P.op("vector", CP(hxT[hb][:, :, t * 128:(t + 1) * 128], pT.rearrange("p (k t) -> p k t", k=8)),
                 [PB[7]], [RhxT[hb]])

    def mmG(gi):
        hb = gi % 3
        for ci, (col0, is_silu, dst) in enumerate(FM_CHUNKS):
            bk = ci % 4
            for kk in range(8):
                P.op("tensor", MM(pb[bk][:, :], wbf[:, kk, col0:col0 + 128], hxT[hb][:, kk, :], kk == 0, kk == 7),
                     [Rwbf, RhxT[hb]], [PB[bk]])
            if is_silu:
                P.op("scalar", ACTF(fo[bk][:], pb[bk][:, :], AF.Silu), [PB[bk]], [Rfo[bk]])
            else:
                P.op("vector", CP(fo[bk][:], pb[bk][:, :]), [PB[bk]], [Rfo[bk]])
            P.dma("gpsimd", k.FT[dst, :, gi * 512:(gi + 1) * 512], fo[bk][:], reads=[Rfo[bk]])
        for t in range(4):
            tok0 = gi * 512 + t * 128
            bk = 4 + t % 2
            for (oap, c0, c1, bres) in ((pb[bk][:, 0:256], 0, 256, PB[bk]), (pb[bk][:, 256:512], 2064, 2320, PB[bk]),
                                        (pb[6][:, t * 16:(t + 1) * 16], 1536, 1552, PB[6])):
                for kk in range(8):
                    lt = hxT[hb][:, kk, t * 128:(t + 1) * 128]
                    P.op("tensor", MM(oap, lt, wbf[:, kk, c0:c1], kk == 0, kk == 7), [Rwbf, RhxT[hb]], [bres])
            ob = tmo[t % 2]
            P.op("vector", CP(ob[:], pb[bk][:, :]), [PB[bk]], [Rtmo[t % 2]])
            P.dma("gpsimd", k.U[tok0:tok0 + 128, :], ob[:, 0:256], reads=[Rtmo[t % 2]])
            P.dma("gpsimd", k.CV[tok0:tok0 + 128, :], ob[:, 256:512], reads=[Rtmo[t % 2]])
        pg = pb[6][:, 0:64].rearrange("p (t c) -> p t c", t=4)
        P.op("vector", TT(gw[:, :, 0:8], pg[:, :, 0:8], dtb[:, :].unsqueeze(1).to_broadcast([128, 4, 8]), ALU.add),
             [PB[6], Rgc], [Rgw])
        P.op("scalar", ACTF(gw[:, :, 0:8], gw[:, :, 0:8], AF.Exp), [Rgw], [Rgw])
        P.op("scalar", ACTF(gw[:, :, 0:8], gw[:, :, 0:8], AF.Ln, bias=1.0), [Rgw], [Rgw])
        P.op("scalar", ACTF(gw[:, :, 8:16], pg[:, :, 8:16], AF.Exp, scale=-1.0), [PB[6], Rgw], [Rgw])
        P.op("vector", TT(gout[:, :, 0:8], gw[:, :, 0:8], negA[:, :].unsqueeze(1).to_broadcast([128, 4, 8]), ALU.mult),
             [Rgw, Rgc], [Rgout])
        P.op("vector", TS(gw[:, :, 8:16], gw[:, :, 8:16], 1.0, None, ALU.add), [Rgw], [Rgw])
        P.op("vector", RECIP(gout[:, :, 8:16], gw[:, :, 8:16]), [Rgw], [Rgout])
        P.dma("gpsimd", k.G[gi * 512:(gi + 1) * 512, :].rearrange("(t p) c -> p t c", p=128), gout[:], reads=[Rgout])

    pipeline(ngroups, [prepA, prepB, mmG])


def phase_out(k, lay, xin, yout):
    nc, P, L = k.nc, k.P, k.L
    A = k.arena
    A.reset()
    pb, PB = k.pb, k.PB
    wst = [A.alloc([128, D_MODEL], F32) for _ in range(2)]
    Rwst = [Res(), Res()]
    wob = A.alloc([128, 8, D_MODEL], BF16)
    Rwob = Res()
    for kk in range(8):
        P.dma("sync", wst[kk % 2][:], k.w_out[lay, kk * 128:(kk + 1) * 128, :], writes=[Rwst[kk % 2]])
        if kk % 2 == 0:
            P.op("vector", CP(wob[:, kk, :], wst[kk % 2][:]), [Rwst[kk % 2]], [Rwob])
        else:
            P.op("scalar", ACTF(wob[:, kk, :], wst[kk % 2][:], AF.Copy), [Rwst[kk % 2]], [Rwob])
    gpost = A.alloc([128, D_MODEL], F32)
    Rgp = Res()
    P.dma("sync", gpost[:], k.post_g[lay].partition_broadcast(128), writes=[Rgp])
    NB = 4
    ycat = [A.alloc([128, 8, 512], BF16) for _ in range(2)]
    Ryc = [Res(), Res()]
    osb = [A.alloc([128, D_MODEL], F32) for _ in range(NB)]
    Ros = [Res() for _ in range(NB)]
    xr = [A.alloc([128, D_MODEL], F32) for _ in range(NB)]
    Rxr = [Res() for _ in range(NB)]
    junk = A.alloc([128, 512], BF16)
    Rjunk = Res()
    stt = [A.alloc([128, 8], F32) for _ in range(NB)]
    Rst = [Res() for _ in range(NB)]

    def s0(i):
        gi, t = i // 4, i % 4
        yb = gi % 2
        if i == 0:
            P.dma("sync", ycat[0][:], k.YT[:, :, 0:512].rearrange("c p l -> p c l"), writes=[Ryc[0]])
        if t == 0 and (gi + 1) * 512 < L:
            P.dma("sync", ycat[(gi + 1) % 2][:], k.YT[:, :, (gi + 1) * 512:(gi + 2) * 512].rearrange("c p l -> p c l"),
                  writes=[Ryc[(gi + 1) % 2]])
        b = i % NB
        P.dma("sync", xr[b][:], xin[i * 128:(i + 1) * 128, :], writes=[Rxr[b]])
        for half in range(2):
            bk = half + 2 * (i % 2)
            for kk in range(8):
                P.op("tensor", MM(pb[bk][:, :], ycat[yb][:, kk, t * 128:(t + 1) * 128],
                                  wob[:, kk, half * 512:(half + 1) * 512], kk == 0, kk == 7), [Ryc[yb], Rwob], [PB[bk]])
            P.op("scalar", ACTF(junk[:], pb[bk][:, :], AF.Square, accum_out=stt[b][:, half:half + 1]), [PB[bk]], [Rjunk, Rst[b]])

    def s1(i):
        b = i % NB
        st = stt[b]
        P.op("vector", TT(st[:, 2:3], st[:, 0:1], st[:, 1:2], ALU.add), [Rst[b]], [Rst[b]])
        P.op("scalar", ACTF(st[:, 3:4], st[:, 2:3], AF.Ln, bias=EPS, scale=1.0 / D_MODEL), [Rst[b]], [Rst[b]])
        P.op("scalar", ACTF(st[:, 4:5], st[:, 3:4], AF.Exp, scale=-0.5), [Rst[b]], [Rst[b]])
        for half in range(2):
            bk = half + 2 * (i % 2)
            P.op("scalar", ACTF(osb[b][:, half * 512:(half + 1) * 512], pb[bk][:, :], AF.Copy, scale=st[:, 4:5]),
                 [PB[bk], Rst[b]], [Ros[b]])

    def s2(i):
        b = i % NB
        P.op("vector", TT(osb[b][:], osb[b][:], gpost[:], ALU.mult), [Ros[b], Rgp], [Ros[b]])
        P.op("gpsimd", TT(osb[b][:], osb[b][:], xr[b][:], ALU.add), [Ros[b], Rxr[b]], [Ros[b]])
        P.dma("sync", yout[i * 128:(i + 1) * 128, :], osb[b][:], reads=[Ros[b]])

    pipeline(L // 128, [s0, s1, s2])


def phase_mem(k, lay):
    nc, P, L = k.nc, k.P, k.L
    A = k.arena
    A.reset()
    pb, PB = k.pb, k.PB
    wst = [A.alloc([128, 512], F32) for _ in range(2)]
    Rwst = [Res(), Res()]
    wkv = A.alloc([128, 8, 512], BF16)
    Rwkv = Res()
    gm = A.alloc([128, 8], F32)
    Rgm = Res()
    P.dma("sync", gm[:], k.mem_g[lay].rearrange("(k p) -> p k", p=128), writes=[Rgm], allow_slow_non_contiguous=True)
    for kk in range(8):
        P.dma("sync", wst[kk % 2][:], k.w_kv[lay, kk * 128:(kk + 1) * 128, :], writes=[Rwst[kk % 2]])
        P.op("vector", TS(wkv[:, kk, :], wst[kk % 2][:], gm[:, kk:kk + 1], None, ALU.mult), [Rwst[kk % 2], Rgm], [Rwkv])
    xt = A.alloc([128, D_MODEL], F32)
    junk = A.alloc([128, D_MODEL], F32)
    st = A.alloc([128, 4], F32)
    xs = A.alloc([128, D_MODEL], BF16)
    Rxt, Rjunk, Rst, Rxs = Res(), Res(), Res(), Res()
    memT = A.alloc([128, 8, 256], BF16)
    RmemT = Res()
    pT = pb[7][:, :].bitcast(BF16)
    for t in range(2):
        P.dma("sync", xt[:], k.mem[t * 128:(t + 1) * 128, :], writes=[Rxt])
        rms_tile(k, xt, Rxt, junk, Rjunk, st, Rst, xs, Rxs)
        for kk in range(8):
            P.op("tensor", TR(pT[:, kk * 128:(kk + 1) * 128], xs[:, kk * 128:(kk + 1) * 128], k.ident[:]), [Rxs, k.Rconst], [PB[7]])
        P.op("vector", CP(memT[:, :, t * 128:(t + 1) * 128], pT.rearrange("p (k t) -> p k t", k=8)), [PB[7]], [RmemT])
    kmT = A.alloc([64, 4, 256], BF16)
    vm = A.alloc([128, 2, 256], BF16)
    Rkm, Rvm = Res(), Res()
    for h in range(4):
        bk = h % 2
        for kk in range(8):
            P.op("tensor", MM(pb[bk][0:64, 0:256], wkv[:, kk, h * 64:(h + 1) * 64], memT[:, kk, :], kk == 0, kk == 7),
                 [Rwkv, RmemT], [PB[bk]])
        P.op("vector", CP(kmT[:, h, :], pb[bk][0:64, 0:256]), [PB[bk]], [Rkm])
    for mc in range(2):
        bk = 2 + mc
        for kk in range(8):
            P.op("tensor", MM(pb[bk][:, 0:256], memT[:, kk, mc * 128:(mc + 1) * 128], wkv[:, kk, 256:512], kk == 0, kk == 7),
                 [Rwkv, RmemT], [PB[bk]])
        P.op("vector", CP(vm[:, mc, :], pb[bk][:, 0:256]), [PB[bk]], [Rvm])
    NB = 4
    mk = lambda shape, dt: [A.alloc(shape, dt) for _ in range(NB)]
    rs = lambda: [Res() for _ in range(NB)]
    qT, gz = mk([64, 512], BF16), mk([64, 512], BF16)
    PT = mk([128, 2, 512], BF16)
    rden, y1 = mk([64, 512], F32), mk([64, 512], F32)
    yo = mk([64, 512], BF16)
    Rq, Rgz, RPT, Rrd, Ry1, Ryo = rs(), rs(), rs(), rs(), rs(), rs()

    def geo(i):
        gi, h = i // 4, i % 4
        return slice(gi * 512, (gi + 1) * 512), h, (h % 2) * 64, i % NB

    def m0(i):
        sl, h, p0, b = geo(i)
        P.dma("sync", qT[b][:], k.FT[18 + h // 2, p0:p0 + 64, sl], writes=[Rq[b]])
        P.dma("sync", gz[b][:], k.FT[6 + h // 2, p0:p0 + 64, sl], writes=[Rgz[b]])
        for mc in range(2):
            bk = 4 * (i % 2) + mc
            P.op("tensor", MM(pb[bk][:, :], kmT[:, h, mc * 128:(mc + 1) * 128], qT[b][:]), [Rkm, Rq[b]], [PB[bk]])
            P.op("scalar", ACTF(PT[b][:, mc, :], pb[bk][:, :], AF.Exp, scale=0.125), [PB[bk]], [RPT[b]])

    def m1(i):
        sl, h, p0, b = geo(i)
        bo, bd = 4 * (i % 2) + 2, 4 * (i % 2) + 3
        for mc in range(2):
            P.op("tensor", MM(pb[bo][0:64, :], vm[:, mc, h * 64:(h + 1) * 64], PT[b][:, mc, :], mc == 0, mc == 1),
                 [Rvm, RPT[b]], [PB[bo]])
        for mc in range(2):
            P.op("tensor", MM(pb[bd][0:64, :], k.ones_bf[:, :], PT[b][:, mc, :], mc == 0, mc == 1),
                 [k.Rconst, RPT[b]], [PB[bd]])
        P.op("scalar", ACTF(rden[b][:], pb[bd][0:64, :], AF.Ln), [PB[bd]], [Rrd[b]])
        P.op("scalar", ACTF(rden[b][:], rden[b][:], AF.Exp, scale=-1.0), [Rrd[b]], [Rrd[b]])
        P.op("vector", TT(y1[b][:], pb[bo][0:64, :], rden[b][:], ALU.mult), [PB[bo], Rrd[b]], [Ry1[b]])

    def m2(i):
        sl, h, p0, b = geo(i)
        P.op("gpsimd", TT(yo[b][:], y1[b][:], gz[b][:], ALU.mult), [Ry1[b], Rgz[b]], [Ryo[b]])
        P.dma("gpsimd", k.YT[6 + h // 2, p0:p0 + 64, sl], yo[b][:], reads=[Ryo[b]])

    pipeline((L // 512) * 4, [m0, m1, m2])

def make_consts(L):
    N1 = L // 64
    bf = ml_dtypes.bfloat16
    c = {}
    c["c_ident"] = np.eye(128, dtype=np.float32).astype(bf)
    c["c_identf"] = np.eye(128, dtype=np.float32)
    l1 = np.arange(N1)
    ang1 = 2 * np.pi * np.outer(l1, l1) / N1
    c["c_dft1"] = np.concatenate([np.cos(ang1), np.sin(ang1)], axis=1).astype(np.float32).astype(bf)
    angt = 2 * np.pi * np.outer(np.arange(N1), np.arange(64)) / L
    sc = 1.0 / math.sqrt(L * 64.0)
    c["c_tw"] = np.stack([np.cos(angt) * sc, -np.sin(angt) * sc, -np.cos(angt) * sc], axis=1).astype(np.float32)
    a2 = 2 * np.pi * np.outer(np.arange(64), np.arange(64)) / 64
    C2, S2 = np.cos(a2), np.sin(a2)
    c["c_dft2"] = np.stack([np.concatenate([C2, -S2], 1), np.concatenate([S2, C2], 1)], axis=1).astype(np.float32).astype(bf)
    bdc = np.zeros((128, 128)); bds = np.zeros((128, 128))
    for b in range(2):
        bdc[b * 64:(b + 1) * 64, b * 64:(b + 1) * 64] = C2
        bds[b * 64:(b + 1) * 64, b * 64:(b + 1) * 64] = S2
    c["c_bd64"] = np.stack([bdc, bds], axis=1).astype(np.float32).astype(bf)
    j = np.arange(128)[:, None]
    s = np.arange(128)[None, :]
    same = (j // 64) == (s // 64)
    gm = np.zeros((128, 12, 128), np.float32)
    gm[:, 0] = (j > s) & same
    gm[:, 1] = (j < s) & same
    gm[:, 2] = (j <= s) & same
    gm[:, 3] = (j >= s) & same
    gm[:, 4] = (j >= s) & same
    gm[:, 5] = (j <= s) & same
    gm[:, 6] = (j > s) & same
    gm[:, 7] = (j < s) & same
    gm[:, 8] = (j > s) & same
    gm[:, 9] = (j < s) & same
    gm[:, 10] = (j < 64) & (s >= 0)
    gm[:, 11] = (j >= 64) & (s >= 0)
    c["c_gmask"] = gm
    bo = np.zeros((128, 128), np.float32)
    bo[:64, :64] = 1
    bo[64:, 64:] = 1
    c["c_bones"] = bo.astype(bf)
    return c


_prog_cache = {}


def run_cores(per_core_inputs, L, depth, debug=False, stop_after=None):
    key = (L, depth, debug, stop_after)
    nc = build_program(L, depth, debug, stop_after)
    consts = make_consts(L)
    in_maps = []
    for d in per_core_inputs:
        m = dict(consts)
        m.update(d)
        in_maps.append(m)
    res = run_bass_kernel_spmd(nc, in_maps, core_ids=list(range(len(in_maps))))
    return res.results


def kernel(x_prompt, x_sample, mem_prompt, mem_sample, pre_norm_g, post_norm_g, w_in, w_fnet, gdn_conv_w,
           gdn_a_log, gdn_dt_bias, gdn_norm_g, na_rpb, mem_norm_g, w_mem_kv, w_out):
    f = lambda a: np.ascontiguousarray(np.asarray(a, dtype=np.float32))
    xs = [f(x_prompt[i]) for i in range(4)] + [f(x_sample[i]) for i in range(2)]
    ms = [f(mem_prompt[i]) for i in range(4)] + [f(mem_sample[i]) for i in range(2)]
    shared = dict(pre_norm_g=f(pre_norm_g), post_norm_g=f(post_norm_g), w_in=f(w_in), w_fnet=f(w_fnet),
                  gdn_conv_w=f(gdn_conv_w), gdn_a_log=f(gdn_a_log).reshape(DEPTH, 8),
                  gdn_dt_bias=f(gdn_dt_bias).reshape(DEPTH, 8), gdn_norm_g=f(gdn_norm_g), na_rpb=f(na_rpb),
                  mem_norm_g=f(mem_norm_g), w_mem_kv=f(w_mem_kv), w_out=f(w_out))
    per_core = []
    for c in range(8):
        s = c if c < 6 else c - 6
        d = dict(shared)
        d["x"] = xs[s]
        d["mem"] = ms[s]
        per_core.append(d)
    res = run_cores(per_core, SEQ, DEPTH)
    y_prompt = np.stack([res[i]["y"] for i in range(4)], axis=0).astype(np.float32)
    y_sample = np.stack([res[4 + i]["y"] for i in range(2)], axis=0).astype(np.float32)
    return (y_prompt, y_sample)


def phase_fnet(k, lay):
    nc, P, L, N1 = k.nc, k.P, k.L, k.N1
    A = k.arena
    A.reset()
    pb, PB = k.pb, k.PB
    wf = A.alloc([128, 2, 256], F32)
    wfb = A.alloc([128, 2, 256], BF16)
    bd = A.alloc([128, 2, 128], BF16)
    wmix = A.alloc([128, 2, 2, 256], BF16)
    dft2 = A.alloc([64, 2, 128], BF16)
    Rw, Rmix = Res(), Res()
    P.dma("sync", wf[:], k.w_fnet[lay].rearrange("(c p) o -> p c o", p=128), writes=[Rw])
    P.dma("sync", bd[:], k.c_bd64, writes=[Rw])
    P.dma("sync", dft2[:], k.c_dft2, writes=[Rw])
    P.op("vector", CP(wfb[:], wf[:]), [Rw], [Rw])
    for cc in range(2):
        for ri in range(2):
            bk = cc * 2 + ri
            P.op("tensor", MM(pb[bk][:, 0:256], bd[:, ri, :], wfb[:, cc, :]), [Rw], [PB[bk]])
            P.op("vector", CP(wmix[:, cc, ri, :], pb[bk][:, 0:256]), [PB[bk]], [Rmix])
    mark = A.off
    dft1 = A.alloc([N1, 2 * N1], BF16)
    tw = A.alloc([N1, 3, 64], F32)
    X = A.alloc([N1, 64 * 256], BF16)
    Bsb = A.alloc([N1, 64, 2, 256], BF16)
    Rc1, RX, RB = Res(), Res(), Res()
    P.dma("sync", dft1[:], k.c_dft1, writes=[Rc1])
    P.dma("sync", tw[:], k.c_tw, writes=[Rc1])
    P.dma("sync", X[:], k.U.rearrange("(a b) c -> a (b c)", b=64), writes=[RX])
    t1 = [A.alloc([N1, 256], F32) for _ in range(2)]
    t2 = [A.alloc([N1, 256], F32) for _ in range(2)]
    Rt1, Rt2 = [Res(), Res()], [Res(), Res()]
    RBd = Res()
    it = 0
    for n in range(32):
        if n > 0 and n % 8 == 0:
            q = n // 8 - 1
            P.dma("gpsimd", k.Bd[:, q * 16:(q + 1) * 16, :, :], Bsb[:, q * 16:(q + 1) * 16, :, :], reads=[RB], writes=[RBd])
        ba, bs = 2 * (n % 2), 2 * (n % 2) + 1
        P.op("tensor", MM(pb[ba][:N1, :], dft1[:, 0:N1], X[:, n * 512:(n + 1) * 512]), [Rc1, RX], [PB[ba]])
        P.op("tensor", MM(pb[bs][:N1, :], dft1[:, N1:2 * N1], X[:, n * 512:(n + 1) * 512]), [Rc1, RX], [PB[bs]])
        for hh in range(2):
            l2 = 2 * n + hh
            cs = slice(hh * 256, (hh + 1) * 256)
            b = it % 2
            it += 1
            P.op("scalar", ACTF(t1[b][:], pb[ba][:N1, cs], AF.Copy, scale=tw[:, 0, l2:l2 + 1]), [PB[ba], Rc1], [Rt1[b]])
            P.op("scalar", ACTF(t2[b][:], pb[ba][:N1, cs], AF.Copy, scale=tw[:, 1, l2:l2 + 1]), [PB[ba], Rc1], [Rt2[b]])
            P.op("vector", STT(Bsb[:, l2, 0, :], pb[bs][:N1, cs], tw[:, 1, l2:l2 + 1], t1[b][:], ALU.mult, ALU.add),
                 [PB[bs], Rc1, Rt1[b]], [RB])
            P.op("vector", STT(Bsb[:, l2, 1, :], pb[bs][:N1, cs], tw[:, 2, l2:l2 + 1], t2[b][:], ALU.mult, ALU.add),
                 [PB[bs], Rc1, Rt2[b]], [RB])
    P.dma("gpsimd", k.Bd[:, 48:64, :, :], Bsb[:, 48:64, :, :], reads=[RB], writes=[RBd])
    P.barrier()
    A.reset(mark)
    B2 = A.alloc([64, N1, 2, 128], BF16)
    YTs = A.alloc([128, 2, 2, L], BF16)
    RB2, RYT = Res(), Res()
    ev = 0
    for cc in range(2):
        for ri in range(2):
            P.dma("sync", B2[:, :, ri, :], k.Bd[:, :, ri, cc * 128:(cc + 1) * 128].rearrange("k l c -> l k c"),
                  reads=[RBd], writes=[RB2])
        for k1 in range(N1):
            bk = (k1 // 4) % 4
            slot = k1 % 4
            oap = pb[bk][:, slot * 128:(slot + 1) * 128]
            P.op("tensor", MM(oap, B2[:, k1, 0, :], dft2[:, 0, :], True, False), [RB2, Rw], [PB[bk]])
            P.op("tensor", MM(oap, B2[:, k1, 1, :], dft2[:, 1, :], False, True), [RB2, Rw], [PB[bk]])
            if slot == 3:
                for ri in range(2):
                    src = pb[bk][:, :].rearrange("p (s r q) -> p s r q", s=4, r=2)[:, :, ri, :]
                    dst = YTs[:, cc, ri, :].rearrange("p (q a) -> p a q", a=N1)[:, k1 - 3:k1 + 1, :]
                    if ev % 2 == 0:
                        P.op("vector", CP(dst, src), [PB[bk]], [RYT])
                    else:
                        P.op("scalar", ACTF(dst, src, AF.Copy), [PB[bk]], [RYT])
                    ev += 1
    gz = [A.alloc([128, 512], BF16) for _ in range(2)]
    yo = [A.alloc([128, 512], BF16) for _ in range(2)]
    Rgz, Ryo = [Res(), Res()], [Res(), Res()]
    it = 0
    for gi in range(L // 512):
        sl = slice(gi * 512, (gi + 1) * 512)
        for oc in range(2):
            b = it % 2
            bk = 4 + it % 4
            it += 1
            P.dma("sync", gz[b][:], k.FT[oc, :, sl], writes=[Rgz[b]])
            n = 0
            for cc in range(2):
                for ri in range(2):
                    P.op("tensor", MM(pb[bk][:, :], wmix[:, cc, ri, oc * 128:(oc + 1) * 128], YTs[:, cc, ri, sl], n == 0, n == 3),
                         [Rmix, RYT], [PB[bk]])
                    n += 1
            P.op("vector", TT(yo[b][:], pb[bk][:, :], gz[b][:], ALU.mult), [PB[bk], Rgz[b]], [Ryo[b]])
            P.dma("gpsimd", k.YT[oc, :, sl], yo[b][:], reads=[Ryo[b]])


def phase_na(k, lay):
    nc, P, L = k.nc, k.P, k.L
    rows = L // 64
    A = k.arena
    A.reset()
    pb, PB = k.pb, k.PB
    negt = A.alloc([64, 512], F32)
    Rneg = Res()
    P.op("vector", MSET(negt[:], NEG), writes=[Rneg])
    Rfill = {}
    Rdiag = []
    for h in range(4):
        for dl in range(8):
            Rfill[(h, dl)] = Res()
            P.dma("sync" if (h * 8 + dl) % 2 == 0 else "scalar", k.NAB[h, dl, :, :], negt[:], reads=[Rneg], writes=[Rfill[(h, dl)]])
    nabt = k.NAB.tensor
    rpbt = k.rpb.tensor
    n = 0
    for h in range(4):
        for dl in range(8):
            dbase = (h * 8 + dl) * 64 * 512
            sbase = ((lay * 4 + h) * 15 + (7 - dl)) * 31
            q = "sync" if n % 2 == 0 else "scalar"
            n += 1
            rf = [Rfill[(h, dl)]]
            r_ = Res()
            Rdiag.append(r_)
            P.dma(q, bass.AP(nabt, dbase + 8 * 512, [[513, 49], [64, 8], [1, 16]]),
                  bass.AP(rpbt, sbase + 7, [[0, 49], [31, 8], [1, 16]]), reads=rf, writes=[r_])
            r_ = Res()
            Rdiag.append(r_)
            P.dma(q, bass.AP(nabt, dbase, [[64, 8], [512, 8], [1, 16]]),
                  bass.AP(rpbt, sbase + 15, [[31, 8], [-1, 8], [1, 16]]), reads=rf, writes=[r_])
            r_ = Res()
            Rdiag.append(r_)
            P.dma(q, bass.AP(nabt, dbase + 57 * 512 + 48, [[64, 8], [512, 7], [1, 16]]),
                  bass.AP(rpbt, sbase + 6, [[31, 8], [-1, 7], [1, 16]]), reads=rf, writes=[r_])
    P.barrier()
    A.reset()
    tb = A.alloc([64, 8, 512], F32)
    qT = A.alloc([64, L], BF16)
    kT = A.alloc([64, L], BF16)
    gzh = A.alloc([64, L], BF16)
    vh = A.alloc([64, rows, 64], BF16)
    yrow = A.alloc([64, L], BF16)
    Rtb, Rq, Rk, Rgz, Rv, Ry = Res(), Res(), Res(), Res(), Res(), Res()
    NB = 4
    s1 = [A.alloc([64, 512], F32) for _ in range(NB)]
    pp = [A.alloc([64, 512], F32) for _ in range(NB)]
    pn = [A.alloc([64, 512], BF16) for _ in range(NB)]
    den = [A.alloc([64, 2], F32) for _ in range(NB)]
    pTs = [A.alloc([64, 8, 64], BF16) for _ in range(NB)]
    Rs1, Rpp, Rpn, Rden, RpT = ([Res() for _ in range(NB)] for _ in range(5))
    for h in range(4):
        p0 = (h % 2) * 64
        P.dma("sync", tb[:], k.NAB[h].rearrange("d w x -> w d x"), writes=[Rtb])
        P.dma("sync", qT[:], k.FT[14 + h // 2, p0:p0 + 64, :], writes=[Rq])
        P.dma("sync", kT[:], k.FT[16 + h // 2, p0:p0 + 64, :], writes=[Rk])
        P.dma("sync", gzh[:], k.FT[4 + h // 2, p0:p0 + 64, :], writes=[Rgz])
        P.dma("sync", vh[:], k.CV[:, h * 64:(h + 1) * 64].rearrange("(r w) d -> w r d", w=64), writes=[Rv])
        P.op("vector", TS(qT[:], qT[:], 0.125, None, ALU.mult), [Rq], [Rq])
        rsof = lambda r: min(max(r - 4, 0), rows - 8)

        def stA1(r):
            rs = rsof(r)
            b, bS = r % NB, r % 2
            P.op("tensor", MM(pb[bS][0:64, :], qT[:, r * 64:(r + 1) * 64], kT[:, rs * 64:rs * 64 + 512]), [Rq, Rk], [PB[bS]])
            P.op("vector", TT(s1[b][:], pb[bS][0:64, :], tb[:, r - rs, :], ALU.add), [PB[bS], Rtb], [Rs1[b]])

        def stA2(r):
            b = r % NB
            P.op("scalar", ACTF(pp[b][:], s1[b][:], AF.Exp, accum_out=den[b][:, 0:1]), [Rs1[b]], [Rpp[b], Rden[b]])
            P.op("vector", RECIP(den[b][:, 1:2], den[b][:, 0:1]), [Rden[b]], [Rden[b]])
            P.op("vector", TS(pn[b][:], pp[b][:], den[b][:, 1:2], None, ALU.mult), [Rpp[b], Rden[b]], [Rpn[b]])

        def stB(r):
            b = r % NB
            bT = 2 + r % 2
            pTp = pb[bT][:, 0:256].bitcast(BF16)
            for i in range(8):
                P.op("tensor", TR(pTp[0:64, i * 64:(i + 1) * 64], pn[b][:, i * 64:(i + 1) * 64], k.ident[0:64, 0:64]),
                     [Rpn[b], k.Rconst], [PB[bT]])
            P.op("scalar", ACTF(pTs[b][:], pTp[0:64, :].rearrange("p (i q) -> p i q", i=8), AF.Copy), [PB[bT]], [RpT[b]])

        def stC(r):
            rs = rsof(r)
            b = r % NB
            bo = 4 + (r // 8) % 2
            slot = r % 8
            for i in range(8):
                P.op("tensor", MM(pb[bo][0:64, slot * 64:(slot + 1) * 64], vh[:, rs + i, :], pTs[b][:, i, :], i == 0, i == 7),
                     [Rv, RpT[b]], [PB[bo]])
            if slot == 7:
                sl = slice((r - 7) * 64, (r + 1) * 64)
                P.op("vector", TT(yrow[:, sl], pb[bo][0:64, :], gzh[:, sl], ALU.mult), [PB[bo], Rgz], [Ry])

        for t in range(rows + 3):
            if t < rows:
                stA1(t)
            if 0 <= t - 1 < rows:
                stA2(t - 1)
            if 0 <= t - 2 < rows:
                stB(t - 2)
            if 0 <= t - 3 < rows:
                stC(t - 3)
        P.dma("gpsimd", k.YT[4 + h // 2, p0:p0 + 64, :], yrow[:], reads=[Ry])

def phase_gdn(k, lay):
    nc, P, L = k.nc, k.P, k.L
    A = k.arena
    A.reset()
    pb, PB = k.pb, k.PB
    id64 = k.ident[0:64, 0:64]
    cw = A.alloc([128, 6, 5], F32)
    Dg = A.alloc([128, 6, 5, 128], BF16)
    Rcw, RDg = Res(), Res()
    for c in range(6):
        P.dma("sync", cw[:, c, :], k.conv_w[lay, :, c * 128:(c + 1) * 128].rearrange("j p -> p j"), writes=[Rcw],
              allow_slow_non_contiguous=True)
    for c in range(6):
        for j in range(5):
            P.op("vector", TS(Dg[:, c, j, :], k.identf[:], cw[:, c, j:j + 1], None, ALU.mult), [Rcw, k.Rconst], [RDg])
    xc = [A.alloc([128, 6, 516], BF16) for _ in range(2)]
    actf = [A.alloc([128, 4, 512], F32) for _ in range(3)]
    sq = [A.alloc([128, 4, 512], BF16) for _ in range(2)]
    lnt = [A.alloc([128, 4, 512], F32) for _ in range(2)]
    qn = [A.alloc([128, 6, 512], BF16) for _ in range(3)]
    tm = [A.alloc([128, 6, 4, 128], BF16) for _ in range(2)]
    Rxc = [[Res() for _ in range(6)] for _ in range(2)]
    Ract = [[Res() for _ in range(4)] for _ in range(3)]
    Rsq = [[Res() for _ in range(4)] for _ in range(2)]
    Rln = [[Res() for _ in range(4)] for _ in range(2)]
    Rqn = [[Res() for _ in range(6)] for _ in range(3)]
    Rtm = [[Res() for _ in range(6)] for _ in range(2)]
    ng = L // 512

    def gA(gi):
        b = gi % 2
        tok0 = gi * 512
        for c in range(6):
            lo = tok0 - 2 if gi > 0 else tok0
            hi = tok0 + 514 if gi < ng - 1 else tok0 + 512
            if gi == 0:
                P.op("gpsimd", MSET(xc[b][:, c, 0:2], 0.0), writes=[Rxc[b][c]])
            if gi == ng - 1:
                P.op("gpsimd", MSET(xc[b][:, c, 514:516], 0.0), writes=[Rxc[b][c]])
            P.dma("sync", xc[b][:, c, (lo - (tok0 - 2)):(hi - (tok0 - 2))], k.FT[8 + c, :, lo:hi], writes=[Rxc[b][c]])
        for c in range(6):
            bk = c % 4
            for j in range(5):
                P.op("tensor", MM(pb[bk][:, :], Dg[:, c, j, :], xc[b][:, c, j:j + 512], j == 0, j == 4), [RDg, Rxc[b][c]], [PB[bk]])
            if c < 4:
                P.op("scalar", ACTF(actf[gi % 3][:, c, :], pb[bk][:, :], AF.Silu), [PB[bk]], [Ract[gi % 3][c]])
            else:
                P.op("scalar", ACTF(qn[gi % 3][:, c, :], pb[bk][:, :], AF.Silu), [PB[bk]], [Rqn[gi % 3][c]])

    def gB(gi):
        b = gi % 2
        for c in range(4):
            P.op("gpsimd", TT(sq[b][:, c, :], actf[gi % 3][:, c, :], actf[gi % 3][:, c, :], ALU.mult), [Ract[gi % 3][c]], [Rsq[b][c]])
            bk2 = 4 + c % 2
            P.op("tensor", MM(pb[bk2][:, :], k.bones[:], sq[b][:, c, :]), [k.Rconst, Rsq[b][c]], [PB[bk2]])
            P.op("scalar", ACTF(lnt[b][:, c, :], pb[bk2][:, :], AF.Ln, bias=EPS), [PB[bk2]], [Rln[b][c]])
        for c in range(4):
            P.op("scalar", ACTF(lnt[b][:, c, :], lnt[b][:, c, :], AF.Exp, scale=-0.5), [Rln[b][c]], [Rln[b][c]])

    def gC(gi):
        b = gi % 2
        q3 = gi % 3
        tok0 = gi * 512
        for c in range(4):
            P.op("vector", STT(qn[q3][:, c, :], actf[q3][:, c, :], 0.125 if c < 2 else 1.0, lnt[b][:, c, :], ALU.mult, ALU.mult),
                 [Ract[q3][c], Rln[b][c]], [Rqn[q3][c]])
            P.dma("gpsimd", k.QKn[c, :, tok0:tok0 + 512], qn[q3][:, c, :], reads=[Rqn[q3][c]])
        for c in range(6):
            bk = 6 + c % 2
            pT = pb[bk][:, 0:256].bitcast(BF16)
            for t in range(4):
                P.op("tensor", TR(pT[:, t * 128:(t + 1) * 128], qn[q3][:, c, t * 128:(t + 1) * 128], k.ident[:]),
                     [Rqn[q3][c], k.Rconst], [PB[bk]])
            P.op("vector", CP(tm[b][:, c, :, :], pT.rearrange("p (t c) -> p t c", t=4)), [PB[bk]], [Rtm[b][c]])
            P.dma("gpsimd", k.QKVt[tok0:tok0 + 512, c * 128:(c + 1) * 128].rearrange("(t p) c -> p t c", p=128), tm[b][:, c, :, :],
                  reads=[Rtm[b][c]])

    pipeline(ng, [gA, gB, gC])
    P.barrier()
    A.reset()
    ntile = L // 128
    gm = A.alloc([128, 12, 128], F32)
    rmask = A.alloc([128, 2], F32)
    idr = A.alloc([128, 128], F32R)
    Rgm = Res()
    P.dma("sync", gm[:], k.c_gmask, writes=[Rgm])
    P.op("vector", CP(idr[:], k.identf[:]), [k.Rconst], [Rgm])
    P.op("vector", CP(rmask[:, :], gm[:, 10:12, 0]), [Rgm], [Rgm])
    MRk = [gm[:, 0, :], gm[:, 1, :]]
    Tm = [gm[:, 2, :], gm[:, 3, :]]
    INCLk = [gm[:, 4, :], gm[:, 5, :]]
    STRk = [gm[:, 6, :], gm[:, 7, :]]
    M2 = [gm[:, 8, :], gm[:, 9, :]]
    ONEC = [gm[:, 10, :], gm[:, 11, :]]
    QKg = [[A.alloc([64, 8, 512], BF16) for _ in range(2)] for _ in range(2)]
    TMg = [[A.alloc([128, 4, 768], BF16) for _ in range(2)] for _ in range(2)]
    Gg = [[A.alloc([128, 4, 16], F32) for _ in range(2)] for _ in range(2)]
    Rgrp = [[Res(), Res()], [Res(), Res()]]
    S32 = [A.alloc([64, 4, 64], F32) for _ in range(2)]
    Sbf = [A.alloc([64, 4, 64], BF16) for _ in range(2)]
    RS32, RSbf = [Res(), Res()], [Res(), Res()]
    for d in range(2):
        P.op("vector", MSET(S32[d][:], 0.0), writes=[RS32[d]])
        P.op("vector", MSET(Sbf[d][:], 0.0), writes=[RSbf[d]])

    def al2(shape, dt):
        return [A.alloc(shape, dt) for _ in range(2)]

    Grhs, E, EMi, EMs, t1 = (al2([128, 4, 128], F32) for _ in range(5))
    EG = al2([128, 16], F32)
    ekm = al2([128, 2, 4], F32)
    nb = al2([128, 4], F32)
    be = al2([128, 4], F32)
    qkb, qkT = (al2([128, 4, 128], BF16) for _ in range(2))
    wT, qdT = (al2([64, 4, 128], BF16) for _ in range(2))
    qd, vn = (al2([128, 4, 64], BF16) for _ in range(2))
    kdm = [al2([128, 4, 64], BF16) for _ in range(2)]
    Rkdm = [[Res(), Res()], [Res(), Res()]]
    Rekm = [Res(), Res()]
    for b_ in range(2):
        P.op("vector", MSET(vn[b_][:], 0.0), writes=[Rgm])
    Rm = [al2([128, 4, 128], F32R) for _ in range(2)]
    XPt = [al2([128, 4, 2, 128], F32R) for _ in range(2)]
    Xm = [[XPt[s_][b_][:, :, 0, :] for b_ in range(2)] for s_ in range(2)]
    Pm = [[XPt[s_][b_][:, :, 1, :] for b_ in range(2)] for s_ in range(2)]
    osb = al2([128, 4, 64], F32)
    R_ = lambda: [Res(), Res()]
    RGrhs, RE, REMi, REMs, Rt1, REG, Rnb, Rbe, Rqkb, RqkT, RwT, Rqd, RqdT, Rkd, Rvn, Rosb = (R_() for _ in range(16))
    RPm = [R_() for _ in range(2)]
    RRm = [R_() for _ in range(2)]
    RXm = [R_() for _ in range(2)]
    NLV = 6
    all_steps = []
    for hs in range(2 * ntile):
        d = hs % 2
        ti = hs // 2
        tl = ti if d == 0 else ntile - 1 - ti
        n = tl % 4
        gb = (ti // 4) % 2
        b = d
        q = [0, 1, 2, 3] if d == 0 else [4, 5, 6, 7]
        stg = []
        cur = []

        def add(eng, fn, reads=(), writes=()):
            cur.append((eng, fn, tuple(reads), tuple(writes), False, None))

        def adddma(eng, out, in_, reads=(), writes=()):
            cur.append((eng, (out, in_), tuple(reads), tuple(writes), True, None))

        def stage():
            if cur:
                stg.append(list(cur))
                del cur[:]

        if ti % 4 == 0:
            g0 = (tl // 4) * 512
            sl = slice(g0, g0 + 512)
            for h in range(4):
                p0 = (h % 2) * 64
                adddma("sync", QKg[d][gb][:, h, :], k.QKn[h // 2, p0:p0 + 64, sl], writes=[Rgrp[d][gb]])
                adddma("sync", QKg[d][gb][:, 4 + h, :], k.QKn[2 + h // 2, p0:p0 + 64, sl], writes=[Rgrp[d][gb]])
            adddma("sync", TMg[d][gb][:], k.QKVt[sl, :].rearrange("(n p) c -> p n c", p=128), writes=[Rgrp[d][gb]])
            adddma("sync", Gg[d][gb][:], k.G[sl, :].rearrange("(n p) c -> p n c", p=128), writes=[Rgrp[d][gb]])
        QK, TM, GG, RG = QKg[d][gb], TMg[d][gb], Gg[d][gb], Rgrp[d][gb]
        cs = slice(n * 128, n * 128 + 128)
        gcol = GG[:, n, 4 * d:4 * d + 4]
        bcol = GG[:, n, 8 + 4 * d:12 + 4 * d]
        bcw = lambda ap: ap.unsqueeze(2).to_broadcast([128, 4, 128])
        bc64 = lambda ap: ap.unsqueeze(2).to_broadcast([128, 4, 64])
        mk = lambda m: m.unsqueeze(1).to_broadcast([128, 4, 128])
        v4 = lambda ap: ap.rearrange("p (a b) -> p a b", a=4)
        fl = lambda ap: ap.rearrange("p a b -> p (a b)")
        add("gpsimd", TT(Grhs[b][:], mk(MRk[d]), bcw(gcol), ALU.mult), [Rgm, RG], [RGrhs[b]])
        for (qq, lt) in enumerate((Tm[d], M2[d], ONEC[0], ONEC[1])):
            add("tensor", MM(pb[q[1]][:, 4 * qq:4 * qq + 4], lt, gcol), [Rgm, RG], [PB[q[1]]])
        add("scalar", ACTF(EG[b][:], pb[q[1]][:, 0:16], AF.Exp), [PB[q[1]]], [REG[b]])
        for f in range(2):
            add("vector", TS(ekm[b][:, f, :], EG[b][:, 4:8], rmask[:, f:f + 1], None, ALU.mult), [REG[b], Rgm], [Rekm[b]])
        stage()
        add("tensor", MM(pb[q[0]][:, :], Tm[d], fl(Grhs[b][:])), [Rgm, RGrhs[b]], [PB[q[0]]])
        add("scalar", ACTF(fl(E[b][:]), pb[q[0]][:, :], AF.Exp), [PB[q[0]]], [RE[b]])
        for h in range(4):
            add("tensor", MM(pb[q[2]][:, h * 128:(h + 1) * 128], QK[:, 4 + h, cs], QK[:, 4 + h, cs]), [RG], [PB[q[2]]])
        for h in range(4):
            add("tensor", MM(pb[q[3]][:, h * 128:(h + 1) * 128], QK[:, h, cs], QK[:, 4 + h, cs]), [RG], [PB[q[3]]])
        add("vector", TS(nb[b][:], bcol, -1.0, None, ALU.mult), [RG], [Rnb[b]])
        add("vector", TT(be[b][:], bcol, EG[b][:, 0:4], ALU.mult), [RG, REG[b]], [Rbe[b]])
        stage()
        add("vector", TT(EMs[b][:], E[b][:], mk(STRk[d]), ALU.mult), [RE[b], Rgm], [REMs[b]])
        add("gpsimd", TT(EMi[b][:], E[b][:], mk(INCLk[d]), ALU.mult), [RE[b], Rgm], [REMi[b]])
        add("vector", TT(Xm[0][b][:, :, 0:64], v4(TM[:, n, 512:768]), bc64(bcol), ALU.mult), [RG], [RXm[0][b]])
        add("vector", TT(Xm[0][b][:, :, 64:128], v4(TM[:, n, 256:512]), bc64(be[b][:, :]), ALU.mult), [RG, Rbe[b]], [RXm[0][b]])
        stage()
        add("vector", TT(t1[b][:], v4(pb[q[2]][:, :]), EMs[b][:], ALU.mult), [PB[q[2]], REMs[b]], [Rt1[b]])
        add("vector", TT(Pm[0][b], t1[b][:], bcw(nb[b][:, :]), ALU.mult), [Rt1[b], Rnb[b]], [RPm[0][b]])
        add("vector", TT(qkb[b][:], v4(pb[q[3]][:, :]), EMi[b][:], ALU.mult), [PB[q[3]], REMi[b]], [Rqkb[b]])
        add("gpsimd", TT(qd[b][:], v4(TM[:, n, 0:256]), bc64(EG[b][:, 0:4]), ALU.mult), [RG, REG[b]], [Rqd[b]])
        for f in range(2):
            add("gpsimd", TT(kdm[f][b][:], v4(TM[:, n, 256:512]), bc64(ekm[b][:, f, :]), ALU.mult), [RG, Rekm[b]], [Rkdm[f][b]])
        stage()
        for h in range(4):
            add("tensor", MM(pb[q[0]][:, h * 128:(h + 1) * 128], Pm[0][b][:, h, :], idr[:, :]), [RPm[0][b], Rgm], [PB[q[0]]])
        add("scalar", ACTF(fl(Rm[0][b][:]), pb[q[0]][:, :], AF.Copy), [PB[q[0]]], [RRm[0][b]])
        pTb = pb[q[1]][:, 0:256].bitcast(BF16)
        for h in range(4):
            add("tensor", TR(pTb[:, h * 128:(h + 1) * 128], qkb[b][:, h, :], k.ident[:]), [Rqkb[b], k.Rconst], [PB[q[1]]])
        add("vector", CP(fl(qkT[b][:]), pTb[:, :]), [PB[q[1]]], [RqkT[b]])
        stage()
        for h in range(4):
            add("tensor", TR(pTb[0:64, h * 128:(h + 1) * 128], qd[b][:, h, :], k.ident[:]), [Rqd[b], k.Rconst], [PB[q[1]]])
        add("vector", CP(fl(qdT[b][:]), pTb[0:64, :]), [PB[q[1]]], [RqdT[b]])
        stage()
        for j in range(NLV):
            sj, sn = j % 2, (j + 1) % 2
            wide = j < NLV - 2
            if j < NLV - 1:
                for h in range(4):
                    add("tensor", MM(pb[q[1]][:, h * 128:(h + 1) * 128], Pm[sj][b][:, h, :], Rm[sj][b][:, h, :]),
                        [RRm[sj][b], RPm[sj][b]], [PB[q[1]]])
                add("scalar", ACTF(fl(Rm[sn][b][:]), pb[q[1]][:, :], AF.Copy), [PB[q[1]]], [RRm[sn][b]])
                stage()
            if wide:
                for h in range(4):
                    bk = q[2 + h // 2]
                    add("tensor", MM(pb[bk][:, (h % 2) * 256:(h % 2) * 256 + 256], Rm[sj][b][:, h, :],
                                     XPt[sj][b][:, h, :, :].rearrange("p a c -> p (a c)")),
                        [RRm[sj][b], RXm[sj][b], RPm[sj][b]], [PB[bk]])
                pv = k.pball[:, q[2] * 512:(q[2] + 2) * 512].rearrange("p (h a c) -> p h a c", h=4, a=2)
                add("vector", CP(Pm[sn][b], pv[:, :, 1, :]), [PB[q[2]], PB[q[3]]], [RPm[sn][b]])
                add("vector", TT(Xm[sn][b], pv[:, :, 0, :], Xm[sj][b], ALU.add), [PB[q[2]], PB[q[3]], RXm[sj][b]], [RXm[sn][b]])
            else:
                for h in range(4):
                    add("tensor", MM(pb[q[2]][:, h * 128:(h + 1) * 128], Rm[sj][b][:, h, :], Xm[sj][b][:, h, :]),
                        [RRm[sj][b], RXm[sj][b]], [PB[q[2]]])
                add("vector", TT(Xm[sn][b], v4(pb[q[2]][:, :]), Xm[sj][b], ALU.add), [PB[q[2]], RXm[sj][b]], [RXm[sn][b]])
            stage()
        XF = Xm[NLV % 2][b]
        RXF = RXm[NLV % 2][b]
        for h in range(4):
            add("tensor", MM(pb[q[0]][0:64, h * 128:(h + 1) * 128], XF[:, h, 64:128], idr[:, :]), [RXF, Rgm], [PB[q[0]]])
        add("scalar", ACTF(fl(wT[b][:]), pb[q[0]][0:64, :], AF.Copy), [PB[q[0]]], [RwT[b]])
        stage()
        for f in ((0, 1) if d == 0 else (1, 0)):
            rows = slice(64 * f, 64 * f + 64)
            for h in range(4):
                add("tensor", MM(pb[q[0]][:, h * 64:(h + 1) * 64], wT[b][:, h, :], Sbf[d][:, h, :]), [RwT[b], RSbf[d]], [PB[q[0]]])
            add("vector", TT(vn[b][rows, :, :], XF[rows, :, 0:64], v4(pb[q[0]][rows, 0:256]), ALU.subtract), [RXF, PB[q[0]]], [Rvn[b]])
            add("gpsimd", TT(S32[d][:], S32[d][:], EG[b][0:64, 8 + 4 * f:12 + 4 * f].unsqueeze(2).to_broadcast([64, 4, 64]), ALU.mult),
                [RS32[d], REG[b]], [RS32[d]])
            stage()
            for h in range(4):
                add("tensor", MM(pb[q[0]][0:64, h * 64:(h + 1) * 64], kdm[f][b][:, h, :], vn[b][:, h, :]), [Rkdm[f][b], Rvn[b]], [PB[q[0]]])
            for h in range(4):
                o = pb[q[1]][:, h * 64:(h + 1) * 64]
                add("tensor", MM(o, qdT[b][:, h, :], Sbf[d][:, h, :], True, False), [RqdT[b], RSbf[d]], [PB[q[1]]])
                add("tensor", MM(o, qkT[b][:, h, :], vn[b][:, h, :], False, True), [RqkT[b], Rvn[b]], [PB[q[1]]])
            add("vector", TT(S32[d][:], S32[d][:], v4(pb[q[0]][0:64, 0:256]), ALU.add), [RS32[d], PB[q[0]]], [RS32[d]])
            add("scalar", ACTF(Sbf[d][:], S32[d][:], AF.Copy), [RS32[d]], [RSbf[d]])
            add("scalar", ACTF(fl(osb[b][rows, :, :]), pb[q[1]][rows, 0:256], AF.Copy), [PB[q[1]]], [Rosb[b]])
            stage()
        adddma("gpsimd", (k.OF if d == 0 else k.OB)[tl * 128:tl * 128 + 128, :], fl(osb[b][:]), reads=[Rosb[b]])
        stage()
        all_steps.append(stg)
    nst = max(len(sg) for sg in all_steps)
    KS = nst // 2 + 1
    nhs = len(all_steps)
    for t in range((nhs - 1) * KS + nst):
        for i in range(max(0, (t - nst) // KS), min(nhs - 1, t // KS) + 1):
            si = t - i * KS
            if 0 <= si < len(all_steps[i]):
                for (eng, fn, reads, writes, isdma, _) in all_steps[i][si]:
                    if isdma:
                        P.dma(eng, fn[0], fn[1], reads=reads, writes=writes)
                    else:
                        P.op(eng, fn, reads, writes)
    P.barrier()
    A.reset()
    gng = A.alloc([128, 64], F32)
    Rgng = Res()
    P.dma("sync", gng[:], k.gdn_g[lay].partition_broadcast(128), writes=[Rgng])
    NB = 4
    aln = lambda shape, dt: [A.alloc(shape, dt) for _ in range(NB)]
    Rn = lambda: [Res() for _ in range(NB)]
    of, ob, osum, sqq, y1 = (aln([128, 256], F32) for _ in range(5))
    ss = aln([128, 8], F32)
    ytm = aln([128, 256], BF16)
    gz = al2([128, 2, 512], BF16)
    yo = al2([128, 2, 512], BF16)
    Rof, Rob, Ros, Rsqq, Ry1, Rss, Rytm = (Rn() for _ in range(7))
    Rgz, Ryo = R_(), R_()
    v4 = lambda ap: ap.rearrange("p (a b) -> p a b", a=4)

    def c0(i):
        gi, t = i // 4, i % 4
        g2 = gi % 2
        b = i % NB
        if t == 0:
            P.dma("sync", gz[g2][:], k.FT[2:4, :, gi * 512:(gi + 1) * 512].rearrange("c p l -> p c l"), writes=[Rgz[g2]])
        P.dma("sync", of[b][:], k.OF[i * 128:(i + 1) * 128, :], writes=[Rof[b]])
        P.dma("sync", ob[b][:], k.OB[i * 128:(i + 1) * 128, :], writes=[Rob[b]])
        P.op("vector", TT(osum[b][:], of[b][:], ob[b][:], ALU.add), [Rof[b], Rob[b]], [Ros[b]])
        P.op("gpsimd", TT(sqq[b][:], osum[b][:], osum[b][:], ALU.mult), [Ros[b]], [Rsqq[b]])

    def c1(i):
        b = i % NB
        P.op("vector", lambda e, o_=ss[b][:, 0:4], i_=v4(sqq[b][:]): e.reduce_sum(o_, i_, AX.X), [Rsqq[b]], [Rss[b]])
        P.op("scalar", ACTF(ss[b][:, 4:8], ss[b][:, 0:4], AF.Ln, bias=EPS, scale=1.0 / 64), [Rss[b]], [Rss[b]])
        P.op("scalar", ACTF(ss[b][:, 4:8], ss[b][:, 4:8], AF.Exp, scale=-0.5), [Rss[b]], [Rss[b]])

    def c2(i):
        b = i % NB
        P.op("vector", TT(v4(y1[b][:]), v4(osum[b][:]), ss[b][:, 4:8].unsqueeze(2).to_broadcast([128, 4, 64]), ALU.mult),
             [Ros[b], Rss[b]], [Ry1[b]])
        P.op("gpsimd", TT(v4(ytm[b][:]), v4(y1[b][:]), gng[:, :].unsqueeze(1).to_broadcast([128, 4, 64]), ALU.mult),
             [Ry1[b], Rgng], [Rytm[b]])

    def c3(i):
        gi, t = i // 4, i % 4
        g2 = gi % 2
        b = i % NB
        bk = 4 + i % 2
        pT = pb[bk][:, 0:128].bitcast(BF16)
        for j in range(2):
            P.op("tensor", TR(pT[:, j * 128:(j + 1) * 128], ytm[b][:, j * 128:(j + 1) * 128], k.ident[:]), [Rytm[b], k.Rconst], [PB[bk]])
        P.op("vector", TT(yo[g2][:, :, t * 128:(t + 1) * 128], pT.rearrange("p (c t) -> p c t", c=2),
                          gz[g2][:, :, t * 128:(t + 1) * 128], ALU.mult), [PB[bk], Rgz[g2]], [Ryo[g2]])
        if t == 3:
            for j in range(2):
                P.dma("gpsimd", k.YT[2 + j, :, gi * 512:(gi + 1) * 512], yo[g2][:, j, :], reads=[Ryo[g2]])

    pipeline(L // 128, [c0, c1, c2, c3])
```

```python
import math
import numpy as np
import ml_dtypes
from contextlib import ExitStack
import concourse.bass as bass
import concourse.mybir as mybir
from concourse.bass_utils import run_bass_kernel_spmd

F32 = mybir.dt.float32
BF16 = mybir.dt.bfloat16
F32R = mybir.dt.float32r
AF = mybir.ActivationFunctionType
ALU = mybir.AluOpType
AX = mybir.AxisListType

D_MODEL = 1024
DIN = 3088
DEPTH = 2
SEQ = 8192
NMEM = 256
EPS = 1e-6
NEG = -30000.0

ENGS = ("tensor", "vector", "scalar", "gpsimd", "sync")


class Res:
    __slots__ = ("name", "last_w", "readers")

    def __init__(self, name=""):
        self.name = name
        self.last_w = None
        self.readers = []


class Op:
    __slots__ = ("eng", "fn", "deps", "is_dma", "sig", "dma_slot", "needs")

    def __init__(self, eng, fn, is_dma):
        self.eng = eng
        self.fn = fn
        self.deps = []
        self.is_dma = is_dma
        self.sig = None
        self.needs = False
        self.dma_slot = None


class Prog:
    NDMA = 12

    def __init__(self, nc):
        self.nc = nc
        self.ops = {e: [] for e in ENGS}
        self.pending = None
        self.pending_done = set()

    def op(self, eng, fn, reads=(), writes=(), dma=False):
        o = Op(eng, fn, dma)
        deps = []
        for r in reads:
            if r.last_w is not None:
                deps.append(r.last_w)
        for w in writes:
            if w.last_w is not None:
                deps.append(w.last_w)
            deps.extend(w.readers)
        if self.pending is not None and eng not in self.pending_done:
            deps.extend(self.pending)
            self.pending_done.add(eng)
        seen = set()
        for d in deps:
            if id(d) in seen:
                continue
            seen.add(id(d))
            if d.eng == "tensor" and eng == "tensor" and not d.is_dma and not dma:
                continue
            o.deps.append(d)
            d.needs = True
        for r in reads:
            r.readers.append(o)
        for w in writes:
            w.last_w = o
            w.readers = []
        self.ops[eng].append(o)
        return o

    def dma(self, eng, out, in_, reads=(), writes=(), **kw):
        return self.op(eng, lambda e: e.dma_start(out=out, in_=in_, **kw), reads, writes, dma=True)

    def barrier(self):
        deps = []
        for e in ENGS:
            ops = self.ops[e]
            for o in reversed(ops):
                if not o.is_dma:
                    deps.append(o)
                    o.needs = True
                    break
            deps.extend([o for o in ops if o.is_dma][-self.NDMA:])
        self.pending = deps
        self.pending_done = set()

    def emit(self):
        nc = self.nc
        with ExitStack() as st:
            sems = {e: st.enter_context(nc.semaphore("s_" + e)) for e in ENGS}
            dsems = {e: [st.enter_context(nc.semaphore("d_%s_%d" % (e, i))) for i in range(self.NDMA)]
                     for e in ("sync", "gpsimd", "scalar")}
            for e in ENGS:
                cnt = 0
                dcnt = 0
                for o in self.ops[e]:
                    if o.is_dma:
                        o.dma_slot = dcnt
                        o.sig = (dsems[e][dcnt % self.NDMA], 16 * (dcnt // self.NDMA + 1))
                        dcnt += 1
                    elif o.needs:
                        cnt += 1
                        o.sig = (sems[e], cnt)
            block = st.enter_context(nc.Block())
            prog = self

            def run_engine(e, eng):
                waited = {}
                dma_list = [o for o in prog.ops[e] if o.is_dma]

                def wait(sem, val):
                    k = id(sem)
                    if waited.get(k, 0) >= val:
                        return
                    waited[k] = val
                    eng.wait_ge(sem, val)

                for o in prog.ops[e]:
                    for d in o.deps:
                        wait(*d.sig)
                    if o.is_dma and o.dma_slot >= prog.NDMA:
                        wait(*dma_list[o.dma_slot - prog.NDMA].sig)
                    ins = o.fn(eng)
                    if o.is_dma:
                        ins.then_inc(o.sig[0], 16)
                    elif o.needs:
                        ins.then_inc(o.sig[0], 1)
                for o in dma_list[-prog.NDMA:]:
                    wait(*o.sig)

            @block.tensor
            def _(eng):
                run_engine("tensor", eng)

            @block.vector
            def _(eng):
                run_engine("vector", eng)

            @block.scalar
            def _(eng):
                run_engine("scalar", eng)

            @block.gpsimd
            def _(eng):
                run_engine("gpsimd", eng)

            @block.sync
            def _(eng):
                run_engine("sync", eng)


def MM(out, lhsT, rhs, start=True, stop=True):
    return lambda e: e.matmul(out, lhsT, rhs, start=start, stop=stop)


def TR(out, in_, ident):
    return lambda e: e.transpose(out, in_, ident)


def ACTF(out, in_, func, **kw):
    return lambda e: e.activation(out, in_, func, **kw)


def TS(out, in0, s1, s2, op0, op1=None):
    if op1 is None:
        return lambda e: e.tensor_scalar(out, in0, s1, s2, op0)
    return lambda e: e.tensor_scalar(out, in0, s1, s2, op0, op1)


def TT(out, in0, in1, op):
    return lambda e: e.tensor_tensor(out, in0, in1, op)


def STT(out, in0, scalar, in1, op0, op1):
    return lambda e: e.scalar_tensor_tensor(out, in0, scalar, in1, op0, op1)


def CP(out, in_):
    return lambda e: e.tensor_copy(out, in_)


def MSET(ap, val):
    return lambda e: e.memset(ap, val)


def RECIP(out, in_):
    return lambda e: e.reciprocal(out, in_)


_uid = [0]


def _dsize(dt):
    return 4 if dt in (F32, F32R) else 2


class Arena:
    def __init__(self, nc, base, limit):
        self.nc = nc
        self.off = base
        self.base = base
        self.limit = limit

    def alloc(self, shape, dt):
        n = _dsize(dt)
        for s in shape[1:]:
            n *= s
        n = (n + 63) // 64 * 64
        _uid[0] += 1
        h = self.nc.alloc_sbuf_tensor_at("t%d" % _uid[0], list(shape), dt, offset=self.off)
        self.off += n
        assert self.off <= self.limit, ("SBUF arena overflow", self.off, self.limit)
        return h

    def reset(self, to=None):
        self.off = self.base if to is None else to


FM_CHUNKS = ([(256 + 128 * j, True, 0 + j) for j in range(2)] + [(512 + 128 * j, False, 8 + j) for j in range(6)] +
             [(1280 + 128 * j, True, 2 + j) for j in range(2)] + [(1552 + 128 * j, False, 14 + j) for j in range(4)] +
             [(2320 + 128 * j, True, 4 + j) for j in range(2)] + [(2576 + 128 * j, False, 18 + j) for j in range(2)] +
             [(2832 + 128 * j, True, 6 + j) for j in range(2)])


def pipeline(n, stages):
    for t in range(n + len(stages) - 1):
        for si, fn in enumerate(stages):
            i = t - si
            if 0 <= i < n:
                fn(i)


class K:
    pass


def build_program(L=SEQ, depth=DEPTH, debug=False, stop_after=None):
    nc = bass.Bass("TRN2", target_bir_lowering=False)
    P = Prog(nc)
    k = K()
    k.nc, k.P, k.L, k.depth = nc, P, L, depth
    N1 = L // 64
    k.N1 = N1

    def din(name, shape, dt=F32):
        return nc.dram_tensor(name, list(shape), dt, kind="ExternalInput").ap()

    skind = "ExternalOutput" if debug else "Internal"

    def dscr(name, shape, dt):
        return nc.dram_tensor(name, list(shape), dt, kind=skind).ap()

    k.x = din("x", [L, D_MODEL])
    k.mem = din("mem", [NMEM, D_MODEL])
    k.pre_g = din("pre_norm_g", [depth, D_MODEL])
    k.post_g = din("post_norm_g", [depth, D_MODEL])
    k.w_in = din("w_in", [depth, D_MODEL, DIN])
    k.w_fnet = din("w_fnet", [depth, 256, 256])
    k.conv_w = din("gdn_conv_w", [depth, 5, 768])
    k.a_log = din("gdn_a_log", [depth, 8])
    k.dt_bias = din("gdn_dt_bias", [depth, 8])
    k.gdn_g = din("gdn_norm_g", [depth, 64])
    k.rpb = din("na_rpb", [depth, 4, 15, 31])
    k.mem_g = din("mem_norm_g", [depth, D_MODEL])
    k.w_kv = din("w_mem_kv", [depth, D_MODEL, 512])
    k.w_out = din("w_out", [depth, D_MODEL, D_MODEL])
    k.c_ident = din("c_ident", [128, 128], BF16)
    k.c_identf = din("c_identf", [128, 128], F32)
    k.c_dft1 = din("c_dft1", [N1, 2 * N1], BF16)
    k.c_tw = din("c_tw", [N1, 3, 64], F32)
    k.c_dft2 = din("c_dft2", [64, 2, 128], BF16)
    k.c_bd64 = din("c_bd64", [128, 2, 128], BF16)
    k.c_gmask = din("c_gmask", [128, 12, 128], F32)
    k.c_bones = din("c_bones", [128, 128], BF16)
    k.y = nc.dram_tensor("y", [L, D_MODEL], F32, kind="ExternalOutput").ap()
    k.X1 = dscr("s_x1", [L, D_MODEL], F32)
    k.FT = dscr("s_ft", [20, 128, L], BF16)
    k.U = dscr("s_u", [L, 256], BF16)
    k.CV = dscr("s_cv", [L, 256], BF16)
    k.G = dscr("s_g", [L, 16], F32)
    k.YT = dscr("s_yt", [8, 128, L], BF16)
    k.Bd = dscr("s_bd", [N1, 64, 2, 256], BF16)
    k.QKn = dscr("s_qkn", [4, 128, L], BF16)
    k.QKVt = dscr("s_qkvt", [L, 768], BF16)
    k.OF = dscr("s_of", [L, 256], F32)
    k.OB = dscr("s_ob", [L, 256], F32)
    k.NAB = dscr("s_nab", [4, 8, 64, 512], F32)

    k.pball = nc.alloc_psum_tensor("pball", [128, 4096], F32)
    k.pb = [k.pball[:, i * 512:(i + 1) * 512] for i in range(8)]
    k.PB = [Res("pb%d" % i) for i in range(8)]

    SB_BASE = 16640
    SB_LIMIT = SB_BASE + 196608
    pa = Arena(nc, SB_BASE, SB_LIMIT)
    k.ident = pa.alloc([128, 128], BF16)
    k.identf = pa.alloc([128, 128], F32)
    k.bones = pa.alloc([128, 128], BF16)
    k.ones_bf = pa.alloc([128, 64], BF16)
    k.idr = pa.alloc([64, 64], F32R)
    k.Rconst = Res("const")
    P.dma("sync", k.ident[:], k.c_ident, writes=[k.Rconst])
    P.dma("sync", k.identf[:], k.c_identf, writes=[k.Rconst])
    P.dma("sync", k.bones[:], k.c_bones, writes=[k.Rconst])
    P.op("vector", MSET(k.ones_bf[:], 1.0), writes=[k.Rconst])
    P.op("vector", CP(k.idr[:], k.identf[0:64, 0:64]), [k.Rconst], [k.Rconst])
    k.arena = Arena(nc, pa.off, SB_LIMIT)

    for lay in range(depth):
        xin = k.x if lay == 0 else k.X1
        yout = k.y if lay == depth - 1 else k.X1
        phase1(k, lay, xin)
        P.barrier()
        if stop_after == "p1":
            break
        phase_mem(k, lay)
        P.barrier()
        if stop_after == "mem":
            break
        phase_fnet(k, lay)
        P.barrier()
        if stop_after == "fnet":
            break
        phase_na(k, lay)
        P.barrier()
        if stop_after == "na":
            break
        phase_gdn(k, lay)
        P.barrier()
        if stop_after == "gdn":
            break
        phase_out(k, lay, xin, yout)
        P.barrier()
    P.emit()
    return nc


def rms_tile(k, xb, Rxb, junk, Rjunk, st, Rst, xs, Rxs, width=D_MODEL):
    P = k.P
    P.op("scalar", ACTF(junk[:], xb[:], AF.Square, accum_out=st[:, 0:1]), [Rxb], [Rjunk, Rst])
    P.op("scalar", ACTF(st[:, 1:2], st[:, 0:1], AF.Ln, bias=EPS, scale=1.0 / width), [Rst], [Rst])
    P.op("scalar", ACTF(st[:, 2:3], st[:, 1:2], AF.Exp, scale=-0.5), [Rst], [Rst])
    P.op("vector", TS(xs[:], xb[:], st[:, 2:3], None, ALU.mult), [Rxb, Rst], [Rxs])


def phase1(k, lay, xin):
    nc, P, L = k.nc, k.P, k.L
    A = k.arena
    A.reset()
    pb, PB = k.pb, k.PB
    wst = [A.alloc([128, DIN], F32) for _ in range(4)]
    Rwst = [Res() for _ in range(4)]
    wbf = A.alloc([128, 8, DIN], BF16)
    Rwbf = Res()
    gpre = A.alloc([128, 8], F32)
    Rg = Res()
    P.dma("sync", gpre[:], k.pre_g[lay].rearrange("(k p) -> p k", p=128), writes=[Rg], allow_slow_non_contiguous=True)
    def wload(kk):
        P.dma("sync" if kk % 2 == 0 else "scalar", wst[kk % 4][:], k.w_in[lay, kk * 128:(kk + 1) * 128, :], writes=[Rwst[kk % 4]])

    for kk in range(4):
        wload(kk)
    for kk in range(8):
        if kk % 2 == 0:
            P.op("vector", TS(wbf[:, kk, :], wst[kk % 4][:], gpre[:, kk:kk + 1], None, ALU.mult), [Rwst[kk % 4], Rg], [Rwbf])
        else:
            P.op("scalar", ACTF(wbf[:, kk, :], wst[kk % 4][:], AF.Copy, scale=gpre[:, kk:kk + 1]), [Rwst[kk % 4], Rg], [Rwbf])
        if kk + 4 < 8:
            wload(kk + 4)
    dtb = A.alloc([128, 8], F32)
    negA = A.alloc([128, 8], F32)
    Rgc = Res()
    P.dma("sync", dtb[:], k.dt_bias[lay].partition_broadcast(128), writes=[Rgc])
    P.dma("sync", negA[:], k.a_log[lay].partition_broadcast(128), writes=[Rgc])
    P.op("scalar", ACTF(negA[:], negA[:], AF.Exp), [Rgc], [Rgc])
    P.op("vector", TS(negA[:], negA[:], -1.0, None, ALU.mult), [Rgc], [Rgc])

    NXB = 8
    xt = [A.alloc([128, D_MODEL], F32) for _ in range(NXB)]
    Rxt = [Res() for _ in range(NXB)]
    junk = A.alloc([128, D_MODEL], F32)
    Rjunk = Res()
    stt = [A.alloc([128, 4], F32) for _ in range(NXB)]
    Rstt = [Res() for _ in range(NXB)]
    xs = [A.alloc([128, D_MODEL], BF16) for _ in range(NXB)]
    Rxs = [Res() for _ in range(NXB)]
    hxT = [A.alloc([128, 8, 512], BF16) for _ in range(3)]
    RhxT = [Res() for _ in range(3)]
    fo = [A.alloc([128, 512], BF16) for _ in range(4)]
    Rfo = [Res() for _ in range(4)]
    tmo = [A.alloc([128, 512], BF16) for _ in range(2)]
    Rtmo = [Res(), Res()]
    gw = A.alloc([128, 4, 16], F32)
    gout = A.alloc([128, 4, 16], F32)
    Rgw, Rgout = Res(), Res()
    pT = pb[7][:, :].bitcast(BF16)
    ngroups = L // 512

    def prepA(gi):
        for t in range(4):
            tok0 = gi * 512 + t * 128
            b = (gi * 4 + t) % NXB
            P.dma("sync", xt[b][:], xin[tok0:tok0 + 128, :], writes=[Rxt[b]])
            rms_tile(k, xt[b], Rxt[b], junk, Rjunk, stt[b], Rstt[b], xs[b], Rxs[b])

    def prepB(gi):
        hb = gi % 3
        for t in range(4):
            b = (gi * 4 + t) % NXB
            for kk in range(8):
                P.op("tensor", TR(pT[:, kk * 128:(kk + 1) * 128], xs[b][:, kk * 128:(kk + 1) * 128], k.ident[:]),
                     [Rxs[b], k.Rconst], [PB[7]])
            P.op("vector", CP(hxT[hb][:, :, t * 128:(t + 1) * 128], pT.rearrange("p (k t) -> p k t", k=8)),
                 [PB[7]], [RhxT[hb]])

    def mmG(gi):
        hb = gi % 3
        for ci, (col0, is_silu, dst) in enumerate(FM_CHUNKS):
            bk = ci % 4
            for kk in range(8):
                P.op("tensor", MM(pb[bk][:, :], wbf[:, kk, col0:col0 + 128], hxT[hb][:, kk, :], kk == 0, kk == 7),
                     [Rwbf, RhxT[hb]], [PB[bk]])
            if is_silu:
                P.op("scalar", ACTF(fo[bk][:], pb[bk][:, :], AF.Silu), [PB[bk]], [Rfo[bk]])
            else:
                P.op("vector", CP(fo[bk][:], pb[bk][:, :]), [PB[bk]], [Rfo[bk]])
            P.dma("gpsimd", k.FT[dst, :, gi * 512:(gi + 1) * 512], fo[bk][:], reads=[Rfo[bk]])
        for t in range(4):
            tok0 = gi * 512 + t * 128
            bk = 4 + t % 2
            for (oap, c0, c1, bres) in ((pb[bk][:, 0:256], 0, 256, PB[bk]), (pb[bk][:, 256:512], 2064, 2320, PB[bk]),
                                        (pb[6][:, t * 16:(t + 1) * 16], 1536, 1552, PB[6])):
                for kk in range(8):
                    lt = hxT[hb][:, kk, t * 128:(t + 1) * 128]
                    P.op("tensor", MM(oap, lt, wbf[:, kk, c0:c1], kk == 0, kk == 7), [Rwbf, RhxT[hb]], [bres])
            ob = tmo[t % 2]
            P.op("vector", CP(ob[:], pb[bk][:, :]), [PB[bk]], [Rtmo[t % 2]])
            P.dma("gpsimd", k.U[tok0:tok0 + 128, :], ob[:, 0:256], reads=[Rtmo[t % 2]])
            P.dma("gpsimd", k.CV[tok0:tok0 + 128, :], ob[:, 256:512], reads=[Rtmo[t % 2]])
        pg = pb[6][:, 0:64].rearrange("p (t c) -> p t c", t=4)
        P.op("vector", TT(gw[:, :, 0:8], pg[:, :, 0:8], dtb[:, :].unsqueeze(1).to_broadcast([128, 4, 8]), ALU.add),
             [PB[6], Rgc], [Rgw])
        P.op("scalar", ACTF(gw[:, :, 0:8], gw[:, :, 0:8], AF.Exp), [Rgw], [Rgw])
        P.op("scalar", ACTF(gw[:, :, 0:8], gw[:, :, 0:8], AF.Ln, bias=1.0), [Rgw], [Rgw])
        P.op("scalar", ACTF(gw[:, :, 8:16], pg[:, :, 8:16], AF.Exp, scale=-1.0), [PB[6], Rgw], [Rgw])
        P.op("vector", TT(gout[:, :, 0:8], gw[:, :, 0:8], negA[:, :].unsqueeze(1).to_broadcast([128, 4, 8]), ALU.mult),
             [Rgw, Rgc], [Rgout])
        P.op("vector", TS(gw[:, :, 8:16], gw[:, :, 8:16], 1.0, None, ALU.add), [Rgw], [Rgw])
        P.op("vector", RECIP(gout[:, :, 8:16], gw[:, :, 8:16]), [Rgw], [Rgout])
        P.dma("gpsimd", k.G[gi * 512:(gi + 1) * 512, :].rearrange("(t p) c -> p t c", p=128), gout[:], reads=[Rgout])

    pipeline(ngroups, [prepA, prepB, mmG])


def phase_out(k, lay, xin, yout):
    nc, P, L = k.nc, k.P, k.L
    A = k.arena
    A.reset()
    pb, PB = k.pb, k.PB
    wst = [A.alloc([128, D_MODEL], F32) for _ in range(2)]
    Rwst = [Res(), Res()]
    wob = A.alloc([128, 8, D_MODEL], BF16)
    Rwob = Res()
    for kk in range(8):
        P.dma("sync", wst[kk % 2][:], k.w_out[lay, kk * 128:(kk + 1) * 128, :], writes=[Rwst[kk % 2]])
        if kk % 2 == 0:
            P.op("vector", CP(wob[:, kk, :], wst[kk % 2][:]), [Rwst[kk % 2]], [Rwob])
        else:
            P.op("scalar", ACTF(wob[:, kk, :], wst[kk % 2][:], AF.Copy), [Rwst[kk % 2]], [Rwob])
    gpost = A.alloc([128, D_MODEL], F32)
    Rgp = Res()
    P.dma("sync", gpost[:], k.post_g[lay].partition_broadcast(128), writes=[Rgp])
    NB = 4
    ycat = [A.alloc([128, 8, 512], BF16) for _ in range(2)]
    Ryc = [Res(), Res()]
    osb = [A.alloc([128, D_MODEL], F32) for _ in range(NB)]
    Ros = [Res() for _ in range(NB)]
    xr = [A.alloc([128, D_MODEL], F32) for _ in range(NB)]
    Rxr = [Res() for _ in range(NB)]
    junk = A.alloc([128, 512], BF16)
    Rjunk = Res()
    stt = [A.alloc([128, 8], F32) for _ in range(NB)]
    Rst = [Res() for _ in range(NB)]

    def s0(i):
        gi, t = i // 4, i % 4
        yb = gi % 2
        if i == 0:
            P.dma("sync", ycat[0][:], k.YT[:, :, 0:512].rearrange("c p l -> p c l"), writes=[Ryc[0]])
        if t == 0 and (gi + 1) * 512 < L:
            P.dma("sync", ycat[(gi + 1) % 2][:], k.YT[:, :, (gi + 1) * 512:(gi + 2) * 512].rearrange("c p l -> p c l"),
                  writes=[Ryc[(gi + 1) % 2]])
        b = i % NB
        P.dma("sync", xr[b][:], xin[i * 128:(i + 1) * 128, :], writes=[Rxr[b]])
        for half in range(2):
            bk = half + 2 * (i % 2)
            for kk in range(8):
                P.op("tensor", MM(pb[bk][:, :], ycat[yb][:, kk, t * 128:(t + 1) * 128],
                                  wob[:, kk, half * 512:(half + 1) * 512], kk == 0, kk == 7), [Ryc[yb], Rwob], [PB[bk]])
            P.op("scalar", ACTF(junk[:], pb[bk][:, :], AF.Square, accum_out=stt[b][:, half:half + 1]), [PB[bk]], [Rjunk, Rst[b]])

    def s1(i):
        b = i % NB
        st = stt[b]
        P.op("vector", TT(st[:, 2:3], st[:, 0:1], st[:, 1:2], ALU.add), [Rst[b]], [Rst[b]])
        P.op("scalar", ACTF(st[:, 3:4], st[:, 2:3], AF.Ln, bias=EPS, scale=1.0 / D_MODEL), [Rst[b]], [Rst[b]])
        P.op("scalar", ACTF(st[:, 4:5], st[:, 3:4], AF.Exp, scale=-0.5), [Rst[b]], [Rst[b]])
        for half in range(2):
            bk = half + 2 * (i % 2)
            P.op("scalar", ACTF(osb[b][:, half * 512:(half + 1) * 512], pb[bk][:, :], AF.Copy, scale=st[:, 4:5]),
                 [PB[bk], Rst[b]], [Ros[b]])

    def s2(i):
        b = i % NB
        P.op("vector", TT(osb[b][:], osb[b][:], gpost[:], ALU.mult), [Ros[b], Rgp], [Ros[b]])
        P.op("gpsimd", TT(osb[b][:], osb[b][:], xr[b][:], ALU.add), [Ros[b], Rxr[b]], [Ros[b]])
        P.dma("sync", yout[i * 128:(i + 1) * 128, :], osb[b][:], reads=[Ros[b]])

    pipeline(L // 128, [s0, s1, s2])


def phase_mem(k, lay):
    nc, P, L = k.nc, k.P, k.L
    A = k.arena
    A.reset()
    pb, PB = k.pb, k.PB
    wst = [A.alloc([128, 512], F32) for _ in range(2)]
    Rwst = [Res(), Res()]
    wkv = A.alloc([128, 8, 512], BF16)
    Rwkv = Res()
    gm = A.alloc([128, 8], F32)
    Rgm = Res()
    P.dma("sync", gm[:], k.mem_g[lay].rearrange("(k p) -> p k", p=128), writes=[Rgm], allow_slow_non_contiguous=True)
    for kk in range(8):
        P.dma("sync", wst[kk % 2][:], k.w_kv[lay, kk * 128:(kk + 1) * 128, :], writes=[Rwst[kk % 2]])
        P.op("vector", TS(wkv[:, kk, :], wst[kk % 2][:], gm[:, kk:kk + 1], None, ALU.mult), [Rwst[kk % 2], Rgm], [Rwkv])
    xt = A.alloc([128, D_MODEL], F32)
    junk = A.alloc([128, D_MODEL], F32)
    st = A.alloc([128, 4], F32)
    xs = A.alloc([128, D_MODEL], BF16)
    Rxt, Rjunk, Rst, Rxs = Res(), Res(), Res(), Res()
    memT = A.alloc([128, 8, 256], BF16)
    RmemT = Res()
    pT = pb[7][:, :].bitcast(BF16)
    for t in range(2):
        P.dma("sync", xt[:], k.mem[t * 128:(t + 1) * 128, :], writes=[Rxt])
        rms_tile(k, xt, Rxt, junk, Rjunk, st, Rst, xs, Rxs)
        for kk in range(8):
            P.op("tensor", TR(pT[:, kk * 128:(kk + 1) * 128], xs[:, kk * 128:(kk + 1) * 128], k.ident[:]), [Rxs, k.Rconst], [PB[7]])
        P.op("vector", CP(memT[:, :, t * 128:(t + 1) * 128], pT.rearrange("p (k t) -> p k t", k=8)), [PB[7]], [RmemT])
    kmT = A.alloc([64, 4, 256], BF16)
    vm = A.alloc([128, 2, 256], BF16)
    Rkm, Rvm = Res(), Res()
    for h in range(4):
        bk = h % 2
        for kk in range(8):
            P.op("tensor", MM(pb[bk][0:64, 0:256], wkv[:, kk, h * 64:(h + 1) * 64], memT[:, kk, :], kk == 0, kk == 7),
                 [Rwkv, RmemT], [PB[bk]])
        P.op("vector", CP(kmT[:, h, :], pb[bk][0:64, 0:256]), [PB[bk]], [Rkm])
    for mc in range(2):
        bk = 2 + mc
        for kk in range(8):
            P.op("tensor", MM(pb[bk][:, 0:256], memT[:, kk, mc * 128:(mc + 1) * 128], wkv[:, kk, 256:512], kk == 0, kk == 7),
                 [Rwkv, RmemT], [PB[bk]])
        P.op("vector", CP(vm[:, mc, :], pb[bk][:, 0:256]), [PB[bk]], [Rvm])
    NB = 4
    mk = lambda shape, dt: [A.alloc(shape, dt) for _ in range(NB)]
    rs = lambda: [Res() for _ in range(NB)]
    qT, gz = mk([64, 512], BF16), mk([64, 512], BF16)
    PT = mk([128, 2, 512], BF16)
    rden, y1 = mk([64, 512], F32), mk([64, 512], F32)
    yo = mk([64, 512], BF16)
    Rq, Rgz, RPT, Rrd, Ry1, Ryo = rs(), rs(), rs(), rs(), rs(), rs()

    def geo(i):
        gi, h = i // 4, i % 4
        return slice(gi * 512, (gi + 1) * 512), h, (h % 2) * 64, i % NB

    def m0(i):
        sl, h, p0, b = geo(i)
        P.dma("sync", qT[b][:], k.FT[18 + h // 2, p0:p0 + 64, sl], writes=[Rq[b]])
        P.dma("sync", gz[b][:], k.FT[6 + h // 2, p0:p0 + 64, sl], writes=[Rgz[b]])
        for mc in range(2):
            bk = 4 * (i % 2) + mc
            P.op("tensor", MM(pb[bk][:, :], kmT[:, h, mc * 128:(mc + 1) * 128], qT[b][:]), [Rkm, Rq[b]], [PB[bk]])
            P.op("scalar", ACTF(PT[b][:, mc, :], pb[bk][:, :], AF.Exp, scale=0.125), [PB[bk]], [RPT[b]])

    def m1(i):
        sl, h, p0, b = geo(i)
        bo, bd = 4 * (i % 2) + 2, 4 * (i % 2) + 3
        for mc in range(2):
            P.op("tensor", MM(pb[bo][0:64, :], vm[:, mc, h * 64:(h + 1) * 64], PT[b][:, mc, :], mc == 0, mc == 1),
                 [Rvm, RPT[b]], [PB[bo]])
        for mc in range(2):
            P.op("tensor", MM(pb[bd][0:64, :], k.ones_bf[:, :], PT[b][:, mc, :], mc == 0, mc == 1),
                 [k.Rconst, RPT[b]], [PB[bd]])
        P.op("scalar", ACTF(rden[b][:], pb[bd][0:64, :], AF.Ln), [PB[bd]], [Rrd[b]])
        P.op("scalar", ACTF(rden[b][:], rden[b][:], AF.Exp, scale=-1.0), [Rrd[b]], [Rrd[b]])
        P.op("vector", TT(y1[b][:], pb[bo][0:64, :], rden[b][:], ALU.mult), [PB[bo], Rrd[b]], [Ry1[b]])

    def m2(i):
        sl, h, p0, b = geo(i)
        P.op("gpsimd", TT(yo[b][:], y1[b][:], gz[b][:], ALU.mult), [Ry1[b], Rgz[b]], [Ryo[b]])
        P.dma("gpsimd", k.YT[6 + h // 2, p0:p0 + 64, sl], yo[b][:], reads=[Ryo[b]])

    pipeline((L // 512) * 4, [m0, m1, m2])

def make_consts(L):
    N1 = L // 64
    bf = ml_dtypes.bfloat16
    c = {}
    c["c_ident"] = np.eye(128, dtype=np.float32).astype(bf)
    c["c_identf"] = np.eye(128, dtype=np.float32)
    l1 = np.arange(N1)
    ang1 = 2 * np.pi * np.outer(l1, l1) / N1
    c["c_dft1"] = np.concatenate([np.cos(ang1), np.sin(ang1)], axis=1).astype(np.float32).astype(bf)
    angt = 2 * np.pi * np.outer(np.arange(N1), np.arange(64)) / L
    sc = 1.0 / math.sqrt(L * 64.0)
    c["c_tw"] = np.stack([np.cos(angt) * sc, -np.sin(angt) * sc, -np.cos(angt) * sc], axis=1).astype(np.float32)
    a2 = 2 * np.pi * np.outer(np.arange(64), np.arange(64)) / 64
    C2, S2 = np.cos(a2), np.sin(a2)
    c["c_dft2"] = np.stack([np.concatenate([C2, -S2], 1), np.concatenate([S2, C2], 1)], axis=1).astype(np.float32).astype(bf)
    bdc = np.zeros((128, 128)); bds = np.zeros((128, 128))
    for b in range(2):
        bdc[b * 64:(b + 1) * 64, b * 64:(b + 1) * 64] = C2
        bds[b * 64:(b + 1) * 64, b * 64:(b + 1) * 64] = S2
    c["c_bd64"] = np.stack([bdc, bds], axis=1).astype(np.float32).astype(bf)
    j = np.arange(128)[:, None]
    s = np.arange(128)[None, :]
    same = (j // 64) == (s // 64)
    gm = np.zeros((128, 12, 128), np.float32)
    gm[:, 0] = (j > s) & same
    gm[:, 1] = (j < s) & same
    gm[:, 2] = (j <= s) & same
    gm[:, 3] = (j >= s) & same
    gm[:, 4] = (j >= s) & same
    gm[:, 5] = (j <= s) & same
    gm[:, 6] = (j > s) & same
    gm[:, 7] = (j < s) & same
    gm[:, 8] = (j > s) & same
    gm[:, 9] = (j < s) & same
    gm[:, 10] = (j < 64) & (s >= 0)
    gm[:, 11] = (j >= 64) & (s >= 0)
    c["c_gmask"] = gm
    bo = np.zeros((128, 128), np.float32)
    bo[:64, :64] = 1
    bo[64:, 64:] = 1
    c["c_bones"] = bo.astype(bf)
    return c


_prog_cache = {}


def run_cores(per_core_inputs, L, depth, debug=False, stop_after=None):
    key = (L, depth, debug, stop_after)
    nc = build_program(L, depth, debug, stop_after)
    consts = make_consts(L)
    in_maps = []
    for d in per_core_inputs:
        m = dict(consts)
        m.update(d)
        in_maps.append(m)
    res = run_bass_kernel_spmd(nc, in_maps, core_ids=list(range(len(in_maps))))
    return res.results


def kernel(x_prompt, x_sample, mem_prompt, mem_sample, pre_norm_g, post_norm_g, w_in, w_fnet, gdn_conv_w,
           gdn_a_log, gdn_dt_bias, gdn_norm_g, na_rpb, mem_norm_g, w_mem_kv, w_out):
    f = lambda a: np.ascontiguousarray(np.asarray(a, dtype=np.float32))
    xs = [f(x_prompt[i]) for i in range(4)] + [f(x_sample[i]) for i in range(2)]
    ms = [f(mem_prompt[i]) for i in range(4)] + [f(mem_sample[i]) for i in range(2)]
    shared = dict(pre_norm_g=f(pre_norm_g), post_norm_g=f(post_norm_g), w_in=f(w_in), w_fnet=f(w_fnet),
                  gdn_conv_w=f(gdn_conv_w), gdn_a_log=f(gdn_a_log).reshape(DEPTH, 8),
                  gdn_dt_bias=f(gdn_dt_bias).reshape(DEPTH, 8), gdn_norm_g=f(gdn_norm_g), na_rpb=f(na_rpb),
                  mem_norm_g=f(mem_norm_g), w_mem_kv=f(w_mem_kv), w_out=f(w_out))
    per_core = []
    for c in range(8):
        s = c if c < 6 else c - 6
        d = dict(shared)
        d["x"] = xs[s]
        d["mem"] = ms[s]
        per_core.append(d)
    res = run_cores(per_core, SEQ, DEPTH)
    y_prompt = np.stack([res[i]["y"] for i in range(4)], axis=0).astype(np.float32)
    y_sample = np.stack([res[4 + i]["y"] for i in range(2)], axis=0).astype(np.float32)
    return (y_prompt, y_sample)


def phase_fnet(k, lay):
    nc, P, L, N1 = k.nc, k.P, k.L, k.N1
    A = k.arena
    A.reset()
    pb, PB = k.pb, k.PB
    wf = A.alloc([128, 2, 256], F32)
    wfb = A.alloc([128, 2, 256], BF16)
    bd = A.alloc([128, 2, 128], BF16)
    wmix = A.alloc([128, 2, 2, 256], BF16)
    dft2 = A.alloc([64, 2, 128], BF16)
    Rw, Rmix = Res(), Res()
    P.dma("sync", wf[:], k.w_fnet[lay].rearrange("(c p) o -> p c o", p=128), writes=[Rw])
    P.dma("sync", bd[:], k.c_bd64, writes=[Rw])
    P.dma("sync", dft2[:], k.c_dft2, writes=[Rw])
    P.op("vector", CP(wfb[:], wf[:]), [Rw], [Rw])
    for cc in range(2):
        for ri in range(2):
            bk = cc * 2 + ri
            P.op("tensor", MM(pb[bk][:, 0:256], bd[:, ri, :], wfb[:, cc, :]), [Rw], [PB[bk]])
            P.op("vector", CP(wmix[:, cc, ri, :], pb[bk][:, 0:256]), [PB[bk]], [Rmix])
    mark = A.off
    dft1 = A.alloc([N1, 2 * N1], BF16)
    tw = A.alloc([N1, 3, 64], F32)
    X = A.alloc([N1, 64 * 256], BF16)
    Bsb = A.alloc([N1, 64, 2, 256], BF16)
    Rc1, RX, RB = Res(), Res(), Res()
    P.dma("sync", dft1[:], k.c_dft1, writes=[Rc1])
    P.dma("sync", tw[:], k.c_tw, writes=[Rc1])
    P.dma("sync", X[:], k.U.rearrange("(a b) c -> a (b c)", b=64), writes=[RX])
    t1 = [A.alloc([N1, 256], F32) for _ in range(2)]
    t2 = [A.alloc([N1, 256], F32) for _ in range(2)]
    Rt1, Rt2 = [Res(), Res()], [Res(), Res()]
    RBd = Res()
    it = 0
    for n in range(32):
        if n > 0 and n % 8 == 0:
            q = n // 8 - 1
            P.dma("gpsimd", k.Bd[:, q * 16:(q + 1) * 16, :, :], Bsb[:, q * 16:(q + 1) * 16, :, :], reads=[RB], writes=[RBd])
        ba, bs = 2 * (n % 2), 2 * (n % 2) + 1
        P.op("tensor", MM(pb[ba][:N1, :], dft1[:, 0:N1], X[:, n * 512:(n + 1) * 512]), [Rc1, RX], [PB[ba]])
        P.op("tensor", MM(pb[bs][:N1, :], dft1[:, N1:2 * N1], X[:, n * 512:(n + 1) * 512]), [Rc1, RX], [PB[bs]])
        for hh in range(2):
            l2 = 2 * n + hh
            cs = slice(hh * 256, (hh + 1) * 256)
            b = it % 2
            it += 1
            P.op("scalar", ACTF(t1[b][:], pb[ba][:N1, cs], AF.Copy, scale=tw[:, 0, l2:l2 + 1]), [PB[ba], Rc1], [Rt1[b]])
            P.op("scalar", ACTF(t2[b][:], pb[ba][:N1, cs], AF.Copy, scale=tw[:, 1, l2:l2 + 1]), [PB[ba], Rc1], [Rt2[b]])
            P.op("vector", STT(Bsb[:, l2, 0, :], pb[bs][:N1, cs], tw[:, 1, l2:l2 + 1], t1[b][:], ALU.mult, ALU.add),
                 [PB[bs], Rc1, Rt1[b]], [RB])
            P.op("vector", STT(Bsb[:, l2, 1, :], pb[bs][:N1, cs], tw[:, 2, l2:l2 + 1], t2[b][:], ALU.mult, ALU.add),
                 [PB[bs], Rc1, Rt2[b]], [RB])
    P.dma("gpsimd", k.Bd[:, 48:64, :, :], Bsb[:, 48:64, :, :], reads=[RB], writes=[RBd])
    P.barrier()
    A.reset(mark)
    B2 = A.alloc([64, N1, 2, 128], BF16)
    YTs = A.alloc([128, 2, 2, L], BF16)
    RB2, RYT = Res(), Res()
    ev = 0
    for cc in range(2):
        for ri in range(2):
            P.dma("sync", B2[:, :, ri, :], k.Bd[:, :, ri, cc * 128:(cc + 1) * 128].rearrange("k l c -> l k c"),
                  reads=[RBd], writes=[RB2])
        for k1 in range(N1):
            bk = (k1 // 4) % 4
            slot = k1 % 4
            oap = pb[bk][:, slot * 128:(slot + 1) * 128]
            P.op("tensor", MM(oap, B2[:, k1, 0, :], dft2[:, 0, :], True, False), [RB2, Rw], [PB[bk]])
            P.op("tensor", MM(oap, B2[:, k1, 1, :], dft2[:, 1, :], False, True), [RB2, Rw], [PB[bk]])
            if slot == 3:
                for ri in range(2):
                    src = pb[bk][:, :].rearrange("p (s r q) -> p s r q", s=4, r=2)[:, :, ri, :]
                    dst = YTs[:, cc, ri, :].rearrange("p (q a) -> p a q", a=N1)[:, k1 - 3:k1 + 1, :]
                    if ev % 2 == 0:
                        P.op("vector", CP(dst, src), [PB[bk]], [RYT])
                    else:
                        P.op("scalar", ACTF(dst, src, AF.Copy), [PB[bk]], [RYT])
                    ev += 1
    gz = [A.alloc([128, 512], BF16) for _ in range(2)]
    yo = [A.alloc([128, 512], BF16) for _ in range(2)]
    Rgz, Ryo = [Res(), Res()], [Res(), Res()]
    it = 0
    for gi in range(L // 512):
        sl = slice(gi * 512, (gi + 1) * 512)
        for oc in range(2):
            b = it % 2
            bk = 4 + it % 4
            it += 1
            P.dma("sync", gz[b][:], k.FT[oc, :, sl], writes=[Rgz[b]])
            n = 0
            for cc in range(2):
                for ri in range(2):
                    P.op("tensor", MM(pb[bk][:, :], wmix[:, cc, ri, oc * 128:(oc + 1) * 128], YTs[:, cc, ri, sl], n == 0, n == 3),
                         [Rmix, RYT], [PB[bk]])
                    n += 1
            P.op("vector", TT(yo[b][:], pb[bk][:, :], gz[b][:], ALU.mult), [PB[bk], Rgz[b]], [Ryo[b]])
            P.dma("gpsimd", k.YT[oc, :, sl], yo[b][:], reads=[Ryo[b]])


def phase_na(k, lay):
    nc, P, L = k.nc, k.P, k.L
    rows = L // 64
    A = k.arena
    A.reset()
    pb, PB = k.pb, k.PB
    negt = A.alloc([64, 512], F32)
    Rneg = Res()
    P.op("vector", MSET(negt[:], NEG), writes=[Rneg])
    Rfill = {}
    Rdiag = []
    for h in range(4):
        for dl in range(8):
            Rfill[(h, dl)] = Res()
            P.dma("sync" if (h * 8 + dl) % 2 == 0 else "scalar", k.NAB[h, dl, :, :], negt[:], reads=[Rneg], writes=[Rfill[(h, dl)]])
    nabt = k.NAB.tensor
    rpbt = k.rpb.tensor
    n = 0
    for h in range(4):
        for dl in range(8):
            dbase = (h * 8 + dl) * 64 * 512
            sbase = ((lay * 4 + h) * 15 + (7 - dl)) * 31
            q = "sync" if n % 2 == 0 else "scalar"
            n += 1
            rf = [Rfill[(h, dl)]]
            r_ = Res()
            Rdiag.append(r_)
            P.dma(q, bass.AP(nabt, dbase + 8 * 512, [[513, 49], [64, 8], [1, 16]]),
                  bass.AP(rpbt, sbase + 7, [[0, 49], [31, 8], [1, 16]]), reads=rf, writes=[r_])
            r_ = Res()
            Rdiag.append(r_)
            P.dma(q, bass.AP(nabt, dbase, [[64, 8], [512, 8], [1, 16]]),
                  bass.AP(rpbt, sbase + 15, [[31, 8], [-1, 8], [1, 16]]), reads=rf, writes=[r_])
            r_ = Res()
            Rdiag.append(r_)
            P.dma(q, bass.AP(nabt, dbase + 57 * 512 + 48, [[64, 8], [512, 7], [1, 16]]),
                  bass.AP(rpbt, sbase + 6, [[31, 8], [-1, 7], [1, 16]]), reads=rf, writes=[r_])
    P.barrier()
    A.reset()
    tb2 = [A.alloc([64, 8, 512], F32) for _ in range(2)]
    qT2 = [A.alloc([64, L], BF16) for _ in range(2)]
    kT2 = [A.alloc([64, L], BF16) for _ in range(2)]
    vh2 = [A.alloc([64, rows, 64], BF16) for _ in range(2)]
    gzh = A.alloc([64, L], BF16)
    yrow = A.alloc([64, L], BF16)
    Rtb2, Rq2, Rk2, Rv2 = [Res(), Res()], [Res(), Res()], [Res(), Res()], [Res(), Res()]
    Rgz, Ry = Res(), Res()

    def head_loads(hh):
        hb_ = hh % 2
        p0_ = (hh % 2) * 64
        P.dma("sync", tb2[hb_][:], k.NAB[hh].rearrange("d w x -> w d x"), writes=[Rtb2[hb_]])
        P.dma("sync", qT2[hb_][:], k.FT[14 + hh // 2, p0_:p0_ + 64, :], writes=[Rq2[hb_]])
        P.dma("sync", kT2[hb_][:], k.FT[16 + hh // 2, p0_:p0_ + 64, :], writes=[Rk2[hb_]])
        P.dma("sync", vh2[hb_][:], k.CV[:, hh * 64:(hh + 1) * 64].rearrange("(r w) d -> w r d", w=64), writes=[Rv2[hb_]])

    head_loads(0)
    NB = 4
    s1 = [A.alloc([64, 512], F32) for _ in range(NB)]
    pp = [A.alloc([64, 512], F32) for _ in range(NB)]
    pn = [A.alloc([64, 512], BF16) for _ in range(NB)]
    den = [A.alloc([64, 2], F32) for _ in range(NB)]
    pTs = [A.alloc([64, 8, 64], BF16) for _ in range(NB)]
    Rs1, Rpp, Rpn, Rden, RpT = ([Res() for _ in range(NB)] for _ in range(5))
    for h in range(4):
        p0 = (h % 2) * 64
        hb = h % 2
        tb, qT, kT, vh = tb2[hb], qT2[hb], kT2[hb], vh2[hb]
        Rtb, Rq, Rk, Rv = Rtb2[hb], Rq2[hb], Rk2[hb], Rv2[hb]
        P.dma("sync", gzh[:], k.FT[4 + h // 2, p0:p0 + 64, :], writes=[Rgz])
        if h + 1 < 4:
            head_loads(h + 1)
        rsof = lambda r: min(max(r - 4, 0), rows - 8)

        def stA1(r):
            rs = rsof(r)
            b, bS = r % NB, r % 2
            P.op("tensor", MM(pb[bS][0:64, :], qT[:, r * 64:(r + 1) * 64], kT[:, rs * 64:rs * 64 + 512]), [Rq, Rk], [PB[bS]])
            P.op("vector", STT(s1[b][:], pb[bS][0:64, :], 0.125, tb[:, r - rs, :], ALU.mult, ALU.add), [PB[bS], Rtb], [Rs1[b]])

        def stA2(r):
            b = r % NB
            P.op("scalar", ACTF(pp[b][:], s1[b][:], AF.Exp, accum_out=den[b][:, 0:1]), [Rs1[b]], [Rpp[b], Rden[b]])
            P.op("vector", RECIP(den[b][:, 1:2], den[b][:, 0:1]), [Rden[b]], [Rden[b]])
            P.op("vector", TS(pn[b][:], pp[b][:], den[b][:, 1:2], None, ALU.mult), [Rpp[b], Rden[b]], [Rpn[b]])

        def stB(r):
            b = r % NB
            bT = 2 + r % 2
            pTp = pb[bT][:, 0:256].bitcast(BF16)
            for i in range(8):
                P.op("tensor", TR(pTp[0:64, i * 64:(i + 1) * 64], pn[b][:, i * 64:(i + 1) * 64], k.ident[0:64, 0:64]),
                     [Rpn[b], k.Rconst], [PB[bT]])
            P.op("scalar", ACTF(pTs[b][:], pTp[0:64, :].rearrange("p (i q) -> p i q", i=8), AF.Copy), [PB[bT]], [RpT[b]])

        def stC(r):
            rs = rsof(r)
            b = r % NB
            bo = 4 + (r // 8) % 2
            slot = r % 8
            for i in range(8):
                P.op("tensor", MM(pb[bo][0:64, slot * 64:(slot + 1) * 64], vh[:, rs + i, :], pTs[b][:, i, :], i == 0, i == 7),
                     [Rv, RpT[b]], [PB[bo]])
            if slot == 7:
                sl = slice((r - 7) * 64, (r + 1) * 64)
                P.op("vector", TT(yrow[:, sl], pb[bo][0:64, :], gzh[:, sl], ALU.mult), [PB[bo], Rgz], [Ry])

        for t in range(rows + 3):
            if t < rows:
                stA1(t)
            if 0 <= t - 1 < rows:
                stA2(t - 1)
            if 0 <= t - 2 < rows:
                stB(t - 2)
            if 0 <= t - 3 < rows:
                stC(t - 3)
        P.dma("gpsimd", k.YT[4 + h // 2, p0:p0 + 64, :], yrow[:], reads=[Ry])

def phase_gdn(k, lay):
    nc, P, L = k.nc, k.P, k.L
    A = k.arena
    A.reset()
    pb, PB = k.pb, k.PB
    id64 = k.ident[0:64, 0:64]
    cw = A.alloc([128, 6, 5], F32)
    Dg = A.alloc([128, 6, 5, 128], BF16)
    Rcw, RDg = Res(), Res()
    for c in range(6):
        P.dma("sync", cw[:, c, :], k.conv_w[lay, :, c * 128:(c + 1) * 128].rearrange("j p -> p j"), writes=[Rcw],
              allow_slow_non_contiguous=True)
    for c in range(6):
        for j in range(5):
            P.op("vector", TS(Dg[:, c, j, :], k.identf[:], cw[:, c, j:j + 1], None, ALU.mult), [Rcw, k.Rconst], [RDg])
    xc = [A.alloc([128, 6, 516], BF16) for _ in range(2)]
    actf = [A.alloc([128, 4, 512], F32) for _ in range(3)]
    sq = [A.alloc([128, 4, 512], BF16) for _ in range(2)]
    lnt = [A.alloc([128, 4, 512], F32) for _ in range(2)]
    qn = [A.alloc([128, 6, 512], BF16) for _ in range(3)]
    tm = [A.alloc([128, 6, 4, 128], BF16) for _ in range(2)]
    Rxc = [[Res() for _ in range(6)] for _ in range(2)]
    Ract = [[Res() for _ in range(4)] for _ in range(3)]
    Rsq = [[Res() for _ in range(4)] for _ in range(2)]
    Rln = [[Res() for _ in range(4)] for _ in range(2)]
    Rqn = [[Res() for _ in range(6)] for _ in range(3)]
    Rtm = [[Res() for _ in range(6)] for _ in range(2)]
    ng = L // 512

    def gA(gi):
        b = gi % 2
        tok0 = gi * 512
        for c in range(6):
            lo = tok0 - 2 if gi > 0 else tok0
            hi = tok0 + 514 if gi < ng - 1 else tok0 + 512
            if gi == 0:
                P.op("gpsimd", MSET(xc[b][:, c, 0:2], 0.0), writes=[Rxc[b][c]])
            if gi == ng - 1:
                P.op("gpsimd", MSET(xc[b][:, c, 514:516], 0.0), writes=[Rxc[b][c]])
            P.dma("sync", xc[b][:, c, (lo - (tok0 - 2)):(hi - (tok0 - 2))], k.FT[8 + c, :, lo:hi], writes=[Rxc[b][c]])
        for c in range(6):
            bk = c % 4
            for j in range(5):
                P.op("tensor", MM(pb[bk][:, :], Dg[:, c, j, :], xc[b][:, c, j:j + 512], j == 0, j == 4), [RDg, Rxc[b][c]], [PB[bk]])
            if c < 4:
                P.op("scalar", ACTF(actf[gi % 3][:, c, :], pb[bk][:, :], AF.Silu), [PB[bk]], [Ract[gi % 3][c]])
            else:
                P.op("scalar", ACTF(qn[gi % 3][:, c, :], pb[bk][:, :], AF.Silu), [PB[bk]], [Rqn[gi % 3][c]])

    def gB(gi):
        b = gi % 2
        for c in range(4):
            P.op("gpsimd", TT(sq[b][:, c, :], actf[gi % 3][:, c, :], actf[gi % 3][:, c, :], ALU.mult), [Ract[gi % 3][c]], [Rsq[b][c]])
            bk2 = 4 + c % 2
            P.op("tensor", MM(pb[bk2][:, :], k.bones[:], sq[b][:, c, :]), [k.Rconst, Rsq[b][c]], [PB[bk2]])
            P.op("scalar", ACTF(lnt[b][:, c, :], pb[bk2][:, :], AF.Ln, bias=EPS), [PB[bk2]], [Rln[b][c]])
        for c in range(4):
            P.op("scalar", ACTF(lnt[b][:, c, :], lnt[b][:, c, :], AF.Exp, scale=-0.5), [Rln[b][c]], [Rln[b][c]])

    def gC(gi):
        b = gi % 2
        q3 = gi % 3
        tok0 = gi * 512
        for c in range(4):
            P.op("vector", STT(qn[q3][:, c, :], actf[q3][:, c, :], 0.125 if c < 2 else 1.0, lnt[b][:, c, :], ALU.mult, ALU.mult),
                 [Ract[q3][c], Rln[b][c]], [Rqn[q3][c]])
            P.dma("gpsimd", k.QKn[c, :, tok0:tok0 + 512], qn[q3][:, c, :], reads=[Rqn[q3][c]])
        for c in range(6):
            bk = 6 + c % 2
            pT = pb[bk][:, 0:256].bitcast(BF16)
            for t in range(4):
                P.op("tensor", TR(pT[:, t * 128:(t + 1) * 128], qn[q3][:, c, t * 128:(t + 1) * 128], k.ident[:]),
                     [Rqn[q3][c], k.Rconst], [PB[bk]])
            P.op("vector", CP(tm[b][:, c, :, :], pT.rearrange("p (t c) -> p t c", t=4)), [PB[bk]], [Rtm[b][c]])
            P.dma("gpsimd", k.QKVt[tok0:tok0 + 512, c * 128:(c + 1) * 128].rearrange("(t p) c -> p t c", p=128), tm[b][:, c, :, :],
                  reads=[Rtm[b][c]])

    pipeline(ng, [gA, gB, gC])
    P.barrier()
    A.reset()
    ntile = L // 128
    gm = A.alloc([128, 12, 128], F32)
    rmask = A.alloc([128, 2], F32)
    idr = A.alloc([128, 128], F32R)
    Rgm = Res()
    P.dma("sync", gm[:], k.c_gmask, writes=[Rgm])
    P.op("vector", CP(idr[:], k.identf[:]), [k.Rconst], [Rgm])
    P.op("vector", CP(rmask[:, :], gm[:, 10:12, 0]), [Rgm], [Rgm])
    MRk = [gm[:, 0, :], gm[:, 1, :]]
    Tm = [gm[:, 2, :], gm[:, 3, :]]
    INCLk = [gm[:, 4, :], gm[:, 5, :]]
    STRk = [gm[:, 6, :], gm[:, 7, :]]
    M2 = [gm[:, 8, :], gm[:, 9, :]]
    ONEC = [gm[:, 10, :], gm[:, 11, :]]
    QKg = [[A.alloc([64, 8, 512], BF16) for _ in range(2)] for _ in range(2)]
    TMg = [[A.alloc([128, 4, 768], BF16) for _ in range(2)] for _ in range(2)]
    Gg = [[A.alloc([128, 4, 16], F32) for _ in range(2)] for _ in range(2)]
    Rgrp = [[Res(), Res()], [Res(), Res()]]
    S32 = [A.alloc([64, 4, 64], F32) for _ in range(2)]
    Sbf = [A.alloc([64, 4, 64], BF16) for _ in range(2)]
    RS32, RSbf = [Res(), Res()], [Res(), Res()]
    for d in range(2):
        P.op("vector", MSET(S32[d][:], 0.0), writes=[RS32[d]])
        P.op("vector", MSET(Sbf[d][:], 0.0), writes=[RSbf[d]])

    def al2(shape, dt):
        return [A.alloc(shape, dt) for _ in range(2)]

    Grhs, E, EMi, EMs, t1 = (al2([128, 4, 128], F32) for _ in range(5))
    EG = al2([128, 16], F32)
    ekm = al2([128, 2, 4], F32)
    nb = al2([128, 4], F32)
    be = al2([128, 4], F32)
    qkb, qkT = (al2([128, 4, 128], BF16) for _ in range(2))
    wT, qdT = (al2([64, 4, 128], BF16) for _ in range(2))
    qd, vn = (al2([128, 4, 64], BF16) for _ in range(2))
    kdm = [al2([128, 4, 64], BF16) for _ in range(2)]
    Rkdm = [[Res(), Res()], [Res(), Res()]]
    Rekm = [Res(), Res()]
    for b_ in range(2):
        P.op("vector", MSET(vn[b_][:], 0.0), writes=[Rgm])
    Rm = [al2([128, 4, 128], F32R) for _ in range(2)]
    XPt = [al2([128, 4, 2, 128], F32R) for _ in range(2)]
    Xm = [[XPt[s_][b_][:, :, 0, :] for b_ in range(2)] for s_ in range(2)]
    Pm = [[XPt[s_][b_][:, :, 1, :] for b_ in range(2)] for s_ in range(2)]
    osb = al2([128, 4, 64], F32)
    R_ = lambda: [Res(), Res()]
    RGrhs, RE, REMi, REMs, Rt1, REG, Rnb, Rbe, Rqkb, RqkT, RwT, Rqd, RqdT, Rkd, Rvn, Rosb = (R_() for _ in range(16))
    RPm = [R_() for _ in range(2)]
    RRm = [R_() for _ in range(2)]
    RXm = [R_() for _ in range(2)]
    NLV = 6
    all_steps = []
    for hs in range(2 * ntile):
        d = hs % 2
        ti = hs // 2
        tl = ti if d == 0 else ntile - 1 - ti
        n = tl % 4
        gb = (ti // 4) % 2
        b = d
        q = [0, 1, 2, 3] if d == 0 else [4, 5, 6, 7]
        stg = []
        cur = []

        def add(eng, fn, reads=(), writes=()):
            cur.append((eng, fn, tuple(reads), tuple(writes), False, None))

        def adddma(eng, out, in_, reads=(), writes=()):
            cur.append((eng, (out, in_), tuple(reads), tuple(writes), True, None))

        def stage():
            if cur:
                stg.append(list(cur))
                del cur[:]

        if ti % 4 == 0:
            g0 = (tl // 4) * 512
            sl = slice(g0, g0 + 512)
            for h in range(4):
                p0 = (h % 2) * 64
                adddma("sync", QKg[d][gb][:, h, :], k.QKn[h // 2, p0:p0 + 64, sl], writes=[Rgrp[d][gb]])
                adddma("sync", QKg[d][gb][:, 4 + h, :], k.QKn[2 + h // 2, p0:p0 + 64, sl], writes=[Rgrp[d][gb]])
            adddma("sync", TMg[d][gb][:], k.QKVt[sl, :].rearrange("(n p) c -> p n c", p=128), writes=[Rgrp[d][gb]])
            adddma("sync", Gg[d][gb][:], k.G[sl, :].rearrange("(n p) c -> p n c", p=128), writes=[Rgrp[d][gb]])
        QK, TM, GG, RG = QKg[d][gb], TMg[d][gb], Gg[d][gb], Rgrp[d][gb]
        cs = slice(n * 128, n * 128 + 128)
        gcol = GG[:, n, 4 * d:4 * d + 4]
        bcol = GG[:, n, 8 + 4 * d:12 + 4 * d]
        bcw = lambda ap: ap.unsqueeze(2).to_broadcast([128, 4, 128])
        bc64 = lambda ap: ap.unsqueeze(2).to_broadcast([128, 4, 64])
        mk = lambda m: m.unsqueeze(1).to_broadcast([128, 4, 128])
        v4 = lambda ap: ap.rearrange("p (a b) -> p a b", a=4)
        fl = lambda ap: ap.rearrange("p a b -> p (a b)")
        add("gpsimd", TT(Grhs[b][:], mk(MRk[d]), bcw(gcol), ALU.mult), [Rgm, RG], [RGrhs[b]])
        for (qq, lt) in enumerate((Tm[d], M2[d], ONEC[0], ONEC[1])):
            add("tensor", MM(pb[q[1]][:, 4 * qq:4 * qq + 4], lt, gcol), [Rgm, RG], [PB[q[1]]])
        add("scalar", ACTF(EG[b][:], pb[q[1]][:, 0:16], AF.Exp), [PB[q[1]]], [REG[b]])
        for f in range(2):
            add("vector", TS(ekm[b][:, f, :], EG[b][:, 4:8], rmask[:, f:f + 1], None, ALU.mult), [REG[b], Rgm], [Rekm[b]])
        stage()
        add("tensor", MM(pb[q[0]][:, :], Tm[d], fl(Grhs[b][:])), [Rgm, RGrhs[b]], [PB[q[0]]])
        add("scalar", ACTF(fl(E[b][:]), pb[q[0]][:, :], AF.Exp), [PB[q[0]]], [RE[b]])
        for h in range(4):
            add("tensor", MM(pb[q[2]][:, h * 128:(h + 1) * 128], QK[:, 4 + h, cs], QK[:, 4 + h, cs]), [RG], [PB[q[2]]])
        for h in range(4):
            add("tensor", MM(pb[q[3]][:, h * 128:(h + 1) * 128], QK[:, h, cs], QK[:, 4 + h, cs]), [RG], [PB[q[3]]])
        add("vector", TS(nb[b][:], bcol, -1.0, None, ALU.mult), [RG], [Rnb[b]])
        add("vector", TT(be[b][:], bcol, EG[b][:, 0:4], ALU.mult), [RG, REG[b]], [Rbe[b]])
        stage()
        add("vector", TT(EMs[b][:], E[b][:], mk(STRk[d]), ALU.mult), [RE[b], Rgm], [REMs[b]])
        add("gpsimd", TT(EMi[b][:], E[b][:], mk(INCLk[d]), ALU.mult), [RE[b], Rgm], [REMi[b]])
        add("vector", TT(Xm[0][b][:, :, 0:64], v4(TM[:, n, 512:768]), bc64(bcol), ALU.mult), [RG], [RXm[0][b]])
        add("vector", TT(Xm[0][b][:, :, 64:128], v4(TM[:, n, 256:512]), bc64(be[b][:, :]), ALU.mult), [RG, Rbe[b]], [RXm[0][b]])
        stage()
        add("vector", TT(t1[b][:], v4(pb[q[2]][:, :]), EMs[b][:], ALU.mult), [PB[q[2]], REMs[b]], [Rt1[b]])
        add("vector", TT(Pm[0][b], t1[b][:], bcw(nb[b][:, :]), ALU.mult), [Rt1[b], Rnb[b]], [RPm[0][b]])
        add("vector", TT(qkb[b][:], v4(pb[q[3]][:, :]), EMi[b][:], ALU.mult), [PB[q[3]], REMi[b]], [Rqkb[b]])
        add("gpsimd", TT(qd[b][:], v4(TM[:, n, 0:256]), bc64(EG[b][:, 0:4]), ALU.mult), [RG, REG[b]], [Rqd[b]])
        for f in range(2):
            add("gpsimd", TT(kdm[f][b][:], v4(TM[:, n, 256:512]), bc64(ekm[b][:, f, :]), ALU.mult), [RG, Rekm[b]], [Rkdm[f][b]])
        stage()
        for h in range(4):
            add("tensor", MM(pb[q[0]][:, h * 128:(h + 1) * 128], Pm[0][b][:, h, :], idr[:, :]), [RPm[0][b], Rgm], [PB[q[0]]])
        add("scalar", ACTF(fl(Rm[0][b][:]), pb[q[0]][:, :], AF.Copy), [PB[q[0]]], [RRm[0][b]])
        pTb = pb[q[1]][:, 0:256].bitcast(BF16)
        for h in range(4):
            add("tensor", TR(pTb[:, h * 128:(h + 1) * 128], qkb[b][:, h, :], k.ident[:]), [Rqkb[b], k.Rconst], [PB[q[1]]])
        add("vector", CP(fl(qkT[b][:]), pTb[:, :]), [PB[q[1]]], [RqkT[b]])
        stage()
        for h in range(4):
            add("tensor", TR(pTb[0:64, h * 128:(h + 1) * 128], qd[b][:, h, :], k.ident[:]), [Rqd[b], k.Rconst], [PB[q[1]]])
        add("vector", CP(fl(qdT[b][:]), pTb[0:64, :]), [PB[q[1]]], [RqdT[b]])
        stage()
        for j in range(NLV):
            sj, sn = j % 2, (j + 1) % 2
            wide = j < NLV - 2
            if j < NLV - 1:
                for h in range(4):
                    add("tensor", MM(pb[q[1]][:, h * 128:(h + 1) * 128], Pm[sj][b][:, h, :], Rm[sj][b][:, h, :]),
                        [RRm[sj][b], RPm[sj][b]], [PB[q[1]]])
                add("scalar", ACTF(fl(Rm[sn][b][:]), pb[q[1]][:, :], AF.Copy), [PB[q[1]]], [RRm[sn][b]])
                stage()
            if wide:
                for h in range(4):
                    bk = q[2 + h // 2]
                    add("tensor", MM(pb[bk][:, (h % 2) * 256:(h % 2) * 256 + 256], Rm[sj][b][:, h, :],
                                     XPt[sj][b][:, h, :, :].rearrange("p a c -> p (a c)")),
                        [RRm[sj][b], RXm[sj][b], RPm[sj][b]], [PB[bk]])
                pv = k.pball[:, q[2] * 512:(q[2] + 2) * 512].rearrange("p (h a c) -> p h a c", h=4, a=2)
                add("vector", CP(Pm[sn][b], pv[:, :, 1, :]), [PB[q[2]], PB[q[3]]], [RPm[sn][b]])
                add("vector", TT(Xm[sn][b], pv[:, :, 0, :], Xm[sj][b], ALU.add), [PB[q[2]], PB[q[3]], RXm[sj][b]], [RXm[sn][b]])
            else:
                for h in range(4):
                    add("tensor", MM(pb[q[2]][:, h * 128:(h + 1) * 128], Rm[sj][b][:, h, :], Xm[sj][b][:, h, :]),
                        [RRm[sj][b], RXm[sj][b]], [PB[q[2]]])
                add("vector", TT(Xm[sn][b], v4(pb[q[2]][:, :]), Xm[sj][b], ALU.add), [PB[q[2]], RXm[sj][b]], [RXm[sn][b]])
            stage()
        XF = Xm[NLV % 2][b]
        RXF = RXm[NLV % 2][b]
        for h in range(4):
            add("tensor", MM(pb[q[0]][0:64, h * 128:(h + 1) * 128], XF[:, h, 64:128], idr[:, :]), [RXF, Rgm], [PB[q[0]]])
        add("scalar", ACTF(fl(wT[b][:]), pb[q[0]][0:64, :], AF.Copy), [PB[q[0]]], [RwT[b]])
        stage()
        for f in ((0, 1) if d == 0 else (1, 0)):
            rows = slice(64 * f, 64 * f + 64)
            for h in range(4):
                add("tensor", MM(pb[q[0]][:, h * 64:(h + 1) * 64], wT[b][:, h, :], Sbf[d][:, h, :]), [RwT[b], RSbf[d]], [PB[q[0]]])
            add("vector", TT(vn[b][rows, :, :], XF[rows, :, 0:64], v4(pb[q[0]][rows, 0:256]), ALU.subtract), [RXF, PB[q[0]]], [Rvn[b]])
            add("gpsimd", TT(S32[d][:], S32[d][:], EG[b][0:64, 8 + 4 * f:12 + 4 * f].unsqueeze(2).to_broadcast([64, 4, 64]), ALU.mult),
                [RS32[d], REG[b]], [RS32[d]])
            stage()
            for h in range(4):
                add("tensor", MM(pb[q[0]][0:64, h * 64:(h + 1) * 64], kdm[f][b][:, h, :], vn[b][:, h, :]), [Rkdm[f][b], Rvn[b]], [PB[q[0]]])
            for h in range(4):
                o = pb[q[1]][:, h * 64:(h + 1) * 64]
                add("tensor", MM(o, qdT[b][:, h, :], Sbf[d][:, h, :], True, False), [RqdT[b], RSbf[d]], [PB[q[1]]])
                add("tensor", MM(o, qkT[b][:, h, :], vn[b][:, h, :], False, True), [RqkT[b], Rvn[b]], [PB[q[1]]])
            add("vector", TT(S32[d][:], S32[d][:], v4(pb[q[0]][0:64, 0:256]), ALU.add), [RS32[d], PB[q[0]]], [RS32[d]])
            add("scalar", ACTF(Sbf[d][:], S32[d][:], AF.Copy), [RS32[d]], [RSbf[d]])
            add("scalar", ACTF(fl(osb[b][rows, :, :]), pb[q[1]][rows, 0:256], AF.Copy), [PB[q[1]]], [Rosb[b]])
            stage()
        adddma("gpsimd", (k.OF if d == 0 else k.OB)[tl * 128:tl * 128 + 128, :], fl(osb[b][:]), reads=[Rosb[b]])
        stage()
        all_steps.append(stg)
    nst = max(len(sg) for sg in all_steps)
    KS = nst // 2 + 1
    nhs = len(all_steps)
    for t in range((nhs - 1) * KS + nst):
        for i in range(max(0, (t - nst) // KS), min(nhs - 1, t // KS) + 1):
            si = t - i * KS
            if 0 <= si < len(all_steps[i]):
                for (eng, fn, reads, writes, isdma, _) in all_steps[i][si]:
                    if isdma:
                        P.dma(eng, fn[0], fn[1], reads=reads, writes=writes)
                    else:
                        P.op(eng, fn, reads, writes)
    P.barrier()
    A.reset()
    gng = A.alloc([128, 64], F32)
    Rgng = Res()
    P.dma("sync", gng[:], k.gdn_g[lay].partition_broadcast(128), writes=[Rgng])
    NB = 4
    aln = lambda shape, dt: [A.alloc(shape, dt) for _ in range(NB)]
    Rn = lambda: [Res() for _ in range(NB)]
    of, ob, osum, sqq, y1 = (aln([128, 256], F32) for _ in range(5))
    ss = aln([128, 8], F32)
    ytm = aln([128, 256], BF16)
    gz = al2([128, 2, 512], BF16)
    yo = al2([128, 2, 512], BF16)
    Rof, Rob, Ros, Rsqq, Ry1, Rss, Rytm = (Rn() for _ in range(7))
    Rgz, Ryo = R_(), R_()
    v4 = lambda ap: ap.rearrange("p (a b) -> p a b", a=4)

    def c0(i):
        gi, t = i // 4, i % 4
        g2 = gi % 2
        b = i % NB
        if t == 0:
            P.dma("sync", gz[g2][:], k.FT[2:4, :, gi * 512:(gi + 1) * 512].rearrange("c p l -> p c l"), writes=[Rgz[g2]])
        P.dma("sync", of[b][:], k.OF[i * 128:(i + 1) * 128, :], writes=[Rof[b]])
        P.dma("sync", ob[b][:], k.OB[i * 128:(i + 1) * 128, :], writes=[Rob[b]])
        P.op("vector", TT(osum[b][:], of[b][:], ob[b][:], ALU.add), [Rof[b], Rob[b]], [Ros[b]])
        P.op("gpsimd", TT(sqq[b][:], osum[b][:], osum[b][:], ALU.mult), [Ros[b]], [Rsqq[b]])

    def c1(i):
        b = i % NB
        P.op("vector", lambda e, o_=ss[b][:, 0:4], i_=v4(sqq[b][:]): e.reduce_sum(o_, i_, AX.X), [Rsqq[b]], [Rss[b]])
        P.op("scalar", ACTF(ss[b][:, 4:8], ss[b][:, 0:4], AF.Ln, bias=EPS, scale=1.0 / 64), [Rss[b]], [Rss[b]])
        P.op("scalar", ACTF(ss[b][:, 4:8], ss[b][:, 4:8], AF.Exp, scale=-0.5), [Rss[b]], [Rss[b]])

    def c2(i):
        b = i % NB
        P.op("vector", TT(v4(y1[b][:]), v4(osum[b][:]), ss[b][:, 4:8].unsqueeze(2).to_broadcast([128, 4, 64]), ALU.mult),
             [Ros[b], Rss[b]], [Ry1[b]])
        P.op("gpsimd", TT(v4(ytm[b][:]), v4(y1[b][:]), gng[:, :].unsqueeze(1).to_broadcast([128, 4, 64]), ALU.mult),
             [Ry1[b], Rgng], [Rytm[b]])

    def c3(i):
        gi, t = i // 4, i % 4
        g2 = gi % 2
        b = i % NB
        bk = 4 + i % 2
        pT = pb[bk][:, 0:128].bitcast(BF16)
        for j in range(2):
            P.op("tensor", TR(pT[:, j * 128:(j + 1) * 128], ytm[b][:, j * 128:(j + 1) * 128], k.ident[:]), [Rytm[b], k.Rconst], [PB[bk]])
        P.op("vector", TT(yo[g2][:, :, t * 128:(t + 1) * 128], pT.rearrange("p (c t) -> p c t", c=2),
                          gz[g2][:, :, t * 128:(t + 1) * 128], ALU.mult), [PB[bk], Rgz[g2]], [Ryo[g2]])
        if t == 3:
            for j in range(2):
                P.dma("gpsimd", k.YT[2 + j, :, gi * 512:(gi + 1) * 512], yo[g2][:, j, :], reads=[Ryo[g2]])

    pipeline(L // 128, [c0, c1, c2, c3])
```

```python
import math
import numpy as np
import ml_dtypes
from contextlib import ExitStack
import concourse.bass as bass
import concourse.mybir as mybir
from concourse.bass_utils import run_bass_kernel_spmd

F32 = mybir.dt.float32
BF16 = mybir.dt.bfloat16
F32R = mybir.dt.float32r
AF = mybir.ActivationFunctionType
ALU = mybir.AluOpType
AX = mybir.AxisListType

D_MODEL = 1024
DIN = 3088
DEPTH = 2
SEQ = 8192
NMEM = 256
EPS = 1e-6
NEG = -30000.0

ENGS = ("tensor", "vector", "scalar", "gpsimd", "sync")


class Res:
    __slots__ = ("name", "last_w", "readers")

    def __init__(self, name=""):
        self.name = name
        self.last_w = None
        self.readers = []


class Op:
    __slots__ = ("eng", "fn", "deps", "is_dma", "sig", "dma_slot", "needs")

    def __init__(self, eng, fn, is_dma):
        self.eng = eng
        self.fn = fn
        self.deps = []
        self.is_dma = is_dma
        self.sig = None
        self.needs = False
        self.dma_slot = None


class Prog:
    NDMA = 12

    def __init__(self, nc):
        self.nc = nc
        self.ops = {e: [] for e in ENGS}
        self.pending = None
        self.pending_done = set()

    def op(self, eng, fn, reads=(), writes=(), dma=False):
        o = Op(eng, fn, dma)
        deps = []
        for r in reads:
            if r.last_w is not None:
                deps.append(r.last_w)
        for w in writes:
            if w.last_w is not None:
                deps.append(w.last_w)
            deps.extend(w.readers)
        if self.pending is not None and eng not in self.pending_done:
            deps.extend(self.pending)
            self.pending_done.add(eng)
        seen = set()
        for d in deps:
            if id(d) in seen:
                continue
            seen.add(id(d))
            if d.eng == "tensor" and eng == "tensor" and not d.is_dma and not dma:
                continue
            o.deps.append(d)
            d.needs = True
        for r in reads:
            r.readers.append(o)
        for w in writes:
            w.last_w = o
            w.readers = []
        self.ops[eng].append(o)
        return o

    def dma(self, eng, out, in_, reads=(), writes=(), **kw):
        return self.op(eng, lambda e: e.dma_start(out=out, in_=in_, **kw), reads, writes, dma=True)

    def barrier(self):
        deps = []
        for e in ENGS:
            ops = self.ops[e]
            for o in reversed(ops):
                if not o.is_dma:
                    deps.append(o)
                    o.needs = True
                    break
            deps.extend([o for o in ops if o.is_dma][-self.NDMA:])
        self.pending = deps
        self.pending_done = set()

    def emit(self):
        nc = self.nc
        with ExitStack() as st:
            sems = {e: st.enter_context(nc.semaphore("s_" + e)) for e in ENGS}
            dsems = {e: [st.enter_context(nc.semaphore("d_%s_%d" % (e, i))) for i in range(self.NDMA)]
                     for e in ("sync", "gpsimd", "scalar")}
            for e in ENGS:
                cnt = 0
                dcnt = 0
                for o in self.ops[e]:
                    if o.is_dma:
                        o.dma_slot = dcnt
                        o.sig = (dsems[e][dcnt % self.NDMA], 16 * (dcnt // self.NDMA + 1))
                        dcnt += 1
                    elif o.needs:
                        cnt += 1
                        o.sig = (sems[e], cnt)
            block = st.enter_context(nc.Block())
            prog = self

            def run_engine(e, eng):
                waited = {}
                dma_list = [o for o in prog.ops[e] if o.is_dma]

                def wait(sem, val):
                    k = id(sem)
                    if waited.get(k, 0) >= val:
                        return
                    waited[k] = val
                    eng.wait_ge(sem, val)

                for o in prog.ops[e]:
                    for d in o.deps:
                        wait(*d.sig)
                    if o.is_dma and o.dma_slot >= prog.NDMA:
                        wait(*dma_list[o.dma_slot - prog.NDMA].sig)
                    ins = o.fn(eng)
                    if o.is_dma:
                        ins.then_inc(o.sig[0], 16)
                    elif o.needs:
                        ins.then_inc(o.sig[0], 1)
                for o in dma_list[-prog.NDMA:]:
                    wait(*o.sig)

            @block.tensor
            def _(eng):
                run_engine("tensor", eng)

            @block.vector
            def _(eng):
                run_engine("vector", eng)

            @block.scalar
            def _(eng):
                run_engine("scalar", eng)

            @block.gpsimd
            def _(eng):
                run_engine("gpsimd", eng)

            @block.sync
            def _(eng):
                run_engine("sync", eng)


def MM(out, lhsT, rhs, start=True, stop=True):
    return lambda e: e.matmul(out, lhsT, rhs, start=start, stop=stop)


def TR(out, in_, ident):
    return lambda e: e.transpose(out, in_, ident)


def ACTF(out, in_, func, **kw):
    return lambda e: e.activation(out, in_, func, **kw)


def TS(out, in0, s1, s2, op0, op1=None):
    if op1 is None:
        return lambda e: e.tensor_scalar(out, in0, s1, s2, op0)
    return lambda e: e.tensor_scalar(out, in0, s1, s2, op0, op1)


def TT(out, in0, in1, op):
    return lambda e: e.tensor_tensor(out, in0, in1, op)


def STT(out, in0, scalar, in1, op0, op1):
    return lambda e: e.scalar_tensor_tensor(out, in0, scalar, in1, op0, op1)


def CP(out, in_):
    return lambda e: e.tensor_copy(out, in_)


def MSET(ap, val):
    return lambda e: e.memset(ap, val)


def RECIP(out, in_):
    return lambda e: e.reciprocal(out, in_)


_uid = [0]


def _dsize(dt):
    return 4 if dt in (F32, F32R) else 2


class Arena:
    def __init__(self, nc, base, limit):
        self.nc = nc
        self.off = base
        self.base = base
        self.limit = limit

    def alloc(self, shape, dt):
        n = _dsize(dt)
        for s in shape[1:]:
            n *= s
        n = (n + 63) // 64 * 64
        _uid[0] += 1
        h = self.nc.alloc_sbuf_tensor_at("t%d" % _uid[0], list(shape), dt, offset=self.off)
        self.off += n
        assert self.off <= self.limit, ("SBUF arena overflow", self.off, self.limit)
        return h

    def reset(self, to=None):
        self.off = self.base if to is None else to


FM_CHUNKS = ([(256 + 128 * j, True, 0 + j) for j in range(2)] + [(512 + 128 * j, False, 8 + j) for j in range(6)] +
             [(1280 + 128 * j, True, 2 + j) for j in range(2)] + [(1552 + 128 * j, False, 14 + j) for j in range(4)] +
             [(2320 + 128 * j, True, 4 + j) for j in range(2)] + [(2576 + 128 * j, False, 18 + j) for j in range(2)] +
             [(2832 + 128 * j, True, 6 + j) for j in range(2)])


def pipeline(n, stages):
    for t in range(n + len(stages) - 1):
        for si, fn in enumerate(stages):
            i = t - si
            if 0 <= i < n:
                fn(i)


class K:
    pass


def build_program(L=SEQ, depth=DEPTH, debug=False, stop_after=None):
    nc = bass.Bass("TRN2", target_bir_lowering=False)
    P = Prog(nc)
    k = K()
    k.nc, k.P, k.L, k.depth = nc, P, L, depth
    N1 = L // 64
    k.N1 = N1

    def din(name, shape, dt=F32):
        return nc.dram_tensor(name, list(shape), dt, kind="ExternalInput").ap()

    skind = "ExternalOutput" if debug else "Internal"

    def dscr(name, shape, dt):
        return nc.dram_tensor(name, list(shape), dt, kind=skind).ap()

    k.x = din("x", [L, D_MODEL])
    k.mem = din("mem", [NMEM, D_MODEL])
    k.pre_g = din("pre_norm_g", [depth, D_MODEL])
    k.post_g = din("post_norm_g", [depth, D_MODEL])
    k.w_in = din("w_in", [depth, D_MODEL, DIN])
    k.w_fnet = din("w_fnet", [depth, 256, 256])
    k.conv_w = din("gdn_conv_w", [depth, 5, 768])
    k.a_log = din("gdn_a_log", [depth, 8])
    k.dt_bias = din("gdn_dt_bias", [depth, 8])
    k.gdn_g = din("gdn_norm_g", [depth, 64])
    k.rpb = din("na_rpb", [depth, 4, 15, 31])
    k.mem_g = din("mem_norm_g", [depth, D_MODEL])
    k.w_kv = din("w_mem_kv", [depth, D_MODEL, 512])
    k.w_out = din("w_out", [depth, D_MODEL, D_MODEL])
    k.c_ident = din("c_ident", [128, 128], BF16)
    k.c_identf = din("c_identf", [128, 128], F32)
    k.c_dft1 = din("c_dft1", [N1, 2 * N1], BF16)
    k.c_tw = din("c_tw", [N1, 3, 64], F32)
    k.c_dft2 = din("c_dft2", [64, 2, 128], BF16)
    k.c_bd64 = din("c_bd64", [128, 2, 128], BF16)
    k.c_gmask = din("c_gmask", [128, 12, 128], F32)
    k.c_bones = din("c_bones", [128, 128], BF16)
    k.y = nc.dram_tensor("y", [L, D_MODEL], F32, kind="ExternalOutput").ap()
    k.X1 = dscr("s_x1", [L, D_MODEL], F32)
    k.FT = dscr("s_ft", [20, 128, L], BF16)
    k.U = dscr("s_u", [L, 256], BF16)
    k.CV = dscr("s_cv", [L, 256], BF16)
    k.G = dscr("s_g", [L, 16], F32)
    k.YT = dscr("s_yt", [8, 128, L], BF16)
    k.Bd = dscr("s_bd", [N1, 64, 2, 256], BF16)
    k.QKn = dscr("s_qkn", [4, 128, L], BF16)
    k.QKVt = dscr("s_qkvt", [L, 768], BF16)
    k.OF = dscr("s_of", [L, 256], F32)
    k.OB = dscr("s_ob", [L, 256], F32)
    k.NAB = dscr("s_nab", [4, 8, 64, 512], F32)

    k.pball = nc.alloc_psum_tensor("pball", [128, 4096], F32)
    k.pb = [k.pball[:, i * 512:(i + 1) * 512] for i in range(8)]
    k.PB = [Res("pb%d" % i) for i in range(8)]

    SB_BASE = 16640
    SB_LIMIT = SB_BASE + 196608
    pa = Arena(nc, SB_BASE, SB_LIMIT)
    k.ident = pa.alloc([128, 128], BF16)
    k.identf = pa.alloc([128, 128], F32)
    k.bones = pa.alloc([128, 128], BF16)
    k.ones_bf = pa.alloc([128, 64], BF16)
    k.idr = pa.alloc([64, 64], F32R)
    k.negt = pa.alloc([64, 512], F32)
    k.Rconst = Res("const")
    P.dma("sync", k.ident[:], k.c_ident, writes=[k.Rconst])
    P.dma("sync", k.identf[:], k.c_identf, writes=[k.Rconst])
    P.dma("sync", k.bones[:], k.c_bones, writes=[k.Rconst])
    P.op("vector", MSET(k.ones_bf[:], 1.0), writes=[k.Rconst])
    P.op("vector", MSET(k.negt[:], NEG), writes=[k.Rconst])
    P.op("vector", CP(k.idr[:], k.identf[0:64, 0:64]), [k.Rconst], [k.Rconst])
    k.arena = Arena(nc, pa.off, SB_LIMIT)

    for lay in range(depth):
        xin = k.x if lay == 0 else k.X1
        yout = k.y if lay == depth - 1 else k.X1
        phase1(k, lay, xin)
        P.barrier()
        if stop_after == "p1":
            break
        phase_mem(k, lay)
        P.barrier()
        if stop_after == "mem":
            break
        phase_fnet(k, lay)
        P.barrier()
        if stop_after == "fnet":
            break
        phase_na(k, lay)
        P.barrier()
        if stop_after == "na":
            break
        phase_gdn(k, lay)
        P.barrier()
        if stop_after == "gdn":
            break
        phase_out(k, lay, xin, yout)
        P.barrier()
    P.emit()
    return nc


def rms_tile(k, xb, Rxb, junk, Rjunk, st, Rst, xs, Rxs, width=D_MODEL):
    P = k.P
    P.op("scalar", ACTF(junk[:], xb[:], AF.Square, accum_out=st[:, 0:1]), [Rxb], [Rjunk, Rst])
    P.op("scalar", ACTF(st[:, 1:2], st[:, 0:1], AF.Ln, bias=EPS, scale=1.0 / width), [Rst], [Rst])
    P.op("scalar", ACTF(st[:, 2:3], st[:, 1:2], AF.Exp, scale=-0.5), [Rst], [Rst])
    P.op("vector", TS(xs[:], xb[:], st[:, 2:3], None, ALU.mult), [Rxb, Rst], [Rxs])


def phase1(k, lay, xin):
    nc, P, L = k.nc, k.P, k.L
    A = k.arena
    A.reset()
    pb, PB = k.pb, k.PB
    wst = [A.alloc([128, DIN], F32) for _ in range(4)]
    Rwst = [Res() for _ in range(4)]
    wbf = A.alloc([128, 8, DIN], BF16)
    Rwbf = Res()
    gpre = A.alloc([128, 8], F32)
    Rg = Res()
    P.dma("sync", gpre[:], k.pre_g[lay].rearrange("(k p) -> p k", p=128), writes=[Rg], allow_slow_non_contiguous=True)
    def wload(kk):
        P.dma("sync" if kk % 2 == 0 else "scalar", wst[kk % 4][:], k.w_in[lay, kk * 128:(kk + 1) * 128, :], writes=[Rwst[kk % 4]])

    for kk in range(4):
        wload(kk)
    for kk in range(8):
        if kk % 2 == 0:
            P.op("vector", TS(wbf[:, kk, :], wst[kk % 4][:], gpre[:, kk:kk + 1], None, ALU.mult), [Rwst[kk % 4], Rg], [Rwbf])
        else:
            P.op("scalar", ACTF(wbf[:, kk, :], wst[kk % 4][:], AF.Copy, scale=gpre[:, kk:kk + 1]), [Rwst[kk % 4], Rg], [Rwbf])
        if kk + 4 < 8:
            wload(kk + 4)
    dtb = A.alloc([128, 8], F32)
    negA = A.alloc([128, 8], F32)
    Rgc = Res()
    P.dma("sync", dtb[:], k.dt_bias[lay].partition_broadcast(128), writes=[Rgc])
    P.dma("sync", negA[:], k.a_log[lay].partition_broadcast(128), writes=[Rgc])
    P.op("scalar", ACTF(negA[:], negA[:], AF.Exp), [Rgc], [Rgc])
    P.op("vector", TS(negA[:], negA[:], -1.0, None, ALU.mult), [Rgc], [Rgc])

    NXB = 8
    xt = [A.alloc([128, D_MODEL], F32) for _ in range(NXB)]
    Rxt = [Res() for _ in range(NXB)]
    junk = A.alloc([128, D_MODEL], F32)
    Rjunk = Res()
    stt = [A.alloc([128, 4], F32) for _ in range(NXB)]
    Rstt = [Res() for _ in range(NXB)]
    xs = [A.alloc([128, D_MODEL], BF16) for _ in range(NXB)]
    Rxs = [Res() for _ in range(NXB)]
    hxT = [A.alloc([128, 8, 512], BF16) for _ in range(3)]
    RhxT = [Res() for _ in range(3)]
    fo = [A.alloc([128, 512], BF16) for _ in range(4)]
    Rfo = [Res() for _ in range(4)]
    tmo = [A.alloc([128, 512], BF16) for _ in range(2)]
    Rtmo = [Res(), Res()]
    gw = A.alloc([128, 4, 16], F32)
    gout = A.alloc([128, 4, 16], F32)
    Rgw, Rgout = Res(), Res()
    pT = pb[7][:, :].bitcast(BF16)
    ngroups = L // 512

    def prepA(gi):
        for t in range(4):
            tok0 = gi * 512 + t * 128
            b = (gi * 4 + t) % NXB
            P.dma("sync", xt[b][:], xin[tok0:tok0 + 128, :], writes=[Rxt[b]])
            rms_tile(k, xt[b], Rxt[b], junk, Rjunk, stt[b], Rstt[b], xs[b], Rxs[b])

    def prepB(gi):
        hb = gi % 3
        for t in range(4):
            b = (gi * 4 + t) % NXB
            for kk in range(8):
                P.op("tensor", TR(pT[:, kk * 128:(kk + 1) * 128], xs[b][:, kk * 128:(kk + 1) * 128], k.ident[:]),
                     [Rxs[b], k.Rconst], [PB[7]])
            P.op("vector", CP(hxT[hb][:, :, t * 128:(t + 1) * 128], pT.rearrange("p (k t) -> p k t", k=8)),
                 [PB[7]], [RhxT[hb]])

    def mmG(gi):
        hb = gi % 3
        for ci, (col0, is_silu, dst) in enumerate(FM_CHUNKS):
            bk = ci % 4
            for kk in range(8):
                P.op("tensor", MM(pb[bk][:, :], wbf[:, kk, col0:col0 + 128], hxT[hb][:, kk, :], kk == 0, kk == 7),
                     [Rwbf, RhxT[hb]], [PB[bk]])
            if is_silu:
                P.op("scalar", ACTF(fo[bk][:], pb[bk][:, :], AF.Silu), [PB[bk]], [Rfo[bk]])
            else:
                P.op("vector", CP(fo[bk][:], pb[bk][:, :]), [PB[bk]], [Rfo[bk]])
            P.dma("gpsimd", k.FT[dst, :, gi * 512:(gi + 1) * 512], fo[bk][:], reads=[Rfo[bk]])
        for t in range(4):
            tok0 = gi * 512 + t * 128
            bk = 4 + t % 2
            for (oap, c0, c1, bres) in ((pb[bk][:, 0:256], 0, 256, PB[bk]), (pb[bk][:, 256:512], 2064, 2320, PB[bk]),
                                        (pb[6][:, t * 16:(t + 1) * 16], 1536, 1552, PB[6])):
                for kk in range(8):
                    lt = hxT[hb][:, kk, t * 128:(t + 1) * 128]
                    P.op("tensor", MM(oap, lt, wbf[:, kk, c0:c1], kk == 0, kk == 7), [Rwbf, RhxT[hb]], [bres])
            ob = tmo[t % 2]
            P.op("vector", CP(ob[:], pb[bk][:, :]), [PB[bk]], [Rtmo[t % 2]])
            P.dma("gpsimd", k.U[tok0:tok0 + 128, :], ob[:, 0:256], reads=[Rtmo[t % 2]])
            P.dma("gpsimd", k.CV[tok0:tok0 + 128, :], ob[:, 256:512], reads=[Rtmo[t % 2]])
        pg = pb[6][:, 0:64].rearrange("p (t c) -> p t c", t=4)
        P.op("vector", TT(gw[:, :, 0:8], pg[:, :, 0:8], dtb[:, :].unsqueeze(1).to_broadcast([128, 4, 8]), ALU.add),
             [PB[6], Rgc], [Rgw])
        P.op("scalar", ACTF(gw[:, :, 0:8], gw[:, :, 0:8], AF.Exp), [Rgw], [Rgw])
        P.op("scalar", ACTF(gw[:, :, 0:8], gw[:, :, 0:8], AF.Ln, bias=1.0), [Rgw], [Rgw])
        P.op("scalar", ACTF(gw[:, :, 8:16], pg[:, :, 8:16], AF.Exp, scale=-1.0), [PB[6], Rgw], [Rgw])
        P.op("vector", TT(gout[:, :, 0:8], gw[:, :, 0:8], negA[:, :].unsqueeze(1).to_broadcast([128, 4, 8]), ALU.mult),
             [Rgw, Rgc], [Rgout])
        P.op("vector", TS(gw[:, :, 8:16], gw[:, :, 8:16], 1.0, None, ALU.add), [Rgw], [Rgw])
        P.op("vector", RECIP(gout[:, :, 8:16], gw[:, :, 8:16]), [Rgw], [Rgout])
        P.dma("gpsimd", k.G[gi * 512:(gi + 1) * 512, :].rearrange("(t p) c -> p t c", p=128), gout[:], reads=[Rgout])

    pipeline(ngroups, [prepA, prepB, mmG])


def phase_out(k, lay, xin, yout):
    nc, P, L = k.nc, k.P, k.L
    A = k.arena
    A.reset()
    pb, PB = k.pb, k.PB
    wst = [A.alloc([128, D_MODEL], F32) for _ in range(2)]
    Rwst = [Res(), Res()]
    wob = A.alloc([128, 8, D_MODEL], BF16)
    Rwob = Res()
    for kk in range(8):
        P.dma("sync", wst[kk % 2][:], k.w_out[lay, kk * 128:(kk + 1) * 128, :], writes=[Rwst[kk % 2]])
        if kk % 2 == 0:
            P.op("vector", CP(wob[:, kk, :], wst[kk % 2][:]), [Rwst[kk % 2]], [Rwob])
        else:
            P.op("scalar", ACTF(wob[:, kk, :], wst[kk % 2][:], AF.Copy), [Rwst[kk % 2]], [Rwob])
    gpost = A.alloc([128, D_MODEL], F32)
    Rgp = Res()
    P.dma("sync", gpost[:], k.post_g[lay].partition_broadcast(128), writes=[Rgp])
    NB = 4
    ycat = [A.alloc([128, 8, 512], BF16) for _ in range(2)]
    Ryc = [Res(), Res()]
    osb = [A.alloc([128, D_MODEL], F32) for _ in range(NB)]
    Ros = [Res() for _ in range(NB)]
    xr = [A.alloc([128, D_MODEL], F32) for _ in range(NB)]
    Rxr = [Res() for _ in range(NB)]
    junk = A.alloc([128, 512], BF16)
    Rjunk = Res()
    stt = [A.alloc([128, 8], F32) for _ in range(NB)]
    Rst = [Res() for _ in range(NB)]

    def s0(i):
        gi, t = i // 4, i % 4
        yb = gi % 2
        if i == 0:
            P.dma("sync", ycat[0][:], k.YT[:, :, 0:512].rearrange("c p l -> p c l"), writes=[Ryc[0]])
        if t == 0 and (gi + 1) * 512 < L:
            P.dma("sync", ycat[(gi + 1) % 2][:], k.YT[:, :, (gi + 1) * 512:(gi + 2) * 512].rearrange("c p l -> p c l"),
                  writes=[Ryc[(gi + 1) % 2]])
        b = i % NB
        P.dma("sync", xr[b][:], xin[i * 128:(i + 1) * 128, :], writes=[Rxr[b]])
        for half in range(2):
            bk = half + 2 * (i % 2)
            for kk in range(8):
                P.op("tensor", MM(pb[bk][:, :], ycat[yb][:, kk, t * 128:(t + 1) * 128],
                                  wob[:, kk, half * 512:(half + 1) * 512], kk == 0, kk == 7), [Ryc[yb], Rwob], [PB[bk]])
            P.op("scalar", ACTF(junk[:], pb[bk][:, :], AF.Square, accum_out=stt[b][:, half:half + 1]), [PB[bk]], [Rjunk, Rst[b]])

    def s1(i):
        b = i % NB
        st = stt[b]
        P.op("vector", TT(st[:, 2:3], st[:, 0:1], st[:, 1:2], ALU.add), [Rst[b]], [Rst[b]])
        P.op("scalar", ACTF(st[:, 3:4], st[:, 2:3], AF.Ln, bias=EPS, scale=1.0 / D_MODEL), [Rst[b]], [Rst[b]])
        P.op("scalar", ACTF(st[:, 4:5], st[:, 3:4], AF.Exp, scale=-0.5), [Rst[b]], [Rst[b]])
        for half in range(2):
            bk = half + 2 * (i % 2)
            P.op("scalar", ACTF(osb[b][:, half * 512:(half + 1) * 512], pb[bk][:, :], AF.Copy, scale=st[:, 4:5]),
                 [PB[bk], Rst[b]], [Ros[b]])

    def s2(i):
        b = i % NB
        P.op("vector", TT(osb[b][:], osb[b][:], gpost[:], ALU.mult), [Ros[b], Rgp], [Ros[b]])
        P.op("gpsimd", TT(osb[b][:], osb[b][:], xr[b][:], ALU.add), [Ros[b], Rxr[b]], [Ros[b]])
        P.dma("sync", yout[i * 128:(i + 1) * 128, :], osb[b][:], reads=[Ros[b]])

    pipeline(L // 128, [s0, s1, s2])


def phase_mem(k, lay):
    nc, P, L = k.nc, k.P, k.L
    A = k.arena
    A.reset()
    pb, PB = k.pb, k.PB
    wst = [A.alloc([128, 512], F32) for _ in range(2)]
    Rwst = [Res(), Res()]
    wkv = A.alloc([128, 8, 512], BF16)
    Rwkv = Res()
    gm = A.alloc([128, 8], F32)
    Rgm = Res()
    P.dma("sync", gm[:], k.mem_g[lay].rearrange("(k p) -> p k", p=128), writes=[Rgm], allow_slow_non_contiguous=True)
    for kk in range(8):
        P.dma("sync", wst[kk % 2][:], k.w_kv[lay, kk * 128:(kk + 1) * 128, :], writes=[Rwst[kk % 2]])
        P.op("vector", TS(wkv[:, kk, :], wst[kk % 2][:], gm[:, kk:kk + 1], None, ALU.mult), [Rwst[kk % 2], Rgm], [Rwkv])
    xt = A.alloc([128, D_MODEL], F32)
    junk = A.alloc([128, D_MODEL], F32)
    st = A.alloc([128, 4], F32)
    xs = A.alloc([128, D_MODEL], BF16)
    Rxt, Rjunk, Rst, Rxs = Res(), Res(), Res(), Res()
    memT = A.alloc([128, 8, 256], BF16)
    RmemT = Res()
    pT = pb[7][:, :].bitcast(BF16)
    for t in range(2):
        P.dma("sync", xt[:], k.mem[t * 128:(t + 1) * 128, :], writes=[Rxt])
        rms_tile(k, xt, Rxt, junk, Rjunk, st, Rst, xs, Rxs)
        for kk in range(8):
            P.op("tensor", TR(pT[:, kk * 128:(kk + 1) * 128], xs[:, kk * 128:(kk + 1) * 128], k.ident[:]), [Rxs, k.Rconst], [PB[7]])
        P.op("vector", CP(memT[:, :, t * 128:(t + 1) * 128], pT.rearrange("p (k t) -> p k t", k=8)), [PB[7]], [RmemT])
    kmT = A.alloc([64, 4, 256], BF16)
    vm = A.alloc([128, 2, 256], BF16)
    Rkm, Rvm = Res(), Res()
    for h in range(4):
        bk = h % 2
        for kk in range(8):
            P.op("tensor", MM(pb[bk][0:64, 0:256], wkv[:, kk, h * 64:(h + 1) * 64], memT[:, kk, :], kk == 0, kk == 7),
                 [Rwkv, RmemT], [PB[bk]])
        P.op("vector", CP(kmT[:, h, :], pb[bk][0:64, 0:256]), [PB[bk]], [Rkm])
    for mc in range(2):
        bk = 2 + mc
        for kk in range(8):
            P.op("tensor", MM(pb[bk][:, 0:256], memT[:, kk, mc * 128:(mc + 1) * 128], wkv[:, kk, 256:512], kk == 0, kk == 7),
                 [Rwkv, RmemT], [PB[bk]])
        P.op("vector", CP(vm[:, mc, :], pb[bk][:, 0:256]), [PB[bk]], [Rvm])
    NB = 4
    mk = lambda shape, dt: [A.alloc(shape, dt) for _ in range(NB)]
    rs = lambda: [Res() for _ in range(NB)]
    qT, gz = mk([64, 512], BF16), mk([64, 512], BF16)
    PT = mk([128, 2, 512], BF16)
    rden, y1 = mk([64, 512], F32), mk([64, 512], F32)
    yo = mk([64, 512], BF16)
    Rq, Rgz, RPT, Rrd, Ry1, Ryo = rs(), rs(), rs(), rs(), rs(), rs()

    def geo(i):
        gi, h = i // 4, i % 4
        return slice(gi * 512, (gi + 1) * 512), h, (h % 2) * 64, i % NB

    def m0(i):
        sl, h, p0, b = geo(i)
        P.dma("sync", qT[b][:], k.FT[18 + h // 2, p0:p0 + 64, sl], writes=[Rq[b]])
        P.dma("sync", gz[b][:], k.FT[6 + h // 2, p0:p0 + 64, sl], writes=[Rgz[b]])
        for mc in range(2):
            bk = 4 * (i % 2) + mc
            P.op("tensor", MM(pb[bk][:, :], kmT[:, h, mc * 128:(mc + 1) * 128], qT[b][:]), [Rkm, Rq[b]], [PB[bk]])
            P.op("scalar", ACTF(PT[b][:, mc, :], pb[bk][:, :], AF.Exp, scale=0.125), [PB[bk]], [RPT[b]])

    def m1(i):
        sl, h, p0, b = geo(i)
        bo, bd = 4 * (i % 2) + 2, 4 * (i % 2) + 3
        for mc in range(2):
            P.op("tensor", MM(pb[bo][0:64, :], vm[:, mc, h * 64:(h + 1) * 64], PT[b][:, mc, :], mc == 0, mc == 1),
                 [Rvm, RPT[b]], [PB[bo]])
        for mc in range(2):
            P.op("tensor", MM(pb[bd][0:64, :], k.ones_bf[:, :], PT[b][:, mc, :], mc == 0, mc == 1),
                 [k.Rconst, RPT[b]], [PB[bd]])
        P.op("scalar", ACTF(rden[b][:], pb[bd][0:64, :], AF.Ln), [PB[bd]], [Rrd[b]])
        P.op("scalar", ACTF(rden[b][:], rden[b][:], AF.Exp, scale=-1.0), [Rrd[b]], [Rrd[b]])
        P.op("vector", TT(y1[b][:], pb[bo][0:64, :], rden[b][:], ALU.mult), [PB[bo], Rrd[b]], [Ry1[b]])

    def m2(i):
        sl, h, p0, b = geo(i)
        P.op("gpsimd", TT(yo[b][:], y1[b][:], gz[b][:], ALU.mult), [Ry1[b], Rgz[b]], [Ryo[b]])
        P.dma("gpsimd", k.YT[6 + h // 2, p0:p0 + 64, sl], yo[b][:], reads=[Ryo[b]])

    pipeline((L // 512) * 4, [m0, m1, m2])

def make_consts(L):
    N1 = L // 64
    bf = ml_dtypes.bfloat16
    c = {}
    c["c_ident"] = np.eye(128, dtype=np.float32).astype(bf)
    c["c_identf"] = np.eye(128, dtype=np.float32)
    l1 = np.arange(N1)
    ang1 = 2 * np.pi * np.outer(l1, l1) / N1
    c["c_dft1"] = np.concatenate([np.cos(ang1), np.sin(ang1)], axis=1).astype(np.float32).astype(bf)
    angt = 2 * np.pi * np.outer(np.arange(N1), np.arange(64)) / L
    sc = 1.0 / math.sqrt(L * 64.0)
    c["c_tw"] = np.stack([np.cos(angt) * sc, -np.sin(angt) * sc, -np.cos(angt) * sc], axis=1).astype(np.float32)
    a2 = 2 * np.pi * np.outer(np.arange(64), np.arange(64)) / 64
    C2, S2 = np.cos(a2), np.sin(a2)
    c["c_dft2"] = np.stack([np.concatenate([C2, -S2], 1), np.concatenate([S2, C2], 1)], axis=1).astype(np.float32).astype(bf)
    bdc = np.zeros((128, 128)); bds = np.zeros((128, 128))
    for b in range(2):
        bdc[b * 64:(b + 1) * 64, b * 64:(b + 1) * 64] = C2
        bds[b * 64:(b + 1) * 64, b * 64:(b + 1) * 64] = S2
    c["c_bd64"] = np.stack([bdc, bds], axis=1).astype(np.float32).astype(bf)
    j = np.arange(128)[:, None]
    s = np.arange(128)[None, :]
    same = (j // 64) == (s // 64)
    gm = np.zeros((128, 12, 128), np.float32)
    gm[:, 0] = (j > s) & same
    gm[:, 1] = (j < s) & same
    gm[:, 2] = (j <= s) & same
    gm[:, 3] = (j >= s) & same
    gm[:, 4] = (j >= s) & same
    gm[:, 5] = (j <= s) & same
    gm[:, 6] = (j > s) & same
    gm[:, 7] = (j < s) & same
    gm[:, 8] = (j > s) & same
    gm[:, 9] = (j < s) & same
    gm[:, 10] = (j < 64) & (s >= 0)
    gm[:, 11] = (j >= 64) & (s >= 0)
    c["c_gmask"] = gm
    bo = np.zeros((128, 128), np.float32)
    bo[:64, :64] = 1
    bo[64:, 64:] = 1
    c["c_bones"] = bo.astype(bf)
    return c


_prog_cache = {}


def run_cores(per_core_inputs, L, depth, debug=False, stop_after=None):
    key = (L, depth, debug, stop_after)
    nc = build_program(L, depth, debug, stop_after)
    consts = make_consts(L)
    in_maps = []
    for d in per_core_inputs:
        m = dict(consts)
        m.update(d)
        in_maps.append(m)
    res = run_bass_kernel_spmd(nc, in_maps, core_ids=list(range(len(in_maps))))
    return res.results


def kernel(x_prompt, x_sample, mem_prompt, mem_sample, pre_norm_g, post_norm_g, w_in, w_fnet, gdn_conv_w,
           gdn_a_log, gdn_dt_bias, gdn_norm_g, na_rpb, mem_norm_g, w_mem_kv, w_out):
    f = lambda a: np.ascontiguousarray(np.asarray(a, dtype=np.float32))
    xs = [f(x_prompt[i]) for i in range(4)] + [f(x_sample[i]) for i in range(2)]
    ms = [f(mem_prompt[i]) for i in range(4)] + [f(mem_sample[i]) for i in range(2)]
    shared = dict(pre_norm_g=f(pre_norm_g), post_norm_g=f(post_norm_g), w_in=f(w_in), w_fnet=f(w_fnet),
                  gdn_conv_w=f(gdn_conv_w), gdn_a_log=f(gdn_a_log).reshape(DEPTH, 8),
                  gdn_dt_bias=f(gdn_dt_bias).reshape(DEPTH, 8), gdn_norm_g=f(gdn_norm_g), na_rpb=f(na_rpb),
                  mem_norm_g=f(mem_norm_g), w_mem_kv=f(w_mem_kv), w_out=f(w_out))
    per_core = []
    for c in range(8):
        s = c if c < 6 else c - 6
        d = dict(shared)
        d["x"] = xs[s]
        d["mem"] = ms[s]
        per_core.append(d)
    res = run_cores(per_core, SEQ, DEPTH)
    y_prompt = np.stack([res[i]["y"] for i in range(4)], axis=0).astype(np.float32)
    y_sample = np.stack([res[4 + i]["y"] for i in range(2)], axis=0).astype(np.float32)
    return (y_prompt, y_sample)


def phase_fnet(k, lay):
    nc, P, L, N1 = k.nc, k.P, k.L, k.N1
    A = k.arena
    A.reset()
    pb, PB = k.pb, k.PB
    na_table_build(k, lay)
    wf = A.alloc([128, 2, 256], F32)
    wfb = A.alloc([128, 2, 256], BF16)
    bd = A.alloc([128, 2, 128], BF16)
    wmix = A.alloc([128, 2, 2, 256], BF16)
    dft2 = A.alloc([64, 2, 128], BF16)
    Rw, Rmix = Res(), Res()
    P.dma("sync", wf[:], k.w_fnet[lay].rearrange("(c p) o -> p c o", p=128), writes=[Rw])
    P.dma("sync", bd[:], k.c_bd64, writes=[Rw])
    P.dma("sync", dft2[:], k.c_dft2, writes=[Rw])
    P.op("vector", CP(wfb[:], wf[:]), [Rw], [Rw])
    for cc in range(2):
        for ri in range(2):
            bk = cc * 2 + ri
            P.op("tensor", MM(pb[bk][:, 0:256], bd[:, ri, :], wfb[:, cc, :]), [Rw], [PB[bk]])
            P.op("vector", CP(wmix[:, cc, ri, :], pb[bk][:, 0:256]), [PB[bk]], [Rmix])
    mark = A.off
    dft1 = A.alloc([N1, 2 * N1], BF16)
    tw = A.alloc([N1, 3, 64], F32)
    X = A.alloc([N1, 64 * 256], BF16)
    Bsb = A.alloc([N1, 64, 2, 256], BF16)
    Rc1, RX, RB = Res(), Res(), Res()
    P.dma("sync", dft1[:], k.c_dft1, writes=[Rc1])
    P.dma("sync", tw[:], k.c_tw, writes=[Rc1])
    P.dma("sync", X[:], k.U.rearrange("(a b) c -> a (b c)", b=64), writes=[RX])
    t1 = [A.alloc([N1, 256], F32) for _ in range(2)]
    t2 = [A.alloc([N1, 256], F32) for _ in range(2)]
    Rt1, Rt2 = [Res(), Res()], [Res(), Res()]
    RBd = Res()
    it = 0
    for n in range(32):
        if n > 0 and n % 8 == 0:
            q = n // 8 - 1
            P.dma("gpsimd", k.Bd[:, q * 16:(q + 1) * 16, :, :], Bsb[:, q * 16:(q + 1) * 16, :, :], reads=[RB], writes=[RBd])
        ba, bs = 2 * (n % 2), 2 * (n % 2) + 1
        P.op("tensor", MM(pb[ba][:N1, :], dft1[:, 0:N1], X[:, n * 512:(n + 1) * 512]), [Rc1, RX], [PB[ba]])
        P.op("tensor", MM(pb[bs][:N1, :], dft1[:, N1:2 * N1], X[:, n * 512:(n + 1) * 512]), [Rc1, RX], [PB[bs]])
        for hh in range(2):
            l2 = 2 * n + hh
            cs = slice(hh * 256, (hh + 1) * 256)
            b = it % 2
            it += 1
            P.op("scalar", ACTF(t1[b][:], pb[ba][:N1, cs], AF.Copy, scale=tw[:, 0, l2:l2 + 1]), [PB[ba], Rc1], [Rt1[b]])
            P.op("scalar", ACTF(t2[b][:], pb[ba][:N1, cs], AF.Copy, scale=tw[:, 1, l2:l2 + 1]), [PB[ba], Rc1], [Rt2[b]])
            P.op("vector", STT(Bsb[:, l2, 0, :], pb[bs][:N1, cs], tw[:, 1, l2:l2 + 1], t1[b][:], ALU.mult, ALU.add),
                 [PB[bs], Rc1, Rt1[b]], [RB])
            P.op("vector", STT(Bsb[:, l2, 1, :], pb[bs][:N1, cs], tw[:, 2, l2:l2 + 1], t2[b][:], ALU.mult, ALU.add),
                 [PB[bs], Rc1, Rt2[b]], [RB])
    P.dma("gpsimd", k.Bd[:, 48:64, :, :], Bsb[:, 48:64, :, :], reads=[RB], writes=[RBd])
    P.barrier()
    A.reset(mark)
    B2 = A.alloc([64, N1, 2, 128], BF16)
    YTs = A.alloc([128, 2, 2, L], BF16)
    RB2, RYT = Res(), Res()
    ev = 0
    for cc in range(2):
        for ri in range(2):
            P.dma("sync", B2[:, :, ri, :], k.Bd[:, :, ri, cc * 128:(cc + 1) * 128].rearrange("k l c -> l k c"),
                  reads=[RBd], writes=[RB2])
        for k1 in range(N1):
            bk = (k1 // 4) % 4
            slot = k1 % 4
            oap = pb[bk][:, slot * 128:(slot + 1) * 128]
            P.op("tensor", MM(oap, B2[:, k1, 0, :], dft2[:, 0, :], True, False), [RB2, Rw], [PB[bk]])
            P.op("tensor", MM(oap, B2[:, k1, 1, :], dft2[:, 1, :], False, True), [RB2, Rw], [PB[bk]])
            if slot == 3:
                for ri in range(2):
                    src = pb[bk][:, :].rearrange("p (s r q) -> p s r q", s=4, r=2)[:, :, ri, :]
                    dst = YTs[:, cc, ri, :].rearrange("p (q a) -> p a q", a=N1)[:, k1 - 3:k1 + 1, :]
                    if ev % 2 == 0:
                        P.op("vector", CP(dst, src), [PB[bk]], [RYT])
                    else:
                        P.op("scalar", ACTF(dst, src, AF.Copy), [PB[bk]], [RYT])
                    ev += 1
    gz = [A.alloc([128, 512], BF16) for _ in range(2)]
    yo = [A.alloc([128, 512], BF16) for _ in range(2)]
    Rgz, Ryo = [Res(), Res()], [Res(), Res()]
    it = 0
    for gi in range(L // 512):
        sl = slice(gi * 512, (gi + 1) * 512)
        for oc in range(2):
            b = it % 2
            bk = 4 + it % 4
            it += 1
            P.dma("sync", gz[b][:], k.FT[oc, :, sl], writes=[Rgz[b]])
            n = 0
            for cc in range(2):
                for ri in range(2):
                    P.op("tensor", MM(pb[bk][:, :], wmix[:, cc, ri, oc * 128:(oc + 1) * 128], YTs[:, cc, ri, sl], n == 0, n == 3),
                         [Rmix, RYT], [PB[bk]])
                    n += 1
            P.op("vector", TT(yo[b][:], pb[bk][:, :], gz[b][:], ALU.mult), [PB[bk], Rgz[b]], [Ryo[b]])
            P.dma("gpsimd", k.YT[oc, :, sl], yo[b][:], reads=[Ryo[b]])


def na_table_build(k, lay):
    nc, P, L = k.nc, k.P, k.L
    negt = k.negt
    Rneg = k.Rconst
    Rfill = {}
    Rdiag = []
    for h in range(4):
        for dl in range(8):
            Rfill[(h, dl)] = Res()
            P.dma("gpsimd", k.NAB[h, dl, :, :], negt[:], reads=[Rneg], writes=[Rfill[(h, dl)]])
    nabt = k.NAB.tensor
    rpbt = k.rpb.tensor
    n = 0
    for h in range(4):
        for dl in range(8):
            dbase = (h * 8 + dl) * 64 * 512
            sbase = ((lay * 4 + h) * 15 + (7 - dl)) * 31
            q = "gpsimd"
            n += 1
            rf = [Rfill[(h, dl)]]
            r_ = Res()
            Rdiag.append(r_)
            P.dma(q, bass.AP(nabt, dbase + 8 * 512, [[513, 49], [64, 8], [1, 16]]),
                  bass.AP(rpbt, sbase + 7, [[0, 49], [31, 8], [1, 16]]), reads=rf, writes=[r_])
            r_ = Res()
            Rdiag.append(r_)
            P.dma(q, bass.AP(nabt, dbase, [[64, 8], [512, 8], [1, 16]]),
                  bass.AP(rpbt, sbase + 15, [[31, 8], [-1, 8], [1, 16]]), reads=rf, writes=[r_])
            r_ = Res()
            Rdiag.append(r_)
            P.dma(q, bass.AP(nabt, dbase + 57 * 512 + 48, [[64, 8], [512, 7], [1, 16]]),
                  bass.AP(rpbt, sbase + 6, [[31, 8], [-1, 7], [1, 16]]), reads=rf, writes=[r_])


def phase_na(k, lay):
    nc, P, L = k.nc, k.P, k.L
    rows = L // 64
    A = k.arena
    A.reset()
    pb, PB = k.pb, k.PB
    A.reset()
    tb2 = [A.alloc([64, 8, 512], F32) for _ in range(2)]
    qT2 = [A.alloc([64, L], BF16) for _ in range(2)]
    kT2 = [A.alloc([64, L], BF16) for _ in range(2)]
    vh2 = [A.alloc([64, rows, 64], BF16) for _ in range(2)]
    gzh = A.alloc([64, L], BF16)
    yrow = A.alloc([64, L], BF16)
    Rtb2, Rq2, Rk2, Rv2 = [Res(), Res()], [Res(), Res()], [Res(), Res()], [Res(), Res()]
    Rgz, Ry = Res(), Res()

    def head_loads(hh):
        hb_ = hh % 2
        p0_ = (hh % 2) * 64
        P.dma("sync", tb2[hb_][:], k.NAB[hh].rearrange("d w x -> w d x"), writes=[Rtb2[hb_]])
        P.dma("sync", qT2[hb_][:], k.FT[14 + hh // 2, p0_:p0_ + 64, :], writes=[Rq2[hb_]])
        P.dma("sync", kT2[hb_][:], k.FT[16 + hh // 2, p0_:p0_ + 64, :], writes=[Rk2[hb_]])
        P.dma("sync", vh2[hb_][:], k.CV[:, hh * 64:(hh + 1) * 64].rearrange("(r w) d -> w r d", w=64), writes=[Rv2[hb_]])

    head_loads(0)
    NB = 4
    s1 = [A.alloc([64, 512], F32) for _ in range(NB)]
    pp = [A.alloc([64, 512], F32) for _ in range(NB)]
    pn = [A.alloc([64, 512], BF16) for _ in range(NB)]
    den = [A.alloc([64, 2], F32) for _ in range(NB)]
    pTs = [A.alloc([64, 8, 64], BF16) for _ in range(NB)]
    Rs1, Rpp, Rpn, Rden, RpT = ([Res() for _ in range(NB)] for _ in range(5))
    for h in range(4):
        p0 = (h % 2) * 64
        hb = h % 2
        tb, qT, kT, vh = tb2[hb], qT2[hb], kT2[hb], vh2[hb]
        Rtb, Rq, Rk, Rv = Rtb2[hb], Rq2[hb], Rk2[hb], Rv2[hb]
        P.dma("sync", gzh[:], k.FT[4 + h // 2, p0:p0 + 64, :], writes=[Rgz])
        if h + 1 < 4:
            head_loads(h + 1)
        rsof = lambda r: min(max(r - 4, 0), rows - 8)

        def stA1(r):
            rs = rsof(r)
            b, bS = r % NB, r % 2
            P.op("tensor", MM(pb[bS][0:64, :], qT[:, r * 64:(r + 1) * 64], kT[:, rs * 64:rs * 64 + 512]), [Rq, Rk], [PB[bS]])
            P.op("vector", STT(s1[b][:], pb[bS][0:64, :], 0.125, tb[:, r - rs, :], ALU.mult, ALU.add), [PB[bS], Rtb], [Rs1[b]])

        def stA2(r):
            b = r % NB
            P.op("scalar", ACTF(pp[b][:], s1[b][:], AF.Exp, accum_out=den[b][:, 0:1]), [Rs1[b]], [Rpp[b], Rden[b]])
            P.op("vector", RECIP(den[b][:, 1:2], den[b][:, 0:1]), [Rden[b]], [Rden[b]])
            P.op("vector", TS(pn[b][:], pp[b][:], den[b][:, 1:2], None, ALU.mult), [Rpp[b], Rden[b]], [Rpn[b]])

        def stB(r):
            b = r % NB
            bT = 2 + r % 2
            pTp = pb[bT][:, 0:256].bitcast(BF16)
            for i in range(8):
                P.op("tensor", TR(pTp[0:64, i * 64:(i + 1) * 64], pn[b][:, i * 64:(i + 1) * 64], k.ident[0:64, 0:64]),
                     [Rpn[b], k.Rconst], [PB[bT]])
            P.op("scalar", ACTF(pTs[b][:], pTp[0:64, :].rearrange("p (i q) -> p i q", i=8), AF.Copy), [PB[bT]], [RpT[b]])

        def stC(r):
            rs = rsof(r)
            b = r % NB
            bo = 4 + (r // 8) % 2
            slot = r % 8
            for i in range(8):
                P.op("tensor", MM(pb[bo][0:64, slot * 64:(slot + 1) * 64], vh[:, rs + i, :], pTs[b][:, i, :], i == 0, i == 7),
                     [Rv, RpT[b]], [PB[bo]])
            if slot == 7:
                sl = slice((r - 7) * 64, (r + 1) * 64)
                P.op("vector", TT(yrow[:, sl], pb[bo][0:64, :], gzh[:, sl], ALU.mult), [PB[bo], Rgz], [Ry])

        for t in range(rows + 3):
            if t < rows:
                stA1(t)
            if 0 <= t - 1 < rows:
                stA2(t - 1)
            if 0 <= t - 2 < rows:
                stB(t - 2)
            if 0 <= t - 3 < rows:
                stC(t - 3)
        P.dma("gpsimd", k.YT[4 + h // 2, p0:p0 + 64, :], yrow[:], reads=[Ry])

def phase_gdn(k, lay):
    nc, P, L = k.nc, k.P, k.L
    A = k.arena
    A.reset()
    pb, PB = k.pb, k.PB
    id64 = k.ident[0:64, 0:64]
    cw = A.alloc([128, 6, 5], F32)
    Dg = A.alloc([128, 6, 5, 128], BF16)
    Rcw, RDg = Res(), Res()
    for c in range(6):
        P.dma("sync", cw[:, c, :], k.conv_w[lay, :, c * 128:(c + 1) * 128].rearrange("j p -> p j"), writes=[Rcw],
              allow_slow_non_contiguous=True)
    for c in range(6):
        for j in range(5):
            P.op("vector", TS(Dg[:, c, j, :], k.identf[:], cw[:, c, j:j + 1], None, ALU.mult), [Rcw, k.Rconst], [RDg])
    xc = [A.alloc([128, 6, 516], BF16) for _ in range(2)]
    actf = [A.alloc([128, 4, 512], F32) for _ in range(3)]
    sq = [A.alloc([128, 4, 512], BF16) for _ in range(2)]
    lnt = [A.alloc([128, 4, 512], F32) for _ in range(2)]
    qn = [A.alloc([128, 6, 512], BF16) for _ in range(3)]
    tm = [A.alloc([128, 6, 4, 128], BF16) for _ in range(2)]
    Rxc = [[Res() for _ in range(6)] for _ in range(2)]
    Ract = [[Res() for _ in range(4)] for _ in range(3)]
    Rsq = [[Res() for _ in range(4)] for _ in range(2)]
    Rln = [[Res() for _ in range(4)] for _ in range(2)]
    Rqn = [[Res() for _ in range(6)] for _ in range(3)]
    Rtm = [[Res() for _ in range(6)] for _ in range(2)]
    ng = L // 512

    def gA(gi):
        b = gi % 2
        tok0 = gi * 512
        for c in range(6):
            lo = tok0 - 2 if gi > 0 else tok0
            hi = tok0 + 514 if gi < ng - 1 else tok0 + 512
            if gi == 0:
                P.op("gpsimd", MSET(xc[b][:, c, 0:2], 0.0), writes=[Rxc[b][c]])
            if gi == ng - 1:
                P.op("gpsimd", MSET(xc[b][:, c, 514:516], 0.0), writes=[Rxc[b][c]])
            P.dma("sync", xc[b][:, c, (lo - (tok0 - 2)):(hi - (tok0 - 2))], k.FT[8 + c, :, lo:hi], writes=[Rxc[b][c]])
        for c in range(6):
            bk = c % 4
            for j in range(5):
                P.op("tensor", MM(pb[bk][:, :], Dg[:, c, j, :], xc[b][:, c, j:j + 512], j == 0, j == 4), [RDg, Rxc[b][c]], [PB[bk]])
            if c < 4:
                P.op("scalar", ACTF(actf[gi % 3][:, c, :], pb[bk][:, :], AF.Silu), [PB[bk]], [Ract[gi % 3][c]])
            else:
                P.op("scalar", ACTF(qn[gi % 3][:, c, :], pb[bk][:, :], AF.Silu), [PB[bk]], [Rqn[gi % 3][c]])

    def gB(gi):
        b = gi % 2
        for c in range(4):
            P.op("gpsimd", TT(sq[b][:, c, :], actf[gi % 3][:, c, :], actf[gi % 3][:, c, :], ALU.mult), [Ract[gi % 3][c]], [Rsq[b][c]])
            bk2 = 4 + c % 2
            P.op("tensor", MM(pb[bk2][:, :], k.bones[:], sq[b][:, c, :]), [k.Rconst, Rsq[b][c]], [PB[bk2]])
            P.op("scalar", ACTF(lnt[b][:, c, :], pb[bk2][:, :], AF.Ln, bias=EPS), [PB[bk2]], [Rln[b][c]])
        for c in range(4):
            P.op("scalar", ACTF(lnt[b][:, c, :], lnt[b][:, c, :], AF.Exp, scale=-0.5), [Rln[b][c]], [Rln[b][c]])

    def gC(gi):
        b = gi % 2
        q3 = gi % 3
        tok0 = gi * 512
        for c in range(4):
            P.op("vector", STT(qn[q3][:, c, :], actf[q3][:, c, :], 0.125 if c < 2 else 1.0, lnt[b][:, c, :], ALU.mult, ALU.mult),
                 [Ract[q3][c], Rln[b][c]], [Rqn[q3][c]])
            P.dma("gpsimd", k.QKn[c, :, tok0:tok0 + 512], qn[q3][:, c, :], reads=[Rqn[q3][c]])
        for c in range(6):
            bk = 6 + c % 2
            pT = pb[bk][:, 0:256].bitcast(BF16)
            for t in range(4):
                P.op("tensor", TR(pT[:, t * 128:(t + 1) * 128], qn[q3][:, c, t * 128:(t + 1) * 128], k.ident[:]),
                     [Rqn[q3][c], k.Rconst], [PB[bk]])
            P.op("vector", CP(tm[b][:, c, :, :], pT.rearrange("p (t c) -> p t c", t=4)), [PB[bk]], [Rtm[b][c]])
            P.dma("gpsimd", k.QKVt[tok0:tok0 + 512, c * 128:(c + 1) * 128].rearrange("(t p) c -> p t c", p=128), tm[b][:, c, :, :],
                  reads=[Rtm[b][c]])

    pipeline(ng, [gA, gB, gC])
    P.barrier()
    A.reset()
    ntile = L // 128
    gm = A.alloc([128, 12, 128], F32)
    rmask = A.alloc([128, 2], F32)
    idr = A.alloc([128, 128], F32R)
    Rgm = Res()
    P.dma("sync", gm[:], k.c_gmask, writes=[Rgm])
    P.op("vector", CP(idr[:], k.identf[:]), [k.Rconst], [Rgm])
    P.op("vector", CP(rmask[:, :], gm[:, 10:12, 0]), [Rgm], [Rgm])
    MRk = [gm[:, 0, :], gm[:, 1, :]]
    Tm = [gm[:, 2, :], gm[:, 3, :]]
    INCLk = [gm[:, 4, :], gm[:, 5, :]]
    STRk = [gm[:, 6, :], gm[:, 7, :]]
    M2 = [gm[:, 8, :], gm[:, 9, :]]
    ONEC = [gm[:, 10, :], gm[:, 11, :]]
    QKg = [[A.alloc([64, 8, 512], BF16) for _ in range(2)] for _ in range(2)]
    TMg = [[A.alloc([128, 4, 768], BF16) for _ in range(2)] for _ in range(2)]
    Gg = [[A.alloc([128, 4, 16], F32) for _ in range(2)] for _ in range(2)]
    Rgrp = [[Res(), Res()], [Res(), Res()]]
    S32 = [A.alloc([64, 4, 64], F32) for _ in range(2)]
    Sbf = [A.alloc([64, 4, 64], BF16) for _ in range(2)]
    RS32, RSbf = [Res(), Res()], [Res(), Res()]
    for d in range(2):
        P.op("vector", MSET(S32[d][:], 0.0), writes=[RS32[d]])
        P.op("vector", MSET(Sbf[d][:], 0.0), writes=[RSbf[d]])

    def al2(shape, dt):
        return [A.alloc(shape, dt) for _ in range(2)]

    Grhs, E, EMi, EMs, t1 = (al2([128, 4, 128], F32) for _ in range(5))
    EG = al2([128, 16], F32)
    ekm = al2([128, 2, 4], F32)
    nb = al2([128, 4], F32)
    be = al2([128, 4], F32)
    qkb, qkT = (al2([128, 4, 128], BF16) for _ in range(2))
    wT, qdT = (al2([64, 4, 128], BF16) for _ in range(2))
    qd, vn = (al2([128, 4, 64], BF16) for _ in range(2))
    kdm = [al2([128, 4, 64], BF16) for _ in range(2)]
    Rkdm = [[Res(), Res()], [Res(), Res()]]
    Rekm = [Res(), Res()]
    for b_ in range(2):
        P.op("vector", MSET(vn[b_][:], 0.0), writes=[Rgm])
    Rm = [al2([128, 4, 128], F32R) for _ in range(2)]
    XPt = [al2([128, 4, 2, 128], F32R) for _ in range(2)]
    Xm = [[XPt[s_][b_][:, :, 0, :] for b_ in range(2)] for s_ in range(2)]
    Pm = [[XPt[s_][b_][:, :, 1, :] for b_ in range(2)] for s_ in range(2)]
    osb = al2([128, 4, 64], F32)
    R_ = lambda: [Res(), Res()]
    RGrhs, RE, REMi, REMs, Rt1, REG, Rnb, Rbe, Rqkb, RqkT, RwT, Rqd, RqdT, Rkd, Rvn, Rosb = (R_() for _ in range(16))
    RPm = [R_() for _ in range(2)]
    RRm = [R_() for _ in range(2)]
    RXm = [R_() for _ in range(2)]
    NLV = 6
    all_steps = []
    for hs in range(2 * ntile):
        d = hs % 2
        ti = hs // 2
        tl = ti if d == 0 else ntile - 1 - ti
        n = tl % 4
        gb = (ti // 4) % 2
        b = d
        q = [0, 1, 2, 3] if d == 0 else [4, 5, 6, 7]
        stg = []
        cur = []

        def add(eng, fn, reads=(), writes=()):
            cur.append((eng, fn, tuple(reads), tuple(writes), False, None))

        def adddma(eng, out, in_, reads=(), writes=()):
            cur.append((eng, (out, in_), tuple(reads), tuple(writes), True, None))

        def stage():
            if cur:
                stg.append(list(cur))
                del cur[:]

        if ti % 4 == 0:
            g0 = (tl // 4) * 512
            sl = slice(g0, g0 + 512)
            for h in range(4):
                p0 = (h % 2) * 64
                adddma("sync", QKg[d][gb][:, h, :], k.QKn[h // 2, p0:p0 + 64, sl], writes=[Rgrp[d][gb]])
                adddma("sync", QKg[d][gb][:, 4 + h, :], k.QKn[2 + h // 2, p0:p0 + 64, sl], writes=[Rgrp[d][gb]])
            adddma("sync", TMg[d][gb][:], k.QKVt[sl, :].rearrange("(n p) c -> p n c", p=128), writes=[Rgrp[d][gb]])
            adddma("sync", Gg[d][gb][:], k.G[sl, :].rearrange("(n p) c -> p n c", p=128), writes=[Rgrp[d][gb]])
        QK, TM, GG, RG = QKg[d][gb], TMg[d][gb], Gg[d][gb], Rgrp[d][gb]
        cs = slice(n * 128, n * 128 + 128)
        gcol = GG[:, n, 4 * d:4 * d + 4]
        bcol = GG[:, n, 8 + 4 * d:12 + 4 * d]
        bcw = lambda ap: ap.unsqueeze(2).to_broadcast([128, 4, 128])
        bc64 = lambda ap: ap.unsqueeze(2).to_broadcast([128, 4, 64])
        mk = lambda m: m.unsqueeze(1).to_broadcast([128, 4, 128])
        v4 = lambda ap: ap.rearrange("p (a b) -> p a b", a=4)
        fl = lambda ap: ap.rearrange("p a b -> p (a b)")
        add("gpsimd", TT(Grhs[b][:], mk(MRk[d]), bcw(gcol), ALU.mult), [Rgm, RG], [RGrhs[b]])
        for (qq, lt) in enumerate((Tm[d], M2[d], ONEC[0], ONEC[1])):
            add("tensor", MM(pb[q[1]][:, 4 * qq:4 * qq + 4], lt, gcol), [Rgm, RG], [PB[q[1]]])
        add("scalar", ACTF(EG[b][:], pb[q[1]][:, 0:16], AF.Exp), [PB[q[1]]], [REG[b]])
        for f in range(2):
            add("vector", TS(ekm[b][:, f, :], EG[b][:, 4:8], rmask[:, f:f + 1], None, ALU.mult), [REG[b], Rgm], [Rekm[b]])
        stage()
        add("tensor", MM(pb[q[0]][:, :], Tm[d], fl(Grhs[b][:])), [Rgm, RGrhs[b]], [PB[q[0]]])
        add("scalar", ACTF(fl(E[b][:]), pb[q[0]][:, :], AF.Exp), [PB[q[0]]], [RE[b]])
        for h in range(4):
            add("tensor", MM(pb[q[2]][:, h * 128:(h + 1) * 128], QK[:, 4 + h, cs], QK[:, 4 + h, cs]), [RG], [PB[q[2]]])
        for h in range(4):
            add("tensor", MM(pb[q[3]][:, h * 128:(h + 1) * 128], QK[:, h, cs], QK[:, 4 + h, cs]), [RG], [PB[q[3]]])
        add("vector", TS(nb[b][:], bcol, -1.0, None, ALU.mult), [RG], [Rnb[b]])
        add("vector", TT(be[b][:], bcol, EG[b][:, 0:4], ALU.mult), [RG, REG[b]], [Rbe[b]])
        stage()
        add("vector", TT(EMs[b][:], E[b][:], mk(STRk[d]), ALU.mult), [RE[b], Rgm], [REMs[b]])
        add("gpsimd", TT(EMi[b][:], E[b][:], mk(INCLk[d]), ALU.mult), [RE[b], Rgm], [REMi[b]])
        add("vector", TT(Xm[0][b][:, :, 0:64], v4(TM[:, n, 512:768]), bc64(bcol), ALU.mult), [RG], [RXm[0][b]])
        add("vector", TT(Xm[0][b][:, :, 64:128], v4(TM[:, n, 256:512]), bc64(be[b][:, :]), ALU.mult), [RG, Rbe[b]], [RXm[0][b]])
        stage()
        add("vector", TT(t1[b][:], v4(pb[q[2]][:, :]), EMs[b][:], ALU.mult), [PB[q[2]], REMs[b]], [Rt1[b]])
        add("vector", TT(Pm[0][b], t1[b][:], bcw(nb[b][:, :]), ALU.mult), [Rt1[b], Rnb[b]], [RPm[0][b]])
        add("vector", TT(qkb[b][:], v4(pb[q[3]][:, :]), EMi[b][:], ALU.mult), [PB[q[3]], REMi[b]], [Rqkb[b]])
        add("gpsimd", TT(qd[b][:], v4(TM[:, n, 0:256]), bc64(EG[b][:, 0:4]), ALU.mult), [RG, REG[b]], [Rqd[b]])
        for f in range(2):
            add("gpsimd", TT(kdm[f][b][:], v4(TM[:, n, 256:512]), bc64(ekm[b][:, f, :]), ALU.mult), [RG, Rekm[b]], [Rkdm[f][b]])
        stage()
        for h in range(4):
            add("tensor", MM(pb[q[0]][:, h * 128:(h + 1) * 128], Pm[0][b][:, h, :], idr[:, :]), [RPm[0][b], Rgm], [PB[q[0]]])
        add("scalar", ACTF(fl(Rm[0][b][:]), pb[q[0]][:, :], AF.Copy), [PB[q[0]]], [RRm[0][b]])
        pTb = pb[q[1]][:, 0:256].bitcast(BF16)
        for h in range(4):
            add("tensor", TR(pTb[:, h * 128:(h + 1) * 128], qkb[b][:, h, :], k.ident[:]), [Rqkb[b], k.Rconst], [PB[q[1]]])
        add("vector", CP(fl(qkT[b][:]), pTb[:, :]), [PB[q[1]]], [RqkT[b]])
        stage()
        for h in range(4):
            add("tensor", TR(pTb[0:64, h * 128:(h + 1) * 128], qd[b][:, h, :], k.ident[:]), [Rqd[b], k.Rconst], [PB[q[1]]])
        add("vector", CP(fl(qdT[b][:]), pTb[0:64, :]), [PB[q[1]]], [RqdT[b]])
        stage()
        for j in range(NLV):
            sj, sn = j % 2, (j + 1) % 2
            wide = j < NLV - 2
            if j < NLV - 1:
                for h in range(4):
                    add("tensor", MM(pb[q[1]][:, h * 128:(h + 1) * 128], Pm[sj][b][:, h, :], Rm[sj][b][:, h, :]),
                        [RRm[sj][b], RPm[sj][b]], [PB[q[1]]])
                add("scalar", ACTF(fl(Rm[sn][b][:]), pb[q[1]][:, :], AF.Copy), [PB[q[1]]], [RRm[sn][b]])
                stage()
            if wide:
                for h in range(4):
                    bk = q[2 + h // 2]
                    add("tensor", MM(pb[bk][:, (h % 2) * 256:(h % 2) * 256 + 256], Rm[sj][b][:, h, :],
                                     XPt[sj][b][:, h, :, :].rearrange("p a c -> p (a c)")),
                        [RRm[sj][b], RXm[sj][b], RPm[sj][b]], [PB[bk]])
                pv = k.pball[:, q[2] * 512:(q[2] + 2) * 512].rearrange("p (h a c) -> p h a c", h=4, a=2)
                add("vector", CP(Pm[sn][b], pv[:, :, 1, :]), [PB[q[2]], PB[q[3]]], [RPm[sn][b]])
                add("vector", TT(Xm[sn][b], pv[:, :, 0, :], Xm[sj][b], ALU.add), [PB[q[2]], PB[q[3]], RXm[sj][b]], [RXm[sn][b]])
            else:
                for h in range(4):
                    add("tensor", MM(pb[q[2]][:, h * 128:(h + 1) * 128], Rm[sj][b][:, h, :], Xm[sj][b][:, h, :]),
                        [RRm[sj][b], RXm[sj][b]], [PB[q[2]]])
                add("vector", TT(Xm[sn][b], v4(pb[q[2]][:, :]), Xm[sj][b], ALU.add), [PB[q[2]], RXm[sj][b]], [RXm[sn][b]])
            stage()
        XF = Xm[NLV % 2][b]
        RXF = RXm[NLV % 2][b]
        for h in range(4):
            add("tensor", MM(pb[q[0]][0:64, h * 128:(h + 1) * 128], XF[:, h, 64:128], idr[:, :]), [RXF, Rgm], [PB[q[0]]])
        add("scalar", ACTF(fl(wT[b][:]), pb[q[0]][0:64, :], AF.Copy), [PB[q[0]]], [RwT[b]])
        stage()
        for f in ((0, 1) if d == 0 else (1, 0)):
            rows = slice(64 * f, 64 * f + 64)
            for h in range(4):
                add("tensor", MM(pb[q[0]][:, h * 64:(h + 1) * 64], wT[b][:, h, :], Sbf[d][:, h, :]), [RwT[b], RSbf[d]], [PB[q[0]]])
            add("vector", TT(vn[b][rows, :, :], XF[rows, :, 0:64], v4(pb[q[0]][rows, 0:256]), ALU.subtract), [RXF, PB[q[0]]], [Rvn[b]])
            add("gpsimd", TT(S32[d][:], S32[d][:], EG[b][0:64, 8 + 4 * f:12 + 4 * f].unsqueeze(2).to_broadcast([64, 4, 64]), ALU.mult),
                [RS32[d], REG[b]], [RS32[d]])
            stage()
            for h in range(4):
                add("tensor", MM(pb[q[0]][0:64, h * 64:(h + 1) * 64], kdm[f][b][:, h, :], vn[b][:, h, :]), [Rkdm[f][b], Rvn[b]], [PB[q[0]]])
            for h in range(4):
                o = pb[q[1]][:, h * 64:(h + 1) * 64]
                add("tensor", MM(o, qdT[b][:, h, :], Sbf[d][:, h, :], True, False), [RqdT[b], RSbf[d]], [PB[q[1]]])
                add("tensor", MM(o, qkT[b][:, h, :], vn[b][:, h, :], False, True), [RqkT[b], Rvn[b]], [PB[q[1]]])
            add("vector", TT(S32[d][:], S32[d][:], v4(pb[q[0]][0:64, 0:256]), ALU.add), [RS32[d], PB[q[0]]], [RS32[d]])
            add("scalar", ACTF(Sbf[d][:], S32[d][:], AF.Copy), [RS32[d]], [RSbf[d]])
            add("scalar", ACTF(fl(osb[b][rows, :, :]), pb[q[1]][rows, 0:256], AF.Copy), [PB[q[1]]], [Rosb[b]])
            stage()
        adddma("gpsimd", (k.OF if d == 0 else k.OB)[tl * 128:tl * 128 + 128, :], fl(osb[b][:]), reads=[Rosb[b]])
        stage()
        all_steps.append(stg)
    nst = max(len(sg) for sg in all_steps)
    KS = nst // 2 + 1
    nhs = len(all_steps)
    for t in range((nhs - 1) * KS + nst):
        for i in range(max(0, (t - nst) // KS), min(nhs - 1, t // KS) + 1):
            si = t - i * KS
            if 0 <= si < len(all_steps[i]):
                for (eng, fn, reads, writes, isdma, _) in all_steps[i][si]:
                    if isdma:
                        P.dma(eng, fn[0], fn[1], reads=reads, writes=writes)
                    else:
                        P.op(eng, fn, reads, writes)
    P.barrier()
    A.reset()
    gng = A.alloc([128, 64], F32)
    Rgng = Res()
    P.dma("sync", gng[:], k.gdn_g[lay].partition_broadcast(128), writes=[Rgng])
    NB = 4
    aln = lambda shape, dt: [A.alloc(shape, dt) for _ in range(NB)]
    Rn = lambda: [Res() for _ in range(NB)]
    of, ob, osum, sqq, y1 = (aln([128, 256], F32) for _ in range(5))
    ss = aln([128, 8], F32)
    ytm = aln([128, 256], BF16)
    gz = al2([128, 2, 512], BF16)
    yo = al2([128, 2, 512], BF16)
    Rof, Rob, Ros, Rsqq, Ry1, Rss, Rytm = (Rn() for _ in range(7))
    Rgz, Ryo = R_(), R_()
    v4 = lambda ap: ap.rearrange("p (a b) -> p a b", a=4)

    def c0(i):
        gi, t = i // 4, i % 4
        g2 = gi % 2
        b = i % NB
        if t == 0:
            P.dma("sync", gz[g2][:], k.FT[2:4, :, gi * 512:(gi + 1) * 512].rearrange("c p l -> p c l"), writes=[Rgz[g2]])
        P.dma("sync", of[b][:], k.OF[i * 128:(i + 1) * 128, :], writes=[Rof[b]])
        P.dma("sync", ob[b][:], k.OB[i * 128:(i + 1) * 128, :], writes=[Rob[b]])
        P.op("vector", TT(osum[b][:], of[b][:], ob[b][:], ALU.add), [Rof[b], Rob[b]], [Ros[b]])
        P.op("gpsimd", TT(sqq[b][:], osum[b][:], osum[b][:], ALU.mult), [Ros[b]], [Rsqq[b]])

    def c1(i):
        b = i % NB
        P.op("vector", lambda e, o_=ss[b][:, 0:4], i_=v4(sqq[b][:]): e.reduce_sum(o_, i_, AX.X), [Rsqq[b]], [Rss[b]])
        P.op("scalar", ACTF(ss[b][:, 4:8], ss[b][:, 0:4], AF.Ln, bias=EPS, scale=1.0 / 64), [Rss[b]], [Rss[b]])
        P.op("scalar", ACTF(ss[b][:, 4:8], ss[b][:, 4:8], AF.Exp, scale=-0.5), [Rss[b]], [Rss[b]])

    def c2(i):
        b = i % NB
        P.op("vector", TT(v4(y1[b][:]), v4(osum[b][:]), ss[b][:, 4:8].unsqueeze(2).to_broadcast([128, 4, 64]), ALU.mult),
             [Ros[b], Rss[b]], [Ry1[b]])
        P.op("gpsimd", TT(v4(ytm[b][:]), v4(y1[b][:]), gng[:, :].unsqueeze(1).to_broadcast([128, 4, 64]), ALU.mult),
             [Ry1[b], Rgng], [Rytm[b]])

    def c3(i):
        gi, t = i // 4, i % 4
        g2 = gi % 2
        b = i % NB
        bk = 4 + i % 2
        pT = pb[bk][:, 0:128].bitcast(BF16)
        for j in range(2):
            P.op("tensor", TR(pT[:, j * 128:(j + 1) * 128], ytm[b][:, j * 128:(j + 1) * 128], k.ident[:]), [Rytm[b], k.Rconst], [PB[bk]])
        P.op("vector", TT(yo[g2][:, :, t * 128:(t + 1) * 128], pT.rearrange("p (c t) -> p c t", c=2),
                          gz[g2][:, :, t * 128:(t + 1) * 128], ALU.mult), [PB[bk], Rgz[g2]], [Ryo[g2]])
        if t == 3:
            for j in range(2):
                P.dma("gpsimd", k.YT[2 + j, :, gi * 512:(gi + 1) * 512], yo[g2][:, j, :], reads=[Ryo[g2]])

    pipeline(L // 128, [c0, c1, c2, c3])
```

```python
import math
import numpy as np
import ml_dtypes
from contextlib import ExitStack
import concourse.bass as bass
import concourse.mybir as mybir
from concourse.bass_utils import run_bass_kernel_spmd

F32 = mybir.dt.float32
BF16 = mybir.dt.bfloat16
F32R = mybir.dt.float32r
AF = mybir.ActivationFunctionType
ALU = mybir.AluOpType
AX = mybir.AxisListType

D_MODEL = 1024
DIN = 3088
DEPTH = 2
SEQ = 8192
NMEM = 256
EPS = 1e-6
NEG = -30000.0

ENGS = ("tensor", "vector", "scalar", "gpsimd", "sync")


class Res:
    __slots__ = ("name", "last_w", "readers")

    def __init__(self, name=""):
        self.name = name
        self.last_w = None
        self.readers = []


class Op:
    __slots__ = ("eng", "fn", "deps", "is_dma", "sig", "dma_slot", "needs")

    def __init__(self, eng, fn, is_dma):
        self.eng = eng
        self.fn = fn
        self.deps = []
        self.is_dma = is_dma
        self.sig = None
        self.needs = False
        self.dma_slot = None


class Prog:
    NDMA = 12

    def __init__(self, nc):
        self.nc = nc
        self.ops = {e: [] for e in ENGS}
        self.pending = None
        self.pending_done = set()

    def op(self, eng, fn, reads=(), writes=(), dma=False):
        o = Op(eng, fn, dma)
        deps = []
        for r in reads:
            if r.last_w is not None:
                deps.append(r.last_w)
        for w in writes:
            if w.last_w is not None:
                deps.append(w.last_w)
            deps.extend(w.readers)
        if self.pending is not None and eng not in self.pending_done:
            deps.extend(self.pending)
            self.pending_done.add(eng)
        seen = set()
        for d in deps:
            if id(d) in seen:
                continue
            seen.add(id(d))
            if d.eng == "tensor" and eng == "tensor" and not d.is_dma and not dma:
                continue
            o.deps.append(d)
            d.needs = True
        for r in reads:
            r.readers.append(o)
        for w in writes:
            w.last_w = o
            w.readers = []
        self.ops[eng].append(o)
        return o

    def dma(self, eng, out, in_, reads=(), writes=(), **kw):
        return self.op(eng, lambda e: e.dma_start(out=out, in_=in_, **kw), reads, writes, dma=True)

    def barrier(self):
        deps = []
        for e in ENGS:
            ops = self.ops[e]
            for o in reversed(ops):
                if not o.is_dma:
                    deps.append(o)
                    o.needs = True
                    break
            deps.extend([o for o in ops if o.is_dma][-self.NDMA:])
        self.pending = deps
        self.pending_done = set()

    def emit(self):
        nc = self.nc
        with ExitStack() as st:
            sems = {e: st.enter_context(nc.semaphore("s_" + e)) for e in ENGS}
            dsems = {e: [st.enter_context(nc.semaphore("d_%s_%d" % (e, i))) for i in range(self.NDMA)]
                     for e in ("sync", "gpsimd", "scalar")}
            for e in ENGS:
                cnt = 0
                dcnt = 0
                for o in self.ops[e]:
                    if o.is_dma:
                        o.dma_slot = dcnt
                        o.sig = (dsems[e][dcnt % self.NDMA], 16 * (dcnt // self.NDMA + 1))
                        dcnt += 1
                    elif o.needs:
                        cnt += 1
                        o.sig = (sems[e], cnt)
            block = st.enter_context(nc.Block())
            prog = self

            def run_engine(e, eng):
                waited = {}
                dma_list = [o for o in prog.ops[e] if o.is_dma]

                def wait(sem, val):
                    k = id(sem)
                    if waited.get(k, 0) >= val:
                        return
                    waited[k] = val
                    eng.wait_ge(sem, val)

                for o in prog.ops[e]:
                    for d in o.deps:
                        wait(*d.sig)
                    if o.is_dma and o.dma_slot >= prog.NDMA:
                        wait(*dma_list[o.dma_slot - prog.NDMA].sig)
                    ins = o.fn(eng)
                    if o.is_dma:
                        ins.then_inc(o.sig[0], 16)
                    elif o.needs:
                        ins.then_inc(o.sig[0], 1)
                for o in dma_list[-prog.NDMA:]:
                    wait(*o.sig)

            @block.tensor
            def _(eng):
                run_engine("tensor", eng)

            @block.vector
            def _(eng):
                run_engine("vector", eng)

            @block.scalar
            def _(eng):
                run_engine("scalar", eng)

            @block.gpsimd
            def _(eng):
                run_engine("gpsimd", eng)

            @block.sync
            def _(eng):
                run_engine("sync", eng)


def MM(out, lhsT, rhs, start=True, stop=True):
    return lambda e: e.matmul(out, lhsT, rhs, start=start, stop=stop)


def TR(out, in_, ident):
    return lambda e: e.transpose(out, in_, ident)


def ACTF(out, in_, func, **kw):
    return lambda e: e.activation(out, in_, func, **kw)


def TS(out, in0, s1, s2, op0, op1=None):
    if op1 is None:
        return lambda e: e.tensor_scalar(out, in0, s1, s2, op0)
    return lambda e: e.tensor_scalar(out, in0, s1, s2, op0, op1)


def TT(out, in0, in1, op):
    return lambda e: e.tensor_tensor(out, in0, in1, op)


def STT(out, in0, scalar, in1, op0, op1):
    return lambda e: e.scalar_tensor_tensor(out, in0, scalar, in1, op0, op1)


def CP(out, in_):
    return lambda e: e.tensor_copy(out, in_)


def MSET(ap, val):
    return lambda e: e.memset(ap, val)


def RECIP(out, in_):
    return lambda e: e.reciprocal(out, in_)


_uid = [0]


def _dsize(dt):
    return 4 if dt in (F32, F32R) else 2


class Arena:
    def __init__(self, nc, base, limit):
        self.nc = nc
        self.off = base
        self.base = base
        self.limit = limit

    def alloc(self, shape, dt):
        n = _dsize(dt)
        for s in shape[1:]:
            n *= s
        n = (n + 63) // 64 * 64
        _uid[0] += 1
        h = self.nc.alloc_sbuf_tensor_at("t%d" % _uid[0], list(shape), dt, offset=self.off)
        self.off += n
        assert self.off <= self.limit, ("SBUF arena overflow", self.off, self.limit)
        return h

    def reset(self, to=None):
        self.off = self.base if to is None else to


FM_CHUNKS = ([(256 + 128 * j, True, 0 + j) for j in range(2)] + [(512 + 128 * j, False, 8 + j) for j in range(6)] +
             [(1280 + 128 * j, True, 2 + j) for j in range(2)] + [(1552 + 128 * j, False, 14 + j) for j in range(4)] +
             [(2320 + 128 * j, True, 4 + j) for j in range(2)] + [(2576 + 128 * j, False, 18 + j) for j in range(2)] +
             [(2832 + 128 * j, True, 6 + j) for j in range(2)])


def pipeline(n, stages):
    for t in range(n + len(stages) - 1):
        for si, fn in enumerate(stages):
            i = t - si
            if 0 <= i < n:
                fn(i)


class K:
    pass


def build_program(L=SEQ, depth=DEPTH, debug=False, stop_after=None):
    nc = bass.Bass("TRN2", target_bir_lowering=False)
    P = Prog(nc)
    k = K()
    k.nc, k.P, k.L, k.depth = nc, P, L, depth
    N1 = L // 64
    k.N1 = N1

    def din(name, shape, dt=F32):
        return nc.dram_tensor(name, list(shape), dt, kind="ExternalInput").ap()

    skind = "ExternalOutput" if debug else "Internal"

    def dscr(name, shape, dt):
        return nc.dram_tensor(name, list(shape), dt, kind=skind).ap()

    k.x = din("x", [L, D_MODEL])
    k.mem = din("mem", [NMEM, D_MODEL])
    k.pre_g = din("pre_norm_g", [depth, D_MODEL])
    k.post_g = din("post_norm_g", [depth, D_MODEL])
    k.w_in = din("w_in", [depth, D_MODEL, DIN])
    k.w_fnet = din("w_fnet", [depth, 256, 256])
    k.conv_w = din("gdn_conv_w", [depth, 5, 768])
    k.a_log = din("gdn_a_log", [depth, 8])
    k.dt_bias = din("gdn_dt_bias", [depth, 8])
    k.gdn_g = din("gdn_norm_g", [depth, 64])
    k.rpb = din("na_rpb", [depth, 4, 15, 31])
    k.mem_g = din("mem_norm_g", [depth, D_MODEL])
    k.w_kv = din("w_mem_kv", [depth, D_MODEL, 512])
    k.w_out = din("w_out", [depth, D_MODEL, D_MODEL])
    k.c_ident = din("c_ident", [128, 128], BF16)
    k.c_identf = din("c_identf", [128, 128], F32)
    k.c_dft1 = din("c_dft1", [N1, 2 * N1], BF16)
    k.c_tw = din("c_tw", [N1, 3, 64], F32)
    k.c_dft2 = din("c_dft2", [64, 2, 128], BF16)
    k.c_bd64 = din("c_bd64", [128, 2, 128], BF16)
    k.c_gmask = din("c_gmask", [128, 12, 128], F32)
    k.c_bones = din("c_bones", [128, 128], BF16)
    k.y = nc.dram_tensor("y", [L, D_MODEL], F32, kind="ExternalOutput").ap()
    k.X1 = dscr("s_x1", [L, D_MODEL], F32)
    k.FT = dscr("s_ft", [20, 128, L], BF16)
    k.U = dscr("s_u", [L, 256], BF16)
    k.CV = dscr("s_cv", [L, 256], BF16)
    k.G = dscr("s_g", [L, 16], F32)
    k.YT = dscr("s_yt", [8, 128, L], BF16)
    k.Bd = dscr("s_bd", [N1, 64, 2, 256], BF16)
    k.QKn = dscr("s_qkn", [4, 128, L], BF16)
    k.QKVt = dscr("s_qkvt", [L, 768], BF16)
    k.OF = dscr("s_of", [L, 256], F32)
    k.OB = dscr("s_ob", [L, 256], F32)
    k.NAB = dscr("s_nab", [4, 8, 64, 512], F32)

    k.pball = nc.alloc_psum_tensor("pball", [128, 4096], F32)
    k.pb = [k.pball[:, i * 512:(i + 1) * 512] for i in range(8)]
    k.PB = [Res("pb%d" % i) for i in range(8)]

    SB_BASE = 16640
    SB_LIMIT = SB_BASE + 196608
    pa = Arena(nc, SB_BASE, SB_LIMIT)
    k.ident = pa.alloc([128, 128], BF16)
    k.identf = pa.alloc([128, 128], F32)
    k.bones = pa.alloc([128, 128], BF16)
    k.ones_bf = pa.alloc([128, 64], BF16)
    k.idr = pa.alloc([64, 64], F32R)
    k.negt = pa.alloc([64, 512], F32)
    k.Rconst = Res("const")
    P.dma("sync", k.ident[:], k.c_ident, writes=[k.Rconst])
    P.dma("sync", k.identf[:], k.c_identf, writes=[k.Rconst])
    P.dma("sync", k.bones[:], k.c_bones, writes=[k.Rconst])
    P.op("vector", MSET(k.ones_bf[:], 1.0), writes=[k.Rconst])
    P.op("vector", MSET(k.negt[:], NEG), writes=[k.Rconst])
    P.op("vector", CP(k.idr[:], k.identf[0:64, 0:64]), [k.Rconst], [k.Rconst])
    k.arena = Arena(nc, pa.off, SB_LIMIT)

    for lay in range(depth):
        xin = k.x if lay == 0 else k.X1
        yout = k.y if lay == depth - 1 else k.X1
        phase1(k, lay, xin)
        P.barrier()
        if stop_after == "p1":
            break
        phase_mem(k, lay)
        P.barrier()
        if stop_after == "mem":
            break
        phase_fnet(k, lay)
        P.barrier()
        if stop_after == "fnet":
            break
        phase_na(k, lay)
        P.barrier()
        if stop_after == "na":
            break
        phase_gdn(k, lay)
        P.barrier()
        if stop_after == "gdn":
            break
        phase_out(k, lay, xin, yout)
        P.barrier()
    P.emit()
    return nc


def rms_tile(k, xb, Rxb, junk, Rjunk, st, Rst, xs, Rxs, width=D_MODEL):
    P = k.P
    P.op("scalar", ACTF(junk[:], xb[:], AF.Square, accum_out=st[:, 0:1]), [Rxb], [Rjunk, Rst])
    P.op("scalar", ACTF(st[:, 1:2], st[:, 0:1], AF.Ln, bias=EPS, scale=1.0 / width), [Rst], [Rst])
    P.op("scalar", ACTF(st[:, 2:3], st[:, 1:2], AF.Exp, scale=-0.5), [Rst], [Rst])
    P.op("vector", TS(xs[:], xb[:], st[:, 2:3], None, ALU.mult), [Rxb, Rst], [Rxs])


def phase1(k, lay, xin):
    nc, P, L = k.nc, k.P, k.L
    A = k.arena
    A.reset()
    pb, PB = k.pb, k.PB
    wst = [A.alloc([128, DIN], F32) for _ in range(4)]
    Rwst = [Res() for _ in range(4)]
    wbf = A.alloc([128, 8, DIN], BF16)
    Rwbf = Res()
    gpre = A.alloc([128, 8], F32)
    Rg = Res()
    P.dma("sync", gpre[:], k.pre_g[lay].rearrange("(k p) -> p k", p=128), writes=[Rg], allow_slow_non_contiguous=True)
    def wload(kk):
        P.dma("sync" if kk % 2 == 0 else "scalar", wst[kk % 4][:], k.w_in[lay, kk * 128:(kk + 1) * 128, :], writes=[Rwst[kk % 4]])

    for kk in range(4):
        wload(kk)
    for kk in range(8):
        if kk % 2 == 0:
            P.op("vector", TS(wbf[:, kk, :], wst[kk % 4][:], gpre[:, kk:kk + 1], None, ALU.mult), [Rwst[kk % 4], Rg], [Rwbf])
        else:
            P.op("scalar", ACTF(wbf[:, kk, :], wst[kk % 4][:], AF.Copy, scale=gpre[:, kk:kk + 1]), [Rwst[kk % 4], Rg], [Rwbf])
        if kk + 4 < 8:
            wload(kk + 4)
    dtb = A.alloc([128, 8], F32)
    negA = A.alloc([128, 8], F32)
    Rgc = Res()
    P.dma("sync", dtb[:], k.dt_bias[lay].partition_broadcast(128), writes=[Rgc])
    P.dma("sync", negA[:], k.a_log[lay].partition_broadcast(128), writes=[Rgc])
    P.op("scalar", ACTF(negA[:], negA[:], AF.Exp), [Rgc], [Rgc])
    P.op("vector", TS(negA[:], negA[:], -1.0, None, ALU.mult), [Rgc], [Rgc])

    NXB = 8
    xt = [A.alloc([128, D_MODEL], F32) for _ in range(NXB)]
    Rxt = [Res() for _ in range(NXB)]
    junk = A.alloc([128, D_MODEL], F32)
    Rjunk = Res()
    stt = [A.alloc([128, 4], F32) for _ in range(NXB)]
    Rstt = [Res() for _ in range(NXB)]
    xs = [A.alloc([128, D_MODEL], BF16) for _ in range(NXB)]
    Rxs = [Res() for _ in range(NXB)]
    hxT = [A.alloc([128, 8, 512], BF16) for _ in range(3)]
    RhxT = [Res() for _ in range(3)]
    fo = [A.alloc([128, 512], BF16) for _ in range(4)]
    Rfo = [Res() for _ in range(4)]
    tmo = [A.alloc([128, 512], BF16) for _ in range(2)]
    Rtmo = [Res(), Res()]
    gw = A.alloc([128, 4, 16], F32)
    gout = A.alloc([128, 4, 16], F32)
    Rgw, Rgout = Res(), Res()
    pT = pb[7][:, :].bitcast(BF16)
    ngroups = L // 512

    def prepA(gi):
        for t in range(4):
            tok0 = gi * 512 + t * 128
            b = (gi * 4 + t) % NXB
            P.dma("sync", xt[b][:], xin[tok0:tok0 + 128, :], writes=[Rxt[b]])
            rms_tile(k, xt[b], Rxt[b], junk, Rjunk, stt[b], Rstt[b], xs[b], Rxs[b])

    def prepB(gi):
        hb = gi % 3
        for t in range(4):
            b = (gi * 4 + t) % NXB
            for kk in range(8):
                P.op("tensor", TR(pT[:, kk * 128:(kk + 1) * 128], xs[b][:, kk * 128:(kk + 1) * 128], k.ident[:]),
                     [Rxs[b], k.Rconst], [PB[7]])
            P.op("vector", CP(hxT[hb][:, :, t * 128:(t + 1) * 128], pT.rearrange("p (k t) -> p k t", k=8)),
                 [PB[7]], [RhxT[hb]])

    def mmG(gi):
        hb = gi % 3
        for ci, (col0, is_silu, dst) in enumerate(FM_CHUNKS):
            bk = ci % 4
            for kk in range(8):
                P.op("tensor", MM(pb[bk][:, :], wbf[:, kk, col0:col0 + 128], hxT[hb][:, kk, :], kk == 0, kk == 7),
                     [Rwbf, RhxT[hb]], [PB[bk]])
            if is_silu:
                P.op("scalar", ACTF(fo[bk][:], pb[bk][:, :], AF.Silu), [PB[bk]], [Rfo[bk]])
            else:
                P.op("vector", CP(fo[bk][:], pb[bk][:, :]), [PB[bk]], [Rfo[bk]])
            P.dma("gpsimd", k.FT[dst, :, gi * 512:(gi + 1) * 512], fo[bk][:], reads=[Rfo[bk]])
        for t in range(4):
            tok0 = gi * 512 + t * 128
            bk = 4 + t % 2
            for (oap, c0, c1, bres) in ((pb[bk][:, 0:256], 0, 256, PB[bk]), (pb[bk][:, 256:512], 2064, 2320, PB[bk]),
                                        (pb[6][:, t * 16:(t + 1) * 16], 1536, 1552, PB[6])):
                for kk in range(8):
                    lt = hxT[hb][:, kk, t * 128:(t + 1) * 128]
                    P.op("tensor", MM(oap, lt, wbf[:, kk, c0:c1], kk == 0, kk == 7), [Rwbf, RhxT[hb]], [bres])
            ob = tmo[t % 2]
            P.op("vector", CP(ob[:], pb[bk][:, :]), [PB[bk]], [Rtmo[t % 2]])
            P.dma("gpsimd", k.U[tok0:tok0 + 128, :], ob[:, 0:256], reads=[Rtmo[t % 2]])
            P.dma("gpsimd", k.CV[tok0:tok0 + 128, :], ob[:, 256:512], reads=[Rtmo[t % 2]])
        pg = pb[6][:, 0:64].rearrange("p (t c) -> p t c", t=4)
        P.op("vector", TT(gw[:, :, 0:8], pg[:, :, 0:8], dtb[:, :].unsqueeze(1).to_broadcast([128, 4, 8]), ALU.add),
             [PB[6], Rgc], [Rgw])
        P.op("scalar", ACTF(gw[:, :, 0:8], gw[:, :, 0:8], AF.Exp), [Rgw], [Rgw])
        P.op("scalar", ACTF(gw[:, :, 0:8], gw[:, :, 0:8], AF.Ln, bias=1.0), [Rgw], [Rgw])
        P.op("scalar", ACTF(gw[:, :, 8:16], pg[:, :, 8:16], AF.Exp, scale=-1.0), [PB[6], Rgw], [Rgw])
        P.op("vector", TT(gout[:, :, 0:8], gw[:, :, 0:8], negA[:, :].unsqueeze(1).to_broadcast([128, 4, 8]), ALU.mult),
             [Rgw, Rgc], [Rgout])
        P.op("vector", TS(gw[:, :, 8:16], gw[:, :, 8:16], 1.0, None, ALU.add), [Rgw], [Rgw])
        P.op("vector", RECIP(gout[:, :, 8:16], gw[:, :, 8:16]), [Rgw], [Rgout])
        P.dma("gpsimd", k.G[gi * 512:(gi + 1) * 512, :].rearrange("(t p) c -> p t c", p=128), gout[:], reads=[Rgout])

    pipeline(ngroups, [prepA, prepB, mmG])


def phase_out(k, lay, xin, yout):
    nc, P, L = k.nc, k.P, k.L
    A = k.arena
    A.reset()
    pb, PB = k.pb, k.PB
    wst = [A.alloc([128, D_MODEL], F32) for _ in range(2)]
    Rwst = [Res(), Res()]
    wob = A.alloc([128, 8, D_MODEL], BF16)
    Rwob = Res()
    for kk in range(8):
        P.dma("sync", wst[kk % 2][:], k.w_out[lay, kk * 128:(kk + 1) * 128, :], writes=[Rwst[kk % 2]])
        if kk % 2 == 0:
            P.op("vector", CP(wob[:, kk, :], wst[kk % 2][:]), [Rwst[kk % 2]], [Rwob])
        else:
            P.op("scalar", ACTF(wob[:, kk, :], wst[kk % 2][:], AF.Copy), [Rwst[kk % 2]], [Rwob])
    gpost = A.alloc([128, D_MODEL], F32)
    Rgp = Res()
    P.dma("sync", gpost[:], k.post_g[lay].partition_broadcast(128), writes=[Rgp])
    NB = 4
    ycat = [A.alloc([128, 8, 512], BF16) for _ in range(2)]
    Ryc = [Res(), Res()]
    osb = [A.alloc([128, D_MODEL], F32) for _ in range(NB)]
    Ros = [Res() for _ in range(NB)]
    xr = [A.alloc([128, D_MODEL], F32) for _ in range(NB)]
    Rxr = [Res() for _ in range(NB)]
    junk = A.alloc([128, 512], BF16)
    Rjunk = Res()
    stt = [A.alloc([128, 8], F32) for _ in range(NB)]
    Rst = [Res() for _ in range(NB)]

    def s0(i):
        gi, t = i // 4, i % 4
        yb = gi % 2
        if i == 0:
            P.dma("sync", ycat[0][:], k.YT[:, :, 0:512].rearrange("c p l -> p c l"), writes=[Ryc[0]])
        if t == 0 and (gi + 1) * 512 < L:
            P.dma("sync", ycat[(gi + 1) % 2][:], k.YT[:, :, (gi + 1) * 512:(gi + 2) * 512].rearrange("c p l -> p c l"),
                  writes=[Ryc[(gi + 1) % 2]])
        b = i % NB
        P.dma("sync", xr[b][:], xin[i * 128:(i + 1) * 128, :], writes=[Rxr[b]])
        for half in range(2):
            bk = half + 2 * (i % 2)
            for kk in range(8):
                P.op("tensor", MM(pb[bk][:, :], ycat[yb][:, kk, t * 128:(t + 1) * 128],
                                  wob[:, kk, half * 512:(half + 1) * 512], kk == 0, kk == 7), [Ryc[yb], Rwob], [PB[bk]])
            P.op("scalar", ACTF(junk[:], pb[bk][:, :], AF.Square, accum_out=stt[b][:, half:half + 1]), [PB[bk]], [Rjunk, Rst[b]])

    def s1(i):
        b = i % NB
        st = stt[b]
        P.op("vector", TT(st[:, 2:3], st[:, 0:1], st[:, 1:2], ALU.add), [Rst[b]], [Rst[b]])
        P.op("scalar", ACTF(st[:, 3:4], st[:, 2:3], AF.Ln, bias=EPS, scale=1.0 / D_MODEL), [Rst[b]], [Rst[b]])
        P.op("scalar", ACTF(st[:, 4:5], st[:, 3:4], AF.Exp, scale=-0.5), [Rst[b]], [Rst[b]])
        for half in range(2):
            bk = half + 2 * (i % 2)
            P.op("scalar", ACTF(osb[b][:, half * 512:(half + 1) * 512], pb[bk][:, :], AF.Copy, scale=st[:, 4:5]),
                 [PB[bk], Rst[b]], [Ros[b]])

    def s2(i):
        b = i % NB
        P.op("vector", TT(osb[b][:], osb[b][:], gpost[:], ALU.mult), [Ros[b], Rgp], [Ros[b]])
        P.op("gpsimd", TT(osb[b][:], osb[b][:], xr[b][:], ALU.add), [Ros[b], Rxr[b]], [Ros[b]])
        P.dma("sync", yout[i * 128:(i + 1) * 128, :], osb[b][:], reads=[Ros[b]])

    pipeline(L // 128, [s0, s1, s2])


def phase_mem(k, lay):
    nc, P, L = k.nc, k.P, k.L
    A = k.arena
    A.reset()
    pb, PB = k.pb, k.PB
    wst = [A.alloc([128, 512], F32) for _ in range(2)]
    Rwst = [Res(), Res()]
    wkv = A.alloc([128, 8, 512], BF16)
    Rwkv = Res()
    gm = A.alloc([128, 8], F32)
    Rgm = Res()
    P.dma("sync", gm[:], k.mem_g[lay].rearrange("(k p) -> p k", p=128), writes=[Rgm], allow_slow_non_contiguous=True)
    for kk in range(8):
        P.dma("sync", wst[kk % 2][:], k.w_kv[lay, kk * 128:(kk + 1) * 128, :], writes=[Rwst[kk % 2]])
        P.op("vector", TS(wkv[:, kk, :], wst[kk % 2][:], gm[:, kk:kk + 1], None, ALU.mult), [Rwst[kk % 2], Rgm], [Rwkv])
    xt = A.alloc([128, D_MODEL], F32)
    junk = A.alloc([128, D_MODEL], F32)
    st = A.alloc([128, 4], F32)
    xs = A.alloc([128, D_MODEL], BF16)
    Rxt, Rjunk, Rst, Rxs = Res(), Res(), Res(), Res()
    memT = A.alloc([128, 8, 256], BF16)
    RmemT = Res()
    pT = pb[7][:, :].bitcast(BF16)
    for t in range(2):
        P.dma("sync", xt[:], k.mem[t * 128:(t + 1) * 128, :], writes=[Rxt])
        rms_tile(k, xt, Rxt, junk, Rjunk, st, Rst, xs, Rxs)
        for kk in range(8):
            P.op("tensor", TR(pT[:, kk * 128:(kk + 1) * 128], xs[:, kk * 128:(kk + 1) * 128], k.ident[:]), [Rxs, k.Rconst], [PB[7]])
        P.op("vector", CP(memT[:, :, t * 128:(t + 1) * 128], pT.rearrange("p (k t) -> p k t", k=8)), [PB[7]], [RmemT])
    kmT = A.alloc([64, 4, 256], BF16)
    vm = A.alloc([128, 2, 256], BF16)
    Rkm, Rvm = Res(), Res()
    for h in range(4):
        bk = h % 2
        for kk in range(8):
            P.op("tensor", MM(pb[bk][0:64, 0:256], wkv[:, kk, h * 64:(h + 1) * 64], memT[:, kk, :], kk == 0, kk == 7),
                 [Rwkv, RmemT], [PB[bk]])
        P.op("vector", CP(kmT[:, h, :], pb[bk][0:64, 0:256]), [PB[bk]], [Rkm])
    for mc in range(2):
        bk = 2 + mc
        for kk in range(8):
            P.op("tensor", MM(pb[bk][:, 0:256], memT[:, kk, mc * 128:(mc + 1) * 128], wkv[:, kk, 256:512], kk == 0, kk == 7),
                 [Rwkv, RmemT], [PB[bk]])
        P.op("vector", CP(vm[:, mc, :], pb[bk][:, 0:256]), [PB[bk]], [Rvm])
    NB = 6
    mk = lambda shape, dt: [A.alloc(shape, dt) for _ in range(NB)]
    rs = lambda: [Res() for _ in range(NB)]
    qT, gz = mk([64, 512], BF16), mk([64, 512], BF16)
    PT = mk([128, 2, 512], BF16)
    rden, y1 = mk([64, 512], F32), mk([64, 512], F32)
    yo = mk([64, 512], BF16)
    Rq, Rgz, RPT, Rrd, Ry1, Ryo = rs(), rs(), rs(), rs(), rs(), rs()

    def geo(i):
        gi, h = i // 4, i % 4
        return slice(gi * 512, (gi + 1) * 512), h, (h % 2) * 64, i % NB

    def mL(i):
        sl, h, p0, b = geo(i)
        P.dma("sync", qT[b][:], k.FT[18 + h // 2, p0:p0 + 64, sl], writes=[Rq[b]])
        P.dma("sync", gz[b][:], k.FT[6 + h // 2, p0:p0 + 64, sl], writes=[Rgz[b]])

    def m0(i):
        sl, h, p0, b = geo(i)
        for mc in range(2):
            bk = 4 * (i % 2) + mc
            P.op("tensor", MM(pb[bk][:, :], kmT[:, h, mc * 128:(mc + 1) * 128], qT[b][:]), [Rkm, Rq[b]], [PB[bk]])
            P.op("scalar", ACTF(PT[b][:, mc, :], pb[bk][:, :], AF.Exp, scale=0.125), [PB[bk]], [RPT[b]])

    def m1(i):
        sl, h, p0, b = geo(i)
        bo, bd = 4 * (i % 2) + 2, 4 * (i % 2) + 3
        for mc in range(2):
            P.op("tensor", MM(pb[bo][0:64, :], vm[:, mc, h * 64:(h + 1) * 64], PT[b][:, mc, :], mc == 0, mc == 1),
                 [Rvm, RPT[b]], [PB[bo]])
        for mc in range(2):
            P.op("tensor", MM(pb[bd][0:64, :], k.ones_bf[:, :], PT[b][:, mc, :], mc == 0, mc == 1),
                 [k.Rconst, RPT[b]], [PB[bd]])
        P.op("scalar", ACTF(rden[b][:], pb[bd][0:64, :], AF.Ln), [PB[bd]], [Rrd[b]])
        P.op("scalar", ACTF(rden[b][:], rden[b][:], AF.Exp, scale=-1.0), [Rrd[b]], [Rrd[b]])
        P.op("vector", TT(y1[b][:], pb[bo][0:64, :], rden[b][:], ALU.mult), [PB[bo], Rrd[b]], [Ry1[b]])

    def m2(i):
        sl, h, p0, b = geo(i)
        P.op("gpsimd", TT(yo[b][:], y1[b][:], gz[b][:], ALU.mult), [Ry1[b], Rgz[b]], [Ryo[b]])
        P.dma("gpsimd", k.YT[6 + h // 2, p0:p0 + 64, sl], yo[b][:], reads=[Ryo[b]])

    pipeline((L // 512) * 4, [mL, m0, m1, m2])

def make_consts(L):
    N1 = L // 64
    bf = ml_dtypes.bfloat16
    c = {}
    c["c_ident"] = np.eye(128, dtype=np.float32).astype(bf)
    c["c_identf"] = np.eye(128, dtype=np.float32)
    l1 = np.arange(N1)
    ang1 = 2 * np.pi * np.outer(l1, l1) / N1
    c["c_dft1"] = np.concatenate([np.cos(ang1), np.sin(ang1)], axis=1).astype(np.float32).astype(bf)
    angt = 2 * np.pi * np.outer(np.arange(N1), np.arange(64)) / L
    sc = 1.0 / math.sqrt(L * 64.0)
    c["c_tw"] = np.stack([np.cos(angt) * sc, -np.sin(angt) * sc, -np.cos(angt) * sc], axis=1).astype(np.float32)
    a2 = 2 * np.pi * np.outer(np.arange(64), np.arange(64)) / 64
    C2, S2 = np.cos(a2), np.sin(a2)
    c["c_dft2"] = np.stack([np.concatenate([C2, -S2], 1), np.concatenate([S2, C2], 1)], axis=1).astype(np.float32).astype(bf)
    bdc = np.zeros((128, 128)); bds = np.zeros((128, 128))
    for b in range(2):
        bdc[b * 64:(b + 1) * 64, b * 64:(b + 1) * 64] = C2
        bds[b * 64:(b + 1) * 64, b * 64:(b + 1) * 64] = S2
    c["c_bd64"] = np.stack([bdc, bds], axis=1).astype(np.float32).astype(bf)
    j = np.arange(128)[:, None]
    s = np.arange(128)[None, :]
    same = (j // 64) == (s // 64)
    gm = np.zeros((128, 12, 128), np.float32)
    gm[:, 0] = (j > s) & same
    gm[:, 1] = (j < s) & same
    gm[:, 2] = (j <= s) & same
    gm[:, 3] = (j >= s) & same
    gm[:, 4] = (j >= s) & same
    gm[:, 5] = (j <= s) & same
    gm[:, 6] = (j > s) & same
    gm[:, 7] = (j < s) & same
    gm[:, 8] = (j > s) & same
    gm[:, 9] = (j < s) & same
    gm[:, 10] = (j < 64) & (s >= 0)
    gm[:, 11] = (j >= 64) & (s >= 0)
    c["c_gmask"] = gm
    bo = np.zeros((128, 128), np.float32)
    bo[:64, :64] = 1
    bo[64:, 64:] = 1
    c["c_bones"] = bo.astype(bf)
    return c


_prog_cache = {}


def run_cores(per_core_inputs, L, depth, debug=False, stop_after=None):
    key = (L, depth, debug, stop_after)
    nc = build_program(L, depth, debug, stop_after)
    consts = make_consts(L)
    in_maps = []
    for d in per_core_inputs:
        m = dict(consts)
        m.update(d)
        in_maps.append(m)
    res = run_bass_kernel_spmd(nc, in_maps, core_ids=list(range(len(in_maps))))
    return res.results


def kernel(x_prompt, x_sample, mem_prompt, mem_sample, pre_norm_g, post_norm_g, w_in, w_fnet, gdn_conv_w,
           gdn_a_log, gdn_dt_bias, gdn_norm_g, na_rpb, mem_norm_g, w_mem_kv, w_out):
    f = lambda a: np.ascontiguousarray(np.asarray(a, dtype=np.float32))
    xs = [f(x_prompt[i]) for i in range(4)] + [f(x_sample[i]) for i in range(2)]
    ms = [f(mem_prompt[i]) for i in range(4)] + [f(mem_sample[i]) for i in range(2)]
    shared = dict(pre_norm_g=f(pre_norm_g), post_norm_g=f(post_norm_g), w_in=f(w_in), w_fnet=f(w_fnet),
                  gdn_conv_w=f(gdn_conv_w), gdn_a_log=f(gdn_a_log).reshape(DEPTH, 8),
                  gdn_dt_bias=f(gdn_dt_bias).reshape(DEPTH, 8), gdn_norm_g=f(gdn_norm_g), na_rpb=f(na_rpb),
                  mem_norm_g=f(mem_norm_g), w_mem_kv=f(w_mem_kv), w_out=f(w_out))
    per_core = []
    for c in range(8):
        s = c if c < 6 else c - 6
        d = dict(shared)
        d["x"] = xs[s]
        d["mem"] = ms[s]
        per_core.append(d)
    res = run_cores(per_core, SEQ, DEPTH)
    y_prompt = np.stack([res[i]["y"] for i in range(4)], axis=0).astype(np.float32)
    y_sample = np.stack([res[4 + i]["y"] for i in range(2)], axis=0).astype(np.float32)
    return (y_prompt, y_sample)


def phase_fnet(k, lay):
    nc, P, L, N1 = k.nc, k.P, k.L, k.N1
    A = k.arena
    A.reset()
    pb, PB = k.pb, k.PB
    na_table_build(k, lay)
    wf = A.alloc([128, 2, 256], F32)
    wfb = A.alloc([128, 2, 256], BF16)
    bd = A.alloc([128, 2, 128], BF16)
    wmix = A.alloc([128, 2, 2, 256], BF16)
    dft2 = A.alloc([64, 2, 128], BF16)
    Rw, Rmix = Res(), Res()
    P.dma("sync", wf[:], k.w_fnet[lay].rearrange("(c p) o -> p c o", p=128), writes=[Rw])
    P.dma("sync", bd[:], k.c_bd64, writes=[Rw])
    P.dma("sync", dft2[:], k.c_dft2, writes=[Rw])
    P.op("vector", CP(wfb[:], wf[:]), [Rw], [Rw])
    for cc in range(2):
        for ri in range(2):
            bk = cc * 2 + ri
            P.op("tensor", MM(pb[bk][:, 0:256], bd[:, ri, :], wfb[:, cc, :]), [Rw], [PB[bk]])
            P.op("vector", CP(wmix[:, cc, ri, :], pb[bk][:, 0:256]), [PB[bk]], [Rmix])
    mark = A.off
    dft1 = A.alloc([N1, 2 * N1], BF16)
    tw = A.alloc([N1, 3, 64], F32)
    X = A.alloc([N1, 64 * 256], BF16)
    Bsb = A.alloc([N1, 64, 2, 256], BF16)
    Rc1, RX, RB = Res(), Res(), Res()
    P.dma("sync", dft1[:], k.c_dft1, writes=[Rc1])
    P.dma("sync", tw[:], k.c_tw, writes=[Rc1])
    P.dma("sync", X[:], k.U.rearrange("(a b) c -> a (b c)", b=64), writes=[RX])
    t1 = [A.alloc([N1, 256], F32) for _ in range(2)]
    t2 = [A.alloc([N1, 256], F32) for _ in range(2)]
    Rt1, Rt2 = [Res(), Res()], [Res(), Res()]
    RBd = Res()
    it = 0
    for n in range(32):
        if n > 0 and n % 8 == 0:
            q = n // 8 - 1
            P.dma("gpsimd", k.Bd[:, q * 16:(q + 1) * 16, :, :], Bsb[:, q * 16:(q + 1) * 16, :, :], reads=[RB], writes=[RBd])
        ba, bs = 2 * (n % 2), 2 * (n % 2) + 1
        P.op("tensor", MM(pb[ba][:N1, :], dft1[:, 0:N1], X[:, n * 512:(n + 1) * 512]), [Rc1, RX], [PB[ba]])
        P.op("tensor", MM(pb[bs][:N1, :], dft1[:, N1:2 * N1], X[:, n * 512:(n + 1) * 512]), [Rc1, RX], [PB[bs]])
        for hh in range(2):
            l2 = 2 * n + hh
            cs = slice(hh * 256, (hh + 1) * 256)
            b = it % 2
            it += 1
            P.op("scalar", ACTF(t1[b][:], pb[ba][:N1, cs], AF.Copy, scale=tw[:, 0, l2:l2 + 1]), [PB[ba], Rc1], [Rt1[b]])
            P.op("scalar", ACTF(t2[b][:], pb[ba][:N1, cs], AF.Copy, scale=tw[:, 1, l2:l2 + 1]), [PB[ba], Rc1], [Rt2[b]])
            P.op("vector", STT(Bsb[:, l2, 0, :], pb[bs][:N1, cs], tw[:, 1, l2:l2 + 1], t1[b][:], ALU.mult, ALU.add),
                 [PB[bs], Rc1, Rt1[b]], [RB])
            P.op("vector", STT(Bsb[:, l2, 1, :], pb[bs][:N1, cs], tw[:, 2, l2:l2 + 1], t2[b][:], ALU.mult, ALU.add),
                 [PB[bs], Rc1, Rt2[b]], [RB])
    P.dma("gpsimd", k.Bd[:, 48:64, :, :], Bsb[:, 48:64, :, :], reads=[RB], writes=[RBd])
    P.barrier()
    A.reset(mark)
    B2 = A.alloc([64, N1, 2, 128], BF16)
    YTs = A.alloc([128, 2, 2, L], BF16)
    RB2, RYT = Res(), Res()
    ev = 0
    for cc in range(2):
        for ri in range(2):
            P.dma("sync", B2[:, :, ri, :], k.Bd[:, :, ri, cc * 128:(cc + 1) * 128].rearrange("k l c -> l k c"),
                  reads=[RBd], writes=[RB2])
        for k1 in range(N1):
            bk = (k1 // 4) % 4
            slot = k1 % 4
            oap = pb[bk][:, slot * 128:(slot + 1) * 128]
            P.op("tensor", MM(oap, B2[:, k1, 0, :], dft2[:, 0, :], True, False), [RB2, Rw], [PB[bk]])
            P.op("tensor", MM(oap, B2[:, k1, 1, :], dft2[:, 1, :], False, True), [RB2, Rw], [PB[bk]])
            if slot == 3:
                for ri in range(2):
                    src = pb[bk][:, :].rearrange("p (s r q) -> p s r q", s=4, r=2)[:, :, ri, :]
                    dst = YTs[:, cc, ri, :].rearrange("p (q a) -> p a q", a=N1)[:, k1 - 3:k1 + 1, :]
                    if ev % 2 == 0:
                        P.op("vector", CP(dst, src), [PB[bk]], [RYT])
                    else:
                        P.op("scalar", ACTF(dst, src, AF.Copy), [PB[bk]], [RYT])
                    ev += 1
    gz = [A.alloc([128, 512], BF16) for _ in range(2)]
    yo = [A.alloc([128, 512], BF16) for _ in range(2)]
    Rgz, Ryo = [Res(), Res()], [Res(), Res()]
    it = 0
    for gi in range(L // 512):
        sl = slice(gi * 512, (gi + 1) * 512)
        for oc in range(2):
            b = it % 2
            bk = 4 + it % 4
            it += 1
            P.dma("sync", gz[b][:], k.FT[oc, :, sl], writes=[Rgz[b]])
            n = 0
            for cc in range(2):
                for ri in range(2):
                    P.op("tensor", MM(pb[bk][:, :], wmix[:, cc, ri, oc * 128:(oc + 1) * 128], YTs[:, cc, ri, sl], n == 0, n == 3),
                         [Rmix, RYT], [PB[bk]])
                    n += 1
            P.op("vector", TT(yo[b][:], pb[bk][:, :], gz[b][:], ALU.mult), [PB[bk], Rgz[b]], [Ryo[b]])
            P.dma("gpsimd", k.YT[oc, :, sl], yo[b][:], reads=[Ryo[b]])


def na_table_build(k, lay):
    nc, P, L = k.nc, k.P, k.L
    negt = k.negt
    Rneg = k.Rconst
    Rfill = {}
    Rdiag = []
    for h in range(4):
        for dl in range(8):
            Rfill[(h, dl)] = Res()
            P.dma("gpsimd", k.NAB[h, dl, :, :], negt[:], reads=[Rneg], writes=[Rfill[(h, dl)]])
    nabt = k.NAB.tensor
    rpbt = k.rpb.tensor
    n = 0
    for h in range(4):
        for dl in range(8):
            dbase = (h * 8 + dl) * 64 * 512
            sbase = ((lay * 4 + h) * 15 + (7 - dl)) * 31
            q = "gpsimd"
            n += 1
            rf = [Rfill[(h, dl)]]
            r_ = Res()
            Rdiag.append(r_)
            P.dma(q, bass.AP(nabt, dbase + 8 * 512, [[513, 49], [64, 8], [1, 16]]),
                  bass.AP(rpbt, sbase + 7, [[0, 49], [31, 8], [1, 16]]), reads=rf, writes=[r_])
            r_ = Res()
            Rdiag.append(r_)
            P.dma(q, bass.AP(nabt, dbase, [[64, 8], [512, 8], [1, 16]]),
                  bass.AP(rpbt, sbase + 15, [[31, 8], [-1, 8], [1, 16]]), reads=rf, writes=[r_])
            r_ = Res()
            Rdiag.append(r_)
            P.dma(q, bass.AP(nabt, dbase + 57 * 512 + 48, [[64, 8], [512, 7], [1, 16]]),
                  bass.AP(rpbt, sbase + 6, [[31, 8], [-1, 7], [1, 16]]), reads=rf, writes=[r_])


def phase_na(k, lay):
    nc, P, L = k.nc, k.P, k.L
    rows = L // 64
    A = k.arena
    A.reset()
    pb, PB = k.pb, k.PB
    A.reset()
    tb2 = [A.alloc([64, 8, 512], F32) for _ in range(2)]
    qT2 = [A.alloc([64, L], BF16) for _ in range(2)]
    kT2 = [A.alloc([64, L], BF16) for _ in range(2)]
    vh2 = [A.alloc([64, rows, 64], BF16) for _ in range(2)]
    gzh = A.alloc([64, L], BF16)
    yrow = A.alloc([64, L], BF16)
    Rtb2, Rq2, Rk2, Rv2 = [Res(), Res()], [Res(), Res()], [Res(), Res()], [Res(), Res()]
    Rgz, Ry = Res(), Res()

    def head_loads(hh):
        hb_ = hh % 2
        p0_ = (hh % 2) * 64
        P.dma("sync", tb2[hb_][:], k.NAB[hh].rearrange("d w x -> w d x"), writes=[Rtb2[hb_]])
        P.dma("sync", qT2[hb_][:], k.FT[14 + hh // 2, p0_:p0_ + 64, :], writes=[Rq2[hb_]])
        P.dma("sync", kT2[hb_][:], k.FT[16 + hh // 2, p0_:p0_ + 64, :], writes=[Rk2[hb_]])
        P.dma("sync", vh2[hb_][:], k.CV[:, hh * 64:(hh + 1) * 64].rearrange("(r w) d -> w r d", w=64), writes=[Rv2[hb_]])

    head_loads(0)
    NB = 4
    s1 = [A.alloc([64, 512], F32) for _ in range(NB)]
    pp = [A.alloc([64, 512], F32) for _ in range(NB)]
    pn = [A.alloc([64, 512], BF16) for _ in range(NB)]
    den = [A.alloc([64, 2], F32) for _ in range(NB)]
    pTs = [A.alloc([64, 8, 64], BF16) for _ in range(NB)]
    Rs1, Rpp, Rpn, Rden, RpT = ([Res() for _ in range(NB)] for _ in range(5))
    for h in range(4):
        p0 = (h % 2) * 64
        hb = h % 2
        tb, qT, kT, vh = tb2[hb], qT2[hb], kT2[hb], vh2[hb]
        Rtb, Rq, Rk, Rv = Rtb2[hb], Rq2[hb], Rk2[hb], Rv2[hb]
        P.dma("sync", gzh[:], k.FT[4 + h // 2, p0:p0 + 64, :], writes=[Rgz])
        if h + 1 < 4:
            head_loads(h + 1)
        rsof = lambda r: min(max(r - 4, 0), rows - 8)

        def stA1(r):
            rs = rsof(r)
            b, bS = r % NB, r % 2
            P.op("tensor", MM(pb[bS][0:64, :], qT[:, r * 64:(r + 1) * 64], kT[:, rs * 64:rs * 64 + 512]), [Rq, Rk], [PB[bS]])
            P.op("vector", STT(s1[b][:], pb[bS][0:64, :], 0.125, tb[:, r - rs, :], ALU.mult, ALU.add), [PB[bS], Rtb], [Rs1[b]])

        def stA2(r):
            b = r % NB
            P.op("scalar", ACTF(pp[b][:], s1[b][:], AF.Exp, accum_out=den[b][:, 0:1]), [Rs1[b]], [Rpp[b], Rden[b]])
            P.op("vector", RECIP(den[b][:, 1:2], den[b][:, 0:1]), [Rden[b]], [Rden[b]])
            P.op("vector", TS(pn[b][:], pp[b][:], den[b][:, 1:2], None, ALU.mult), [Rpp[b], Rden[b]], [Rpn[b]])

        def stB(r):
            b = r % NB
            bT = 2 + r % 2
            pTp = pb[bT][:, 0:256].bitcast(BF16)
            for i in range(8):
                P.op("tensor", TR(pTp[0:64, i * 64:(i + 1) * 64], pn[b][:, i * 64:(i + 1) * 64], k.ident[0:64, 0:64]),
                     [Rpn[b], k.Rconst], [PB[bT]])
            P.op("scalar", ACTF(pTs[b][:], pTp[0:64, :].rearrange("p (i q) -> p i q", i=8), AF.Copy), [PB[bT]], [RpT[b]])

        def stC(r):
            rs = rsof(r)
            b = r % NB
            bo = 4 + (r // 8) % 2
            slot = r % 8
            for i in range(8):
                P.op("tensor", MM(pb[bo][0:64, slot * 64:(slot + 1) * 64], vh[:, rs + i, :], pTs[b][:, i, :], i == 0, i == 7),
                     [Rv, RpT[b]], [PB[bo]])
            if slot == 7:
                sl = slice((r - 7) * 64, (r + 1) * 64)
                P.op("vector", TT(yrow[:, sl], pb[bo][0:64, :], gzh[:, sl], ALU.mult), [PB[bo], Rgz], [Ry])

        for t in range(rows + 3):
            if t < rows:
                stA1(t)
            if 0 <= t - 1 < rows:
                stA2(t - 1)
            if 0 <= t - 2 < rows:
                stB(t - 2)
            if 0 <= t - 3 < rows:
                stC(t - 3)
        P.dma("gpsimd", k.YT[4 + h // 2, p0:p0 + 64, :], yrow[:], reads=[Ry])

def phase_gdn(k, lay):
    nc, P, L = k.nc, k.P, k.L
    A = k.arena
    A.reset()
    pb, PB = k.pb, k.PB
    id64 = k.ident[0:64, 0:64]
    cw = A.alloc([128, 6, 5], F32)
    Dg = A.alloc([128, 6, 5, 128], BF16)
    Rcw, RDg = Res(), Res()
    for c in range(6):
        P.dma("sync", cw[:, c, :], k.conv_w[lay, :, c * 128:(c + 1) * 128].rearrange("j p -> p j"), writes=[Rcw],
              allow_slow_non_contiguous=True)
    for c in range(6):
        for j in range(5):
            P.op("vector", TS(Dg[:, c, j, :], k.identf[:], cw[:, c, j:j + 1], None, ALU.mult), [Rcw, k.Rconst], [RDg])
    xc = [A.alloc([128, 6, 516], BF16) for _ in range(2)]
    actf = [A.alloc([128, 4, 512], F32) for _ in range(3)]
    sq = [A.alloc([128, 4, 512], BF16) for _ in range(2)]
    lnt = [A.alloc([128, 4, 512], F32) for _ in range(2)]
    qn = [A.alloc([128, 6, 512], BF16) for _ in range(3)]
    tm = [A.alloc([128, 6, 4, 128], BF16) for _ in range(2)]
    Rxc = [[Res() for _ in range(6)] for _ in range(2)]
    Ract = [[Res() for _ in range(4)] for _ in range(3)]
    Rsq = [[Res() for _ in range(4)] for _ in range(2)]
    Rln = [[Res() for _ in range(4)] for _ in range(2)]
    Rqn = [[Res() for _ in range(6)] for _ in range(3)]
    Rtm = [[Res() for _ in range(6)] for _ in range(2)]
    ng = L // 512

    def gA(gi):
        b = gi % 2
        tok0 = gi * 512
        for c in range(6):
            lo = tok0 - 2 if gi > 0 else tok0
            hi = tok0 + 514 if gi < ng - 1 else tok0 + 512
            if gi == 0:
                P.op("gpsimd", MSET(xc[b][:, c, 0:2], 0.0), writes=[Rxc[b][c]])
            if gi == ng - 1:
                P.op("gpsimd", MSET(xc[b][:, c, 514:516], 0.0), writes=[Rxc[b][c]])
            P.dma("sync", xc[b][:, c, (lo - (tok0 - 2)):(hi - (tok0 - 2))], k.FT[8 + c, :, lo:hi], writes=[Rxc[b][c]])
        for c in range(6):
            bk = c % 4
            for j in range(5):
                P.op("tensor", MM(pb[bk][:, :], Dg[:, c, j, :], xc[b][:, c, j:j + 512], j == 0, j == 4), [RDg, Rxc[b][c]], [PB[bk]])
            if c < 4:
                P.op("scalar", ACTF(actf[gi % 3][:, c, :], pb[bk][:, :], AF.Silu), [PB[bk]], [Ract[gi % 3][c]])
            else:
                P.op("scalar", ACTF(qn[gi % 3][:, c, :], pb[bk][:, :], AF.Silu), [PB[bk]], [Rqn[gi % 3][c]])

    def gB(gi):
        b = gi % 2
        for c in range(4):
            P.op("gpsimd", TT(sq[b][:, c, :], actf[gi % 3][:, c, :], actf[gi % 3][:, c, :], ALU.mult), [Ract[gi % 3][c]], [Rsq[b][c]])
            bk2 = 4 + c % 2
            P.op("tensor", MM(pb[bk2][:, :], k.bones[:], sq[b][:, c, :]), [k.Rconst, Rsq[b][c]], [PB[bk2]])
            P.op("scalar", ACTF(lnt[b][:, c, :], pb[bk2][:, :], AF.Ln, bias=EPS), [PB[bk2]], [Rln[b][c]])
        for c in range(4):
            P.op("scalar", ACTF(lnt[b][:, c, :], lnt[b][:, c, :], AF.Exp, scale=-0.5), [Rln[b][c]], [Rln[b][c]])

    def gC(gi):
        b = gi % 2
        q3 = gi % 3
        tok0 = gi * 512
        for c in range(4):
            P.op("vector", STT(qn[q3][:, c, :], actf[q3][:, c, :], 0.125 if c < 2 else 1.0, lnt[b][:, c, :], ALU.mult, ALU.mult),
                 [Ract[q3][c], Rln[b][c]], [Rqn[q3][c]])
            P.dma("gpsimd", k.QKn[c, :, tok0:tok0 + 512], qn[q3][:, c, :], reads=[Rqn[q3][c]])
        for c in range(6):
            bk = 6 + c % 2
            pT = pb[bk][:, 0:256].bitcast(BF16)
            for t in range(4):
                P.op("tensor", TR(pT[:, t * 128:(t + 1) * 128], qn[q3][:, c, t * 128:(t + 1) * 128], k.ident[:]),
                     [Rqn[q3][c], k.Rconst], [PB[bk]])
            P.op("vector", CP(tm[b][:, c, :, :], pT.rearrange("p (t c) -> p t c", t=4)), [PB[bk]], [Rtm[b][c]])
            P.dma("gpsimd", k.QKVt[tok0:tok0 + 512, c * 128:(c + 1) * 128].rearrange("(t p) c -> p t c", p=128), tm[b][:, c, :, :],
                  reads=[Rtm[b][c]])

    pipeline(ng, [gA, gB, gC])
    P.barrier()
    A.reset()
    ntile = L // 128
    gm = A.alloc([128, 12, 128], F32)
    rmask = A.alloc([128, 2], F32)
    idr = A.alloc([128, 128], F32R)
    Rgm = Res()
    P.dma("sync", gm[:], k.c_gmask, writes=[Rgm])
    P.op("vector", CP(idr[:], k.identf[:]), [k.Rconst], [Rgm])
    P.op("vector", CP(rmask[:, :], gm[:, 10:12, 0]), [Rgm], [Rgm])
    MRk = [gm[:, 0, :], gm[:, 1, :]]
    Tm = [gm[:, 2, :], gm[:, 3, :]]
    INCLk = [gm[:, 4, :], gm[:, 5, :]]
    STRk = [gm[:, 6, :], gm[:, 7, :]]
    M2 = [gm[:, 8, :], gm[:, 9, :]]
    ONEC = [gm[:, 10, :], gm[:, 11, :]]
    QKg = [[A.alloc([64, 8, 512], BF16) for _ in range(2)] for _ in range(2)]
    TMg = [[A.alloc([128, 4, 768], BF16) for _ in range(2)] for _ in range(2)]
    Gg = [[A.alloc([128, 4, 16], F32) for _ in range(2)] for _ in range(2)]
    Rgrp = [[Res(), Res()], [Res(), Res()]]
    S32 = [A.alloc([64, 4, 64], F32) for _ in range(2)]
    Sbf = [A.alloc([64, 4, 64], BF16) for _ in range(2)]
    RS32, RSbf = [Res(), Res()], [Res(), Res()]
    for d in range(2):
        P.op("vector", MSET(S32[d][:], 0.0), writes=[RS32[d]])
        P.op("vector", MSET(Sbf[d][:], 0.0), writes=[RSbf[d]])

    def al2(shape, dt):
        return [A.alloc(shape, dt) for _ in range(2)]

    Grhs, E, EMi, EMs, t1 = (al2([128, 4, 128], F32) for _ in range(5))
    EG = al2([128, 16], F32)
    ekm = al2([128, 2, 4], F32)
    nb = al2([128, 4], F32)
    be = al2([128, 4], F32)
    qkb, qkT = (al2([128, 4, 128], BF16) for _ in range(2))
    wT, qdT = (al2([64, 4, 128], BF16) for _ in range(2))
    qd, vn = (al2([128, 4, 64], BF16) for _ in range(2))
    kdm = [al2([128, 4, 64], BF16) for _ in range(2)]
    Rkdm = [[Res(), Res()], [Res(), Res()]]
    Rekm = [Res(), Res()]
    for b_ in range(2):
        P.op("vector", MSET(vn[b_][:], 0.0), writes=[Rgm])
    Rm = [al2([128, 4, 128], F32R) for _ in range(2)]
    XPt = [al2([128, 4, 2, 128], F32R) for _ in range(2)]
    Xm = [[XPt[s_][b_][:, :, 0, :] for b_ in range(2)] for s_ in range(2)]
    Pm = [[XPt[s_][b_][:, :, 1, :] for b_ in range(2)] for s_ in range(2)]
    osb = al2([128, 4, 64], F32)
    R_ = lambda: [Res(), Res()]
    RGrhs, RE, REMi, REMs, Rt1, REG, Rnb, Rbe, Rqkb, RqkT, RwT, Rqd, RqdT, Rkd, Rvn, Rosb = (R_() for _ in range(16))
    RPm = [R_() for _ in range(2)]
    RRm = [R_() for _ in range(2)]
    RXm = [R_() for _ in range(2)]
    NLV = 6
    all_steps = []
    for hs in range(2 * ntile):
        d = hs % 2
        ti = hs // 2
        tl = ti if d == 0 else ntile - 1 - ti
        n = tl % 4
        gb = (ti // 4) % 2
        b = d
        q = [0, 1, 2, 3] if d == 0 else [4, 5, 6, 7]
        stg = []
        cur = []

        def add(eng, fn, reads=(), writes=()):
            cur.append((eng, fn, tuple(reads), tuple(writes), False, None))

        def adddma(eng, out, in_, reads=(), writes=()):
            cur.append((eng, (out, in_), tuple(reads), tuple(writes), True, None))

        def stage():
            if cur:
                stg.append(list(cur))
                del cur[:]

        if ti % 4 == 0:
            g0 = (tl // 4) * 512
            sl = slice(g0, g0 + 512)
            for h in range(4):
                p0 = (h % 2) * 64
                adddma("sync", QKg[d][gb][:, h, :], k.QKn[h // 2, p0:p0 + 64, sl], writes=[Rgrp[d][gb]])
                adddma("sync", QKg[d][gb][:, 4 + h, :], k.QKn[2 + h // 2, p0:p0 + 64, sl], writes=[Rgrp[d][gb]])
            adddma("sync", TMg[d][gb][:], k.QKVt[sl, :].rearrange("(n p) c -> p n c", p=128), writes=[Rgrp[d][gb]])
            adddma("sync", Gg[d][gb][:], k.G[sl, :].rearrange("(n p) c -> p n c", p=128), writes=[Rgrp[d][gb]])
        QK, TM, GG, RG = QKg[d][gb], TMg[d][gb], Gg[d][gb], Rgrp[d][gb]
        cs = slice(n * 128, n * 128 + 128)
        gcol = GG[:, n, 4 * d:4 * d + 4]
        bcol = GG[:, n, 8 + 4 * d:12 + 4 * d]
        bcw = lambda ap: ap.unsqueeze(2).to_broadcast([128, 4, 128])
        bc64 = lambda ap: ap.unsqueeze(2).to_broadcast([128, 4, 64])
        mk = lambda m: m.unsqueeze(1).to_broadcast([128, 4, 128])
        v4 = lambda ap: ap.rearrange("p (a b) -> p a b", a=4)
        fl = lambda ap: ap.rearrange("p a b -> p (a b)")
        add("gpsimd", TT(Grhs[b][:], mk(MRk[d]), bcw(gcol), ALU.mult), [Rgm, RG], [RGrhs[b]])
        for (qq, lt) in enumerate((Tm[d], M2[d], ONEC[0], ONEC[1])):
            add("tensor", MM(pb[q[1]][:, 4 * qq:4 * qq + 4], lt, gcol), [Rgm, RG], [PB[q[1]]])
        add("scalar", ACTF(EG[b][:], pb[q[1]][:, 0:16], AF.Exp), [PB[q[1]]], [REG[b]])
        for f in range(2):
            add("vector", TS(ekm[b][:, f, :], EG[b][:, 4:8], rmask[:, f:f + 1], None, ALU.mult), [REG[b], Rgm], [Rekm[b]])
        stage()
        add("tensor", MM(pb[q[0]][:, :], Tm[d], fl(Grhs[b][:])), [Rgm, RGrhs[b]], [PB[q[0]]])
        add("scalar", ACTF(fl(E[b][:]), pb[q[0]][:, :], AF.Exp), [PB[q[0]]], [RE[b]])
        for h in range(4):
            add("tensor", MM(pb[q[2]][:, h * 128:(h + 1) * 128], QK[:, 4 + h, cs], QK[:, 4 + h, cs]), [RG], [PB[q[2]]])
        for h in range(4):
            add("tensor", MM(pb[q[3]][:, h * 128:(h + 1) * 128], QK[:, h, cs], QK[:, 4 + h, cs]), [RG], [PB[q[3]]])
        add("vector", TS(nb[b][:], bcol, -1.0, None, ALU.mult), [RG], [Rnb[b]])
        add("vector", TT(be[b][:], bcol, EG[b][:, 0:4], ALU.mult), [RG, REG[b]], [Rbe[b]])
        stage()
        add("vector", TT(EMs[b][:], E[b][:], mk(STRk[d]), ALU.mult), [RE[b], Rgm], [REMs[b]])
        add("gpsimd", TT(EMi[b][:], E[b][:], mk(INCLk[d]), ALU.mult), [RE[b], Rgm], [REMi[b]])
        add("vector", TT(Xm[0][b][:, :, 0:64], v4(TM[:, n, 512:768]), bc64(bcol), ALU.mult), [RG], [RXm[0][b]])
        add("vector", TT(Xm[0][b][:, :, 64:128], v4(TM[:, n, 256:512]), bc64(be[b][:, :]), ALU.mult), [RG, Rbe[b]], [RXm[0][b]])
        stage()
        add("vector", TT(t1[b][:], v4(pb[q[2]][:, :]), EMs[b][:], ALU.mult), [PB[q[2]], REMs[b]], [Rt1[b]])
        add("vector", TT(Pm[0][b], t1[b][:], bcw(nb[b][:, :]), ALU.mult), [Rt1[b], Rnb[b]], [RPm[0][b]])
        add("vector", TT(qkb[b][:], v4(pb[q[3]][:, :]), EMi[b][:], ALU.mult), [PB[q[3]], REMi[b]], [Rqkb[b]])
        add("gpsimd", TT(qd[b][:], v4(TM[:, n, 0:256]), bc64(EG[b][:, 0:4]), ALU.mult), [RG, REG[b]], [Rqd[b]])
        for f in range(2):
            add("gpsimd", TT(kdm[f][b][:], v4(TM[:, n, 256:512]), bc64(ekm[b][:, f, :]), ALU.mult), [RG, Rekm[b]], [Rkdm[f][b]])
        stage()
        for h in range(4):
            add("tensor", MM(pb[q[0]][:, h * 128:(h + 1) * 128], Pm[0][b][:, h, :], idr[:, :]), [RPm[0][b], Rgm], [PB[q[0]]])
        add("scalar", ACTF(fl(Rm[0][b][:]), pb[q[0]][:, :], AF.Copy), [PB[q[0]]], [RRm[0][b]])
        pTb = pb[q[1]][:, 0:256].bitcast(BF16)
        for h in range(4):
            add("tensor", TR(pTb[:, h * 128:(h + 1) * 128], qkb[b][:, h, :], k.ident[:]), [Rqkb[b], k.Rconst], [PB[q[1]]])
        add("vector", CP(fl(qkT[b][:]), pTb[:, :]), [PB[q[1]]], [RqkT[b]])
        stage()
        for h in range(4):
            add("tensor", TR(pTb[0:64, h * 128:(h + 1) * 128], qd[b][:, h, :], k.ident[:]), [Rqd[b], k.Rconst], [PB[q[1]]])
        add("vector", CP(fl(qdT[b][:]), pTb[0:64, :]), [PB[q[1]]], [RqdT[b]])
        stage()
        for j in range(NLV):
            sj, sn = j % 2, (j + 1) % 2
            wide = j < NLV - 2
            if j < NLV - 1:
                for h in range(4):
                    add("tensor", MM(pb[q[1]][:, h * 128:(h + 1) * 128], Pm[sj][b][:, h, :], Rm[sj][b][:, h, :]),
                        [RRm[sj][b], RPm[sj][b]], [PB[q[1]]])
                add("scalar", ACTF(fl(Rm[sn][b][:]), pb[q[1]][:, :], AF.Copy), [PB[q[1]]], [RRm[sn][b]])
                stage()
            if wide:
                for h in range(4):
                    bk = q[2 + h // 2]
                    add("tensor", MM(pb[bk][:, (h % 2) * 256:(h % 2) * 256 + 256], Rm[sj][b][:, h, :],
                                     XPt[sj][b][:, h, :, :].rearrange("p a c -> p (a c)")),
                        [RRm[sj][b], RXm[sj][b], RPm[sj][b]], [PB[bk]])
                pv = k.pball[:, q[2] * 512:(q[2] + 2) * 512].rearrange("p (h a c) -> p h a c", h=4, a=2)
                add("vector", CP(Pm[sn][b], pv[:, :, 1, :]), [PB[q[2]], PB[q[3]]], [RPm[sn][b]])
                add("vector", TT(Xm[sn][b], pv[:, :, 0, :], Xm[sj][b], ALU.add), [PB[q[2]], PB[q[3]], RXm[sj][b]], [RXm[sn][b]])
            else:
                for h in range(4):
                    add("tensor", MM(pb[q[2]][:, h * 128:(h + 1) * 128], Rm[sj][b][:, h, :], Xm[sj][b][:, h, :]),
                        [RRm[sj][b], RXm[sj][b]], [PB[q[2]]])
                add("vector", TT(Xm[sn][b], v4(pb[q[2]][:, :]), Xm[sj][b], ALU.add), [PB[q[2]], RXm[sj][b]], [RXm[sn][b]])
            stage()
        XF = Xm[NLV % 2][b]
        RXF = RXm[NLV % 2][b]
        for h in range(4):
            add("tensor", MM(pb[q[0]][0:64, h * 128:(h + 1) * 128], XF[:, h, 64:128], idr[:, :]), [RXF, Rgm], [PB[q[0]]])
        add("scalar", ACTF(fl(wT[b][:]), pb[q[0]][0:64, :], AF.Copy), [PB[q[0]]], [RwT[b]])
        stage()
        for f in ((0, 1) if d == 0 else (1, 0)):
            rows = slice(64 * f, 64 * f + 64)
            for h in range(4):
                add("tensor", MM(pb[q[0]][:, h * 64:(h + 1) * 64], wT[b][:, h, :], Sbf[d][:, h, :]), [RwT[b], RSbf[d]], [PB[q[0]]])
            add("vector", TT(vn[b][rows, :, :], XF[rows, :, 0:64], v4(pb[q[0]][rows, 0:256]), ALU.subtract), [RXF, PB[q[0]]], [Rvn[b]])
            add("gpsimd", TT(S32[d][:], S32[d][:], EG[b][0:64, 8 + 4 * f:12 + 4 * f].unsqueeze(2).to_broadcast([64, 4, 64]), ALU.mult),
                [RS32[d], REG[b]], [RS32[d]])
            stage()
            for h in range(4):
                add("tensor", MM(pb[q[0]][0:64, h * 64:(h + 1) * 64], kdm[f][b][:, h, :], vn[b][:, h, :]), [Rkdm[f][b], Rvn[b]], [PB[q[0]]])
            for h in range(4):
                o = pb[q[1]][:, h * 64:(h + 1) * 64]
                add("tensor", MM(o, qdT[b][:, h, :], Sbf[d][:, h, :], True, False), [RqdT[b], RSbf[d]], [PB[q[1]]])
                add("tensor", MM(o, qkT[b][:, h, :], vn[b][:, h, :], False, True), [RqkT[b], Rvn[b]], [PB[q[1]]])
            add("vector", TT(S32[d][:], S32[d][:], v4(pb[q[0]][0:64, 0:256]), ALU.add), [RS32[d], PB[q[0]]], [RS32[d]])
            add("scalar", ACTF(Sbf[d][:], S32[d][:], AF.Copy), [RS32[d]], [RSbf[d]])
            add("scalar", ACTF(fl(osb[b][rows, :, :]), pb[q[1]][rows, 0:256], AF.Copy), [PB[q[1]]], [Rosb[b]])
            stage()
        adddma("gpsimd", (k.OF if d == 0 else k.OB)[tl * 128:tl * 128 + 128, :], fl(osb[b][:]), reads=[Rosb[b]])
        stage()
        all_steps.append(stg)
    nst = max(len(sg) for sg in all_steps)
    KS = nst // 2 + 1
    nhs = len(all_steps)
    for t in range((nhs - 1) * KS + nst):
        for i in range(max(0, (t - nst) // KS), min(nhs - 1, t // KS) + 1):
            si = t - i * KS
            if 0 <= si < len(all_steps[i]):
                for (eng, fn, reads, writes, isdma, _) in all_steps[i][si]:
                    if isdma:
                        P.dma(eng, fn[0], fn[1], reads=reads, writes=writes)
                    else:
                        P.op(eng, fn, reads, writes)
    P.barrier()
    A.reset()
    gng = A.alloc([128, 64], F32)
    Rgng = Res()
    P.dma("sync", gng[:], k.gdn_g[lay].partition_broadcast(128), writes=[Rgng])
    NB = 4
    aln = lambda shape, dt: [A.alloc(shape, dt) for _ in range(NB)]
    Rn = lambda: [Res() for _ in range(NB)]
    of, ob, osum, sqq, y1 = (aln([128, 256], F32) for _ in range(5))
    ss = aln([128, 8], F32)
    ytm = aln([128, 256], BF16)
    gz = al2([128, 2, 512], BF16)
    yo = al2([128, 2, 512], BF16)
    Rof, Rob, Ros, Rsqq, Ry1, Rss, Rytm = (Rn() for _ in range(7))
    Rgz, Ryo = R_(), R_()
    v4 = lambda ap: ap.rearrange("p (a b) -> p a b", a=4)

    def c0(i):
        gi, t = i // 4, i % 4
        g2 = gi % 2
        b = i % NB
        if t == 0:
            P.dma("sync", gz[g2][:], k.FT[2:4, :, gi * 512:(gi + 1) * 512].rearrange("c p l -> p c l"), writes=[Rgz[g2]])
        P.dma("sync", of[b][:], k.OF[i * 128:(i + 1) * 128, :], writes=[Rof[b]])
        P.dma("sync", ob[b][:], k.OB[i * 128:(i + 1) * 128, :], writes=[Rob[b]])
        P.op("vector", TT(osum[b][:], of[b][:], ob[b][:], ALU.add), [Rof[b], Rob[b]], [Ros[b]])
        P.op("gpsimd", TT(sqq[b][:], osum[b][:], osum[b][:], ALU.mult), [Ros[b]], [Rsqq[b]])

    def c1(i):
        b = i % NB
        P.op("vector", lambda e, o_=ss[b][:, 0:4], i_=v4(sqq[b][:]): e.reduce_sum(o_, i_, AX.X), [Rsqq[b]], [Rss[b]])
        P.op("scalar", ACTF(ss[b][:, 4:8], ss[b][:, 0:4], AF.Ln, bias=EPS, scale=1.0 / 64), [Rss[b]], [Rss[b]])
        P.op("scalar", ACTF(ss[b][:, 4:8], ss[b][:, 4:8], AF.Exp, scale=-0.5), [Rss[b]], [Rss[b]])

    def c2(i):
        b = i % NB
        P.op("vector", TT(v4(y1[b][:]), v4(osum[b][:]), ss[b][:, 4:8].unsqueeze(2).to_broadcast([128, 4, 64]), ALU.mult),
             [Ros[b], Rss[b]], [Ry1[b]])
        P.op("gpsimd", TT(v4(ytm[b][:]), v4(y1[b][:]), gng[:, :].unsqueeze(1).to_broadcast([128, 4, 64]), ALU.mult),
             [Ry1[b], Rgng], [Rytm[b]])

    def c3(i):
        gi, t = i // 4, i % 4
        g2 = gi % 2
        b = i % NB
        bk = 4 + i % 2
        pT = pb[bk][:, 0:128].bitcast(BF16)
        for j in range(2):
            P.op("tensor", TR(pT[:, j * 128:(j + 1) * 128], ytm[b][:, j * 128:(j + 1) * 128], k.ident[:]), [Rytm[b], k.Rconst], [PB[bk]])
        P.op("vector", TT(yo[g2][:, :, t * 128:(t + 1) * 128], pT.rearrange("p (c t) -> p c t", c=2),
                          gz[g2][:, :, t * 128:(t + 1) * 128], ALU.mult), [PB[bk], Rgz[g2]], [Ryo[g2]])
        if t == 3:
            for j in range(2):
                P.dma("gpsimd", k.YT[2 + j, :, gi * 512:(gi + 1) * 512], yo[g2][:, j, :], reads=[Ryo[g2]])

    pipeline(L // 128, [c0, c1, c2, c3])
```
